# Optimizing a Trainium2 kernel written in Bass

```python
import jax, jax.numpy as jnp
from jax import lax
import numpy as np

D_MODEL = 1024
BATCH = 2
SEQ = 8192
DEPTH = 2
DEC_BATCH = 32
DEC_SEQ = 16
PAST_LEN = 2048

CHUNK = 64
Q_BLOCK = 128
D_MIX = D_MODEL
FOX_HEADS = 8
FOX_HEAD_DIM = 64
FOX_WIDTH = FOX_HEADS * FOX_HEAD_DIM
HGRN_HEADS = 4
HGRN_HEAD_DIM = 128
HGRN_WIDTH = HGRN_HEADS * HGRN_HEAD_DIM
D_IN_PROJ = 3 * FOX_WIDTH + FOX_HEADS + 4 * HGRN_WIDTH
D_FF = 2816
RMS_EPS = 1e-6
FOX_SCALE = FOX_HEAD_DIM ** -0.5

kernel_name = "hymba_fox_hgrn2_macaron_step"


def rms_norm(x, g):
    xf = x.astype(jnp.float32)
    y = xf * lax.rsqrt(jnp.mean(xf * xf, axis=-1, keepdims=True) + RMS_EPS)
    return (y * g.astype(jnp.float32)).astype(x.dtype)


def swiglu_ffn(x, wi, wo):
    a, b = jnp.split(x @ wi, 2, axis=-1)
    return (jax.nn.silu(a) * b) @ wo


def hgrn_lower_bounds(lb_param):
    cs = jnp.cumsum(jax.nn.softmax(lb_param.astype(jnp.float32), axis=0), axis=0)
    return cs - cs[0:1]


def project_mixers(h, w_in, b_f, lb):
    B, L, _ = h.shape
    F, H, W = FOX_WIDTH, FOX_HEADS, HGRN_WIDTH
    proj = h @ w_in
    q, k, v, fg, hq, hf, hi, hg = jnp.split(
        proj, [F, 2 * F, 3 * F, 3 * F + H, 3 * F + H + W, 3 * F + H + 2 * W, 3 * F + H + 3 * W], axis=-1)
    fox_heads = lambda t: t.reshape(B, L, FOX_HEADS, FOX_HEAD_DIM).transpose(0, 2, 1, 3)
    fox_logf = jax.nn.log_sigmoid(fg.astype(jnp.float32) + b_f.astype(jnp.float32)).transpose(0, 2, 1)
    z = hf.astype(jnp.float32)
    h_logf = jnp.logaddexp(jnp.log(lb), jnp.log1p(-lb) + jax.nn.log_sigmoid(z))
    h_k = (1.0 - lb) * jax.nn.sigmoid(-z)
    hs = lambda t: t.reshape(B, L, HGRN_HEADS, HGRN_HEAD_DIM)
    return (fox_heads(q), fox_heads(k), fox_heads(v), fox_logf,
            hs(jax.nn.silu(hq)), hs(h_logf), hs(h_k), hs(hi), hs(hg))


def fox_attend(q, c_q, q_pos, k, v, c_k):
    s = jnp.einsum('bhqd,bhkd->bhqk', q, k).astype(jnp.float32) * FOX_SCALE
    s = s + (c_q[..., :, None] - c_k[..., None, :])
    mask = jnp.arange(k.shape[2])[None, :] <= q_pos[:, None]
    p = jax.nn.softmax(jnp.where(mask, s, -jnp.inf), axis=-1)
    return jnp.einsum('bhqk,bhkd->bhqd', p.astype(v.dtype), v)


def fox_prompt(q, k, v, logf):
    B, H, L, HD = q.shape
    nb = L // Q_BLOCK
    c = jnp.cumsum(logf, axis=-1)
    qb = q.reshape(B, H, nb, Q_BLOCK, HD).transpose(2, 0, 1, 3, 4)
    cb = c.reshape(B, H, nb, Q_BLOCK).transpose(2, 0, 1, 3)
    pos = jnp.arange(L, dtype=jnp.int32).reshape(nb, Q_BLOCK)
    o = lax.map(lambda a: fox_attend(a[0], a[1], a[2], k, v, c), (qb, cb, pos))
    return o.transpose(1, 2, 0, 3, 4).reshape(B, H, L, HD)


def fox_sample(q, k, v, logf, ck, cv, clogf):
    P, T = ck.shape[2], q.shape[2]
    k_all = jnp.concatenate([ck.astype(k.dtype), k], axis=2)
    v_all = jnp.concatenate([cv.astype(v.dtype), v], axis=2)
    c_all = jnp.cumsum(jnp.concatenate([clogf.astype(jnp.float32), logf], axis=-1), axis=-1)
    pos = P + jnp.arange(T, dtype=jnp.int32)
    return fox_attend(q, c_all[..., P:], pos, k_all, v_all, c_all)


def hgrn2_chunked(q, logf, k, v, s0):
    B, L, H, DK = q.shape
    DV = v.shape[-1]
    C = min(CHUNK, L)
    n = L // C
    blocks = lambda t: t.astype(jnp.float32).reshape(B, n, C, H, t.shape[-1]).transpose(1, 0, 3, 2, 4)
    causal = jnp.tril(jnp.ones((C, C), dtype=bool))

    def step(S, inp):
        qc, lfc, kc, vc = inp
        b = jnp.cumsum(lfc, axis=2)
        diff = b[:, :, :, None, :] - b[:, :, None, :, :]
        decay = jnp.exp(jnp.where(causal[:, :, None], diff, -jnp.inf))
        scores = jnp.einsum('bhtd,bhsd,bhtsd->bhts', qc, kc, decay)
        o = jnp.einsum('bhts,bhsv->bhtv', scores, vc) + jnp.einsum('bhtd,bhdv->bhtv', qc * jnp.exp(b), S)
        b_last = b[:, :, -1]
        S_new = jnp.exp(b_last)[..., None] * S + jnp.einsum(
            'bhsd,bhsv->bhdv', kc * jnp.exp(b_last[:, :, None] - b), vc)
        return S_new, o

    S, o = lax.scan(step, s0.astype(jnp.float32), (blocks(q), blocks(logf), blocks(k), blocks(v)))
    return o.transpose(1, 0, 3, 2, 4).reshape(B, L, H, DV), S


def trunk_layer(x, l, p, lb, fox_fn, s0):
    B, L, _ = x.shape
    x = x + 0.5 * swiglu_ffn(rms_norm(x, p['norm_ffn1'][l]), p['ffn1_wi'][l], p['ffn1_wo'][l])
    h = rms_norm(x, p['norm_mix'][l])
    fq, fk, fv, flogf, hq, hlogf, hk, hv, hg = project_mixers(h, p['w_in'][l], p['b_fgate'][l], lb)
    fox_o = fox_fn(fq, fk, fv, flogf).transpose(0, 2, 1, 3).reshape(B, L, FOX_WIDTH)
    ho, S = hgrn2_chunked(hq, hlogf, hk, hv, s0)
    hgrn_o = (rms_norm(ho, p['hgrn_gnorm'][l]) * jax.nn.silu(hg.astype(jnp.float32))).reshape(B, L, HGRN_WIDTH)
    mix = jnp.concatenate([fox_o, hgrn_o.astype(x.dtype)], axis=-1)
    x = x + mix @ p['w_out'][l]
    x = x + 0.5 * swiglu_ffn(rms_norm(x, p['norm_ffn2'][l]), p['ffn2_wi'][l], p['ffn2_wo'][l])
    return x, fk, fv, flogf, S


def setup_inputs(seed: int = 0) -> dict:
    key = jax.random.key(seed)
    ks = jax.random.split(key, 20)
    nrm = lambda k, shape, scale: jax.random.normal(k, shape, jnp.float32) * scale
    gain = lambda k, shape: 1.0 + nrm(k, shape, 0.05)
    return {
        'x_prompt': nrm(ks[0], (BATCH, SEQ, D_MODEL), 1.0),
        'x_sample': nrm(ks[1], (DEC_BATCH, DEC_SEQ, D_MODEL), 1.0),
        'cache_k': nrm(ks[2], (DEPTH, DEC_BATCH, FOX_HEADS, PAST_LEN, FOX_HEAD_DIM), 1.0),
        'cache_v': nrm(ks[3], (DEPTH, DEC_BATCH, FOX_HEADS, PAST_LEN, FOX_HEAD_DIM), 1.0),
        'cache_logf': jax.nn.log_sigmoid(1.0 + nrm(ks[4], (DEPTH, DEC_BATCH, FOX_HEADS, PAST_LEN), 1.0)),
        'state_hgrn': nrm(ks[5], (DEPTH, DEC_BATCH, HGRN_HEADS, HGRN_HEAD_DIM, HGRN_HEAD_DIM), 0.5),
        'norm_ffn1': gain(ks[6], (DEPTH, D_MODEL)),
        'ffn1_wi': nrm(ks[7], (DEPTH, D_MODEL, 2 * D_FF), D_MODEL ** -0.5),
        'ffn1_wo': nrm(ks[8], (DEPTH, D_FF, D_MODEL), D_FF ** -0.5),
        'norm_mix': gain(ks[9], (DEPTH, D_MODEL)),
        'w_in': nrm(ks[10], (DEPTH, D_MODEL, D_IN_PROJ), D_MODEL ** -0.5),
        'b_fgate': 1.0 + nrm(ks[11], (DEPTH, FOX_HEADS), 0.1),
        'hgrn_lb': nrm(ks[12], (DEPTH, HGRN_WIDTH), 0.5),
        'hgrn_gnorm': gain(ks[13], (DEPTH, HGRN_HEAD_DIM)),
        'w_out': nrm(ks[14], (DEPTH, D_MIX, D_MODEL), D_MIX ** -0.5),
        'norm_ffn2': gain(ks[15], (DEPTH, D_MODEL)),
        'ffn2_wi': nrm(ks[16], (DEPTH, D_MODEL, 2 * D_FF), D_MODEL ** -0.5),
        'ffn2_wo': nrm(ks[17], (DEPTH, D_FF, D_MODEL), D_FF ** -0.5),
        'norm_final': gain(ks[18], (D_MODEL,)),
    }


def reference(x_prompt, x_sample, cache_k, cache_v, cache_logf, state_hgrn,
              norm_ffn1, ffn1_wi, ffn1_wo, norm_mix, w_in, b_fgate, hgrn_lb, hgrn_gnorm,
              w_out, norm_ffn2, ffn2_wi, ffn2_wo, norm_final):
    p = {'norm_ffn1': norm_ffn1, 'ffn1_wi': ffn1_wi, 'ffn1_wo': ffn1_wo, 'norm_mix': norm_mix,
         'w_in': w_in, 'b_fgate': b_fgate, 'hgrn_gnorm': hgrn_gnorm, 'w_out': w_out,
         'norm_ffn2': norm_ffn2, 'ffn2_wi': ffn2_wi, 'ffn2_wo': ffn2_wo}
    lbs = hgrn_lower_bounds(hgrn_lb)
    xp, xs = x_prompt, x_sample
    s0_prompt = jnp.zeros((x_prompt.shape[0], HGRN_HEADS, HGRN_HEAD_DIM, HGRN_HEAD_DIM), jnp.float32)
    kp_l, vp_l, lfp_l, sp_l, ks_l, vs_l, lfs_l, ss_l = [], [], [], [], [], [], [], []
    for l in range(DEPTH):
        xp, kp, vp, lfp, sp = trunk_layer(xp, l, p, lbs[l], fox_prompt, s0_prompt)
        fox_fn = lambda q, k, v, lf, l=l: fox_sample(q, k, v, lf, cache_k[l], cache_v[l], cache_logf[l])
        xs, kss, vss, lfs, ss = trunk_layer(xs, l, p, lbs[l], fox_fn, state_hgrn[l])
        kp_l.append(kp); vp_l.append(vp); lfp_l.append(lfp); sp_l.append(sp)
        ks_l.append(kss); vs_l.append(vss); lfs_l.append(lfs); ss_l.append(ss)
    y_prompt = rms_norm(xp, norm_final)
    y_sample = rms_norm(xs, norm_final)
    return (y_prompt, y_sample,
            jnp.stack(kp_l), jnp.stack(vp_l), jnp.stack(lfp_l), jnp.stack(sp_l),
            jnp.stack(ks_l), jnp.stack(vs_l), jnp.stack(lfs_l), jnp.stack(ss_l))
```

```python
from concourse.bass_utils import run_bass_kernel_spmd
import numpy as np
import concourse.bass as bass
import concourse.mybir as mybir

F32 = mybir.dt.float32
BF16 = mybir.dt.bfloat16
I32 = mybir.dt.int32
AF = mybir.ActivationFunctionType
ALU = mybir.AluOpType
AX = mybir.AxisListType

ENGS = ["pe", "act", "dve", "pool", "sp"]
SEM_ROLL = 30000


class Prog:
    def __init__(self, nc):
        self.nc = nc
        self.q = {e: [] for e in ENGS}
        self.buf = {}
        self.dma_cnt = {}
        self.n_tensors = 0

    arena_base = None
    arena_off = 0
    arena_peak = 0

    def arena_begin(self):
        if self.arena_base is None:
            nc = self.nc
            self.arena_base = (nc.SBUF_PARTITION_SIZE_BYTES - nc.sbuf_bytes_remaining + 63) // 64 * 64
            self.arena_limit = nc.SBUF_PARTITION_SIZE_BYTES
        self.arena_off = self.arena_base

    def sb(self, shape, dtype, name=None):
        self.n_tensors += 1
        nm = "sb_" + (name or "t") + f"_{self.n_tensors}"
        if self.arena_base is None:
            return self.nc.alloc_sbuf_tensor(nm, list(shape), dtype)
        esz = 4 if dtype in (F32, I32) else 2
        nbytes = esz
        for d in shape[1:]:
            nbytes *= d
        nbytes = (nbytes + 63) // 64 * 64
        off = self.arena_off
        assert off + nbytes <= self.arena_limit, f"SBUF arena overflow: {off + nbytes} > {self.arena_limit} ({nm})"
        self.arena_off = off + nbytes
        self.arena_peak = max(self.arena_peak, self.arena_off)
        return self.nc.alloc_sbuf_tensor_at(nm, list(shape), dtype, offset=off)

    def ps(self, shape, dtype=F32, name=None):
        self.n_tensors += 1
        return self.nc.alloc_psum_tensor("ps_" + (name or "t") + f"_{self.n_tensors}", list(shape), dtype)

    def _add(self, eng, fn, reads, writes, dma_key=None, inc=16):
        idx = len(self.q[eng])
        deps = set()
        for b in reads:
            st = self.buf.get(b)
            if st is not None and st[0] is not None:
                deps.add(st[0])
        for b in writes:
            st = self.buf.get(b)
            if st is not None:
                if st[0] is not None:
                    deps.add(st[0])
                deps.update(st[1])
        me = (eng, idx)
        deps.discard(me)
        ins = {"fn": fn, "deps": deps, "dma_key": dma_key, "sig": False, "dma_val": None}
        if dma_key is not None:
            c = self.dma_cnt.get(dma_key, 0) + inc
            self.dma_cnt[dma_key] = c
            ins["dma_val"] = c
            ins["inc"] = inc
        self.q[eng].append(ins)
        for b in reads:
            st = self.buf.get(b)
            if st is None:
                self.buf[b] = [None, [me]]
            else:
                st[1].append(me)
        for b in writes:
            self.buf[b] = [me, []]
        return me

    def op(self, eng, fn, reads=(), writes=()):
        return self._add(eng, fn, tuple(reads), tuple(writes))

    def dma(self, eng, key, out, in_, reads=(), writes=(), **kw):
        return self._add(eng, lambda e: e.dma_start(out=out, in_=in_, **kw), tuple(reads), tuple(writes), dma_key=key)

    def custom_dma(self, eng, key, fn, reads=(), writes=(), inc=16):
        me = self._add(eng, fn, tuple(reads), tuple(writes), dma_key=key, inc=inc)
        return me

    def wait_keys(self, engs, keys):
        snap = {k: self.dma_cnt[k] for k in keys if k in self.dma_cnt}
        for e in engs:
            self.q[e].append({"fn": None, "deps": set(), "dma_key": None, "sig": False, "dma_val": None, "fence": snap})

    def fence(self, skip=()):
        last = set()
        for e in ENGS:
            k = len(self.q[e]) - 1
            while k >= 0 and self.q[e][k]["dma_key"] in skip and self.q[e][k]["dma_key"] is not None:
                k -= 1
            if k >= 0:
                last.add((e, k))
        snap = {k: v for k, v in self.dma_cnt.items() if k not in skip}
        for e in ENGS:
            self.q[e].append({"fn": None, "deps": set(x for x in last if x[0] != e), "dma_key": None, "sig": False,
                              "dma_val": None, "fence": snap})
        self.buf = {}

    def emit(self, final_wait_bufs=()):
        nc = self.nc
        fin = set()
        for b in final_wait_bufs:
            st = self.buf.get(b)
            if st is not None and st[0] is not None:
                fin.add(st[0])
        self.q["sp"].append({"fn": None, "deps": fin, "dma_key": None, "sig": False, "dma_val": None, "final": True})
        for e in ENGS:
            for ins in self.q[e]:
                nd = set()
                for (de, di) in ins["deps"]:
                    d = self.q[de][di]
                    if d["dma_key"] is None:
                        if de == "pe" and e == "pe":
                            continue
                        if d["fn"] is None:
                            k2 = di
                            while k2 >= 0 and (self.q[de][k2]["fn"] is None or self.q[de][k2]["dma_key"] is not None):
                                k2 -= 1
                            if k2 < 0:
                                continue
                            di = k2
                            d = self.q[de][di]
                        d["sig"] = True
                    nd.add((de, di))
                ins["deps"] = nd
        sems = {}
        for e in ENGS:
            cur = nc.alloc_semaphore(f"s_{e}_0")
            n = 0
            k = 0
            for ins in self.q[e]:
                if ins["sig"] and ins["dma_key"] is None:
                    if n >= SEM_ROLL:
                        k += 1
                        cur = nc.alloc_semaphore(f"s_{e}_{k}")
                        n = 0
                    n += 1
                    ins["sem"] = (cur, n)
        dsem = {}
        for key in self.dma_cnt:
            dsem[key] = nc.alloc_semaphore(f"d_{len(dsem)}")
        self.stats = {e: len(self.q[e]) for e in ENGS}

        def run(eng_name, eng):
            known = {}
            nwait = 0
            for ins in self.q[eng_name]:
                need = {}
                for (de, di) in ins["deps"]:
                    d = self.q[de][di]
                    if d["dma_key"] is not None:
                        s, v = dsem[d["dma_key"]], d["dma_val"]
                    else:
                        s, v = d["sem"]
                    kk = id(s)
                    if known.get(kk, 0) >= v:
                        continue
                    if kk not in need or need[kk][1] < v:
                        need[kk] = (s, v)
                for kk, (s, v) in need.items():
                    eng.wait_ge(s, v)
                    known[kk] = v
                    nwait += 1
                if ins["fn"] is None:
                    if ins.get("final"):
                        for key, cnt in self.dma_cnt.items():
                            eng.wait_ge(dsem[key], cnt)
                    if ins.get("fence") is not None:
                        for key, cnt in ins["fence"].items():
                            kk = id(dsem[key])
                            if known.get(kk, 0) < cnt:
                                eng.wait_ge(dsem[key], cnt)
                                known[kk] = cnt
                    continue
                r = ins["fn"](eng)
                if ins["dma_key"] is not None:
                    r.then_inc(dsem[ins["dma_key"]], ins.get("inc", 16))
                elif ins["sig"]:
                    r.then_inc(ins["sem"][0], 1)
            self.stats[eng_name + "_waits"] = nwait

        with nc.Block() as block:
            @block.tensor
            def _(e):
                run("pe", e)

            @block.scalar
            def _(e):
                run("act", e)

            @block.vector
            def _(e):
                run("dve", e)

            @block.gpsimd
            def _(e):
                run("pool", e)

            @block.sync
            def _(e):
                run("sp", e)


EPS = 1e-6
NEG = -30000.0
GROUPS4 = [[0, 1, 2, 3], [4, 5, 6, 7]]


class Ring:
    def __init__(self, tiles, name):
        self.tiles = tiles
        self.name = name
        self.i = 0

    def next(self):
        k = self.i % len(self.tiles)
        self.i += 1
        return self.tiles[k], f"{self.name}{k}"


def LI_F(t, u, r):
    return (t * 2 + u) * 4 + r


def LI_FL(u, r):
    return 24 + u * 4 + r


def LI_H(t, r):
    return 32 + t * 4 + r


def LI_P(t, c):
    return 44 + t * 4 + c


NLISTS = 52
FSEG = 1024
HSEG = 512
TW = 512


def build_fused(NPT, NSB, TQS, PAST, D, DFF, L=2):
    SEQ = 4 * NPT
    NS = NSB * TQS
    NTOK = NPT + NS
    TKS = PAST + TQS
    FOXW, HGW, NFG, HD = 512, 512, 8, 64
    DIN = 3 * FOXW + NFG + 4 * HGW
    KC = D // 128
    FC = DFF // 128
    HG0 = 3 * FOXW + NFG + 3 * HGW
    H0 = 3 * FOXW + NFG
    NFS = NSB * 8
    NHS = NSB * 4
    CH = 64
    assert NPT % TW == 0 and NPT % FSEG == 0 and NS <= TW

    nc = bass.Bass("TRN2", target_bir_lowering=False)
    p = Prog(nc)

    def din(name, shape, dt=F32):
        return nc.dram_tensor(name, list(shape), dt, kind="ExternalInput").ap()

    def dout(name, shape, dt=F32):
        return nc.dram_tensor(name, list(shape), dt, kind="ExternalOutput").ap()

    def dint(name, shape, dt=F32):
        return nc.dram_tensor(name, list(shape), dt).ap()

    xT_d = din("xT", [D, NTOK])
    Wd = dict(
        norm_ffn1=din("norm_ffn1", [L, D]), ffn1_wi=din("ffn1_wi", [L, D, 2 * DFF]), ffn1_wo=din("ffn1_wo", [L, DFF, D]),
        norm_mix=din("norm_mix", [L, D]), w_in=din("w_in", [L, D, DIN]), b_fgate=din("b_fgate", [L, NFG]),
        gnorm=din("gnorm", [L, 128]), w_out=din("w_out", [L, D, D]),
        norm_ffn2=din("norm_ffn2", [L, D]), ffn2_wi=din("ffn2_wi", [L, D, 2 * DFF]), ffn2_wo=din("ffn2_wo", [L, DFF, D]),
        norm_final=din("norm_final", [D]))
    ckT_d = din("cache_kT", [L, NFS, HD, PAST])
    cv_d = din("cache_v", [L, NFS, PAST, HD])
    clf_d = din("cache_logf", [L, NFS, PAST])
    st0_d = din("state0", [L, NHS, 128, 128])
    hplb_d = din("hp_lb", [L, 128, 3])
    hslb_d = din("hs_lb", [L, NHS, 128, 3])
    idx_d = din("idx", [128, NLISTS], I32)
    yT_d = dout("yT", [D, NTOK])
    kv_d = dout("kvT", [L, 2 * FOXW, NTOK])
    lfo_d = dout("logfT", [L, NFG, NTOK])
    hpS_d = dout("hp_S", [L, 128, 128])
    hsS_d = dout("hs_S", [L, NHS, 128, 128])
    projP = [dint(f"projP{l}", [DIN, NPT]) for l in range(L)]
    projS = [dint(f"projS{l}", [DIN, NS]) for l in range(L)]
    lfP = [dint(f"lfP{l}", [NFG, NPT]) for l in range(L)]
    lfS = [dint(f"lfS{l}", [NFG, NS]) for l in range(L)]
    G1a = [dint(f"G1a{l}", [12, 4 * 128, NPT]) for l in range(L)]
    G1h = [dint(f"G1h{l}", [12, 4 * 128, NPT]) for l in range(L)]
    G1l = [dint(f"G1l{l}", [4 * NFG, NPT]) for l in range(L)]
    S2 = [dint(f"S2_{l}", [8, 32, SEQ]) for l in range(L)]
    G2 = [dint(f"G2_{l}", [8, 4 * 32, SEQ]) for l in range(L)]
    foxS = [dint(f"foxS{l}", [FOXW, NS]) for l in range(L)]
    fl_src = dint("flush_src", [8, 64])
    fl_dst = dint("flush_dst", [4 * 8, 64])
    hoS = [dint(f"hoS{l}", [HGW, NS]) for l in range(L)]

    xT = p.sb([128, KC, NTOK], F32, "xT")
    ones_bf = p.sb([128, 128], BF16, "ones_bf")
    ident = p.sb([128, 128], BF16, "ident")
    maskrow = p.sb([128, 512], BF16, "maskrow")
    mask01 = p.sb([128, 128], F32, "mask01")
    zcol = p.sb([128, 1], F32, "zcol")
    idx_sb = p.sb([128, NLISTS], I32, "idx")
    banks = [p.ps([128, 512], F32, f"bank{i}") for i in range(7)]
    pbf = p.ps([128, 1024], BF16, "pbf")

    def consts():
        p.op("pool", lambda e: e.memset(ones_bf[:], 1.0), writes=["ones_bf"])
        p.op("pool", lambda e: e.memset(ident[:], 0.0), writes=["ident"])
        p.op("pool", lambda e: e.affine_select(out=ident[:], in_=ident[:], pattern=[[-1, 128]], compare_op=ALU.not_equal, fill=1.0, base=0, channel_multiplier=1), reads=["ident"], writes=["ident"])
        p.op("pool", lambda e: e.memset(maskrow[:], 0.0), writes=["maskrow"])
        p.op("pool", lambda e: e.affine_select(out=maskrow[:, 0:128], in_=maskrow[:, 0:128], pattern=[[1, 128]], compare_op=ALU.is_ge, fill=NEG, base=0, channel_multiplier=-1), reads=["maskrow"], writes=["maskrow"])
        p.op("pool", lambda e: e.memset(mask01[:], 1.0), writes=["mask01"])
        p.op("pool", lambda e: e.affine_select(out=mask01[:], in_=mask01[:], pattern=[[1, 128]], compare_op=ALU.is_ge, fill=0.0, base=0, channel_multiplier=-1), reads=["mask01"], writes=["mask01"])
        p.op("pool", lambda e: e.memset(zcol[:], 0.0), writes=["zcol"])
        p.dma("sp", "idx", idx_sb[:], idx_d[:, :], writes=["idx"])

    tiles = [(t0, TW) for t0 in range(0, NPT, TW)] + [(NPT, NS)]
    NPTI = NPT // TW

    def gather(key, dst, src2d, n, li, npart, eoff, reads, writes):
        view = src2d.rearrange("r (a n) -> (r a) n", n=n)
        p.custom_dma("pool", key, lambda e: e.indirect_dma_start(out=dst, out_offset=None, in_=view[:, :], in_offset=bass.IndirectOffsetOnAxis(ap=idx_sb[0:npart, li:li + 1], axis=0), element_offset=eoff), reads=list(reads) + ["idx"], writes=writes)

    def allgather(key, src, dst, reads, writes):
        p.custom_dma("pool", key, lambda e: e.collective_compute("AllGather", ALU.bypass, replica_groups=GROUPS4, ins=[src.opt()], outs=[dst.opt()]), reads=reads, writes=writes, inc=1)

    def t_phase(l_post, l_ffn1, do_final, first):
        p.arena_begin()
        hT = p.sb([128, KC, NTOK], BF16, "hT")
        GMAX = 6
        gT = p.sb([128, GMAX, NTOK], BF16, "gT")
        psr = Ring(banks, "bank")
        wst = Ring([p.sb([128, KC, 256], F32, f"wst{i}") for i in range(2)], "wst")
        wbf = Ring([p.sb([128, KC, 256], BF16, f"wbf{i}") for i in range(2)], "wbf")
        wost = Ring([p.sb([128, GMAX, 128], F32, f"wost{i}") for i in range(2)], "wost")
        wobf = Ring([p.sb([128, GMAX, 128], BF16, f"wobf{i}") for i in range(2)], "wobf")
        sq = p.sb([128, KC, 512], BF16, "sq")
        tmp = Ring([p.sb([128, 512], F32, f"tmp{i}") for i in range(12)], "tmp")
        finr = Ring([p.sb([128, 512], F32, f"fin{i}") for i in range(3)], "fin") if do_final else None
        if first:
            consts()
            for ti, (t0, n) in enumerate(tiles):
                p.dma("sp", f"xin{ti}", xT[:, :, t0:t0 + n], xT_d[:, t0:t0 + n].rearrange("(c p) n -> p c n", p=128), writes=[f"xT{ti}"])

        def load_gain(g_d, key):
            g = p.sb([128, KC], F32, key)
            p.dma("sp", key, g[:], g_d.rearrange("(c p) -> p c", p=128), writes=[key], allow_slow_non_contiguous=True)
            return g

        def rmsnorm_to_hT(g, gkey, out_fp32_dram=None):
            for ti, (t0, n) in enumerate(tiles):
                p.op("act", lambda e, t0=t0, n=n: e.activation(out=sq[:, :, 0:n], in_=xT[:, :, t0:t0 + n], func=AF.Square), reads=[f"xT{ti}"], writes=["sq"])
                ps, pk = psr.next()
                for c in range(KC):
                    p.op("pe", lambda e, c=c, n=n, ps=ps: e.matmul(ps[:, 0:n], lhsT=ones_bf[:], rhs=sq[:, c, 0:n], start=(c == 0), stop=(c == KC - 1)), reads=["sq", "ones_bf"], writes=[pk])
                sd, sk = tmp.next()
                p.op("act", lambda e, n=n, ps=ps, sd=sd: e.activation(out=sd[:, 0:n], in_=ps[:, 0:n], func=AF.Sqrt, scale=1.0 / D, bias=EPS), reads=[pk], writes=[sk])
                p.op("dve", lambda e, n=n, sd=sd: e.reciprocal(out=sd[:, 0:n], in_=sd[:, 0:n]), reads=[sk], writes=[sk])
                for c in range(KC):
                    if out_fp32_dram is None:
                        p.op("dve", lambda e, c=c, t0=t0, n=n, sd=sd: e.scalar_tensor_tensor(out=hT[:, c, t0:t0 + n], in0=xT[:, c, t0:t0 + n], scalar=g[:, c:c + 1], in1=sd[:, 0:n], op0=ALU.mult, op1=ALU.mult), reads=[f"xT{ti}", sk, gkey], writes=[f"hT{ti}"])
                    else:
                        o, ok = finr.next()
                        p.op("dve", lambda e, c=c, t0=t0, n=n, sd=sd, o=o: e.scalar_tensor_tensor(out=o[:, 0:n], in0=xT[:, c, t0:t0 + n], scalar=g[:, c:c + 1], in1=sd[:, 0:n], op0=ALU.mult, op1=ALU.mult), reads=[f"xT{ti}", sk, gkey], writes=[ok])
                        p.dma("sp", "o_" + ok, out_fp32_dram[c * 128:(c + 1) * 128, t0:t0 + n], o[:, 0:n], reads=[ok], writes=[])

        def load_w_block(w_d, c0, ncols):
            st, stk = wst.next()
            p.dma("sp", stk, st[:, :, 0:ncols], w_d[:, c0:c0 + ncols].rearrange("(c p) n -> p c n", p=128), writes=[stk])
            wb, wbk = wbf.next()
            p.op("act", lambda e: e.activation(out=wb[:, :, 0:ncols], in_=st[:, :, 0:ncols], func=AF.Copy), reads=[stk], writes=[wbk])
            return wb, wbk

        def ffn(pref, l):
            g = load_gain(Wd["norm_" + pref][l], f"g_{pref}{l}")
            wi_d = Wd[pref + "_wi"][l]
            wo_d = Wd[pref + "_wo"][l]
            rmsnorm_to_hT(g, f"g_{pref}{l}")
            f = 0
            while f < FC:
                gsz = min(GMAX, FC - f)
                f0 = f
                fl = 0
                while fl < gsz:
                    nb = min(2, gsz - fl)
                    wa, wak = load_w_block(wi_d, (f0 + fl) * 128, nb * 128)
                    wb_, wbk_ = load_w_block(wi_d, DFF + (f0 + fl) * 128, nb * 128)
                    for j in range(nb):
                        for ti, (t0, n) in enumerate(tiles):
                            pa, pak = psr.next()
                            pb, pbk = psr.next()
                            for c in range(KC):
                                p.op("pe", lambda e, c=c, j=j, t0=t0, n=n, pa=pa, wa=wa: e.matmul(pa[:, 0:n], lhsT=wa[:, c, j * 128:(j + 1) * 128], rhs=hT[:, c, t0:t0 + n], start=(c == 0), stop=(c == KC - 1)), reads=[wak, f"hT{ti}"], writes=[pak])
                            for c in range(KC):
                                p.op("pe", lambda e, c=c, j=j, t0=t0, n=n, pb=pb, wb_=wb_: e.matmul(pb[:, 0:n], lhsT=wb_[:, c, j * 128:(j + 1) * 128], rhs=hT[:, c, t0:t0 + n], start=(c == 0), stop=(c == KC - 1)), reads=[wbk_, f"hT{ti}"], writes=[pbk])
                            sa, sak = tmp.next()
                            p.op("act", lambda e, n=n, pa=pa, sa=sa: e.activation(out=sa[:, 0:n], in_=pa[:, 0:n], func=AF.Silu), reads=[pak], writes=[sak])
                            p.op("dve", lambda e, n=n, t0=t0, pb=pb, sa=sa, fi=fl + j: e.tensor_tensor(out=gT[:, fi, t0:t0 + n], in0=sa[:, 0:n], in1=pb[:, 0:n], op=ALU.mult), reads=[sak, pbk], writes=[f"gT{ti}"])
                    fl += nb
                for oc in range(KC):
                    st, stk = wost.next()
                    p.dma("sp", stk, st[:, 0:gsz, :], wo_d[f0 * 128:(f0 + gsz) * 128, oc * 128:(oc + 1) * 128].rearrange("(c p) n -> p c n", p=128), writes=[stk])
                    wo, wok = wobf.next()
                    p.op("dve", lambda e, st=st, wo=wo, gsz=gsz: e.tensor_copy(out=wo[:, 0:gsz, :], in_=st[:, 0:gsz, :]), reads=[stk], writes=[wok])
                    for ti, (t0, n) in enumerate(tiles):
                        ps, pk = psr.next()
                        for fi in range(gsz):
                            p.op("pe", lambda e, fi=fi, t0=t0, n=n, ps=ps, wo=wo: e.matmul(ps[:, 0:n], lhsT=wo[:, fi, :], rhs=gT[:, fi, t0:t0 + n], start=(fi == 0), stop=(fi == gsz - 1)), reads=[wok, f"gT{ti}"], writes=[pk])
                        p.op("dve", lambda e, oc=oc, t0=t0, n=n, ps=ps: e.scalar_tensor_tensor(out=xT[:, oc, t0:t0 + n], in0=ps[:, 0:n], scalar=0.5, in1=xT[:, oc, t0:t0 + n], op0=ALU.mult, op1=ALU.add), reads=[pk, f"xT{ti}"], writes=[f"xT{ti}"])
                f += gsz

        if l_post is not None:
            l = l_post
            NH = HGW // 128
            FXC = FOXW // 128
            gn = p.sb([128, 1], F32, "gn")
            p.dma("sp", "gn", gn[:], Wd["gnorm"][l].rearrange("(p o) -> p o", o=1), writes=["gn"])
            g2flat = G2[l].rearrange("b r s -> (b r) s")
            for ti, (t0, n) in enumerate(tiles):
                prompt = ti < NPTI
                for c in range(FXC):
                    a, ak = tmp.next()
                    if prompt:
                        gather(ak, a[:, 0:n], g2flat, TW, LI_P(0, c), 128, t0, reads=["G2"], writes=[ak])
                    else:
                        p.dma("sp", ak, a[:, 0:n], foxS[l][c * 128:(c + 1) * 128, :], reads=["foxS"], writes=[ak])
                    p.op("pool", lambda e, a=a, c=c, t0=t0, n=n: e.tensor_copy(out=hT[:, c, t0:t0 + n], in_=a[:, 0:n]), reads=[ak], writes=[f"hT{ti}"])
                for hd in range(NH):
                    a, ak = tmp.next()
                    b, bk = tmp.next()
                    if prompt:
                        gather(ak, a[:, 0:n], g2flat, TW, LI_P(1, hd), 128, t0, reads=["G2"], writes=[ak])
                        p.dma("sp", bk, b[:, 0:n], projP[l][HG0 + hd * 128:HG0 + (hd + 1) * 128, t0:t0 + n], reads=["projP"], writes=[bk])
                    else:
                        p.dma("sp", ak, a[:, 0:n], hoS[l][hd * 128:(hd + 1) * 128, :], reads=["hoS"], writes=[ak])
                        p.dma("sp", bk, b[:, 0:n], projS[l][HG0 + hd * 128:HG0 + (hd + 1) * 128, :], reads=["projS"], writes=[bk])
                    p.op("act", lambda e, a=a, n=n: e.activation(out=sq[:, 0, 0:n], in_=a[:, 0:n], func=AF.Square), reads=[ak], writes=["sq"])
                    ps, pk = psr.next()
                    p.op("pe", lambda e, n=n, ps=ps: e.matmul(ps[:, 0:n], lhsT=ones_bf[:], rhs=sq[:, 0, 0:n], start=True, stop=True), reads=["sq", "ones_bf"], writes=[pk])
                    sd, sk = tmp.next()
                    p.op("act", lambda e, n=n, ps=ps, sd=sd: e.activation(out=sd[:, 0:n], in_=ps[:, 0:n], func=AF.Sqrt, scale=1.0 / 128, bias=EPS), reads=[pk], writes=[sk])
                    p.op("dve", lambda e, n=n, sd=sd: e.reciprocal(out=sd[:, 0:n], in_=sd[:, 0:n]), reads=[sk], writes=[sk])
                    p.op("act", lambda e, b=b, n=n: e.activation(out=b[:, 0:n], in_=b[:, 0:n], func=AF.Silu), reads=[bk], writes=[bk])
                    p.op("dve", lambda e, a=a, sd=sd, n=n: e.scalar_tensor_tensor(out=a[:, 0:n], in0=a[:, 0:n], scalar=gn[:, 0:1], in1=sd[:, 0:n], op0=ALU.mult, op1=ALU.mult), reads=[ak, sk, "gn"], writes=[ak])
                    p.op("dve", lambda e, a=a, b=b, hd=hd, t0=t0, n=n: e.tensor_tensor(out=hT[:, FXC + hd, t0:t0 + n], in0=a[:, 0:n], in1=b[:, 0:n], op=ALU.mult), reads=[ak, bk], writes=[f"hT{ti}"])
            for oc in range(0, KC, 2):
                w, wk = load_w_block(Wd["w_out"][l], oc * 128, 256)
                for j in range(2):
                    for ti, (t0, n) in enumerate(tiles):
                        ps, pk = psr.next()
                        for c in range(KC):
                            p.op("pe", lambda e, c=c, j=j, t0=t0, n=n, ps=ps, w=w: e.matmul(ps[:, 0:n], lhsT=w[:, c, j * 128:(j + 1) * 128], rhs=hT[:, c, t0:t0 + n], start=(c == 0), stop=(c == KC - 1)), reads=[wk, f"hT{ti}"], writes=[pk])
                        p.op("dve", lambda e, oc=oc + j, t0=t0, n=n, ps=ps: e.tensor_tensor(out=xT[:, oc, t0:t0 + n], in0=ps[:, 0:n], in1=xT[:, oc, t0:t0 + n], op=ALU.add), reads=[pk, f"xT{ti}"], writes=[f"xT{ti}"])
            ffn("ffn2", l)

        if l_ffn1 is not None:
            l = l_ffn1
            ffn("ffn1", l)
            g = load_gain(Wd["norm_mix"][l], f"g_mix{l}")
            win_d = Wd["w_in"][l]
            negb = p.sb([NFG, 1], F32, "negb")
            p.dma("sp", "negb", negb[:], Wd["b_fgate"][l].rearrange("(p o) -> p o", o=1), writes=["negb"])
            p.op("dve", lambda e: e.tensor_scalar(out=negb[:], in0=negb[:], scalar1=-1.0, scalar2=None, op0=ALU.mult), reads=["negb"], writes=["negb"])
            rmsnorm_to_hT(g, f"g_mix{l}")
            FG0 = 3 * FOXW
            blocks = []
            c0 = 0
            while c0 < FG0:
                blocks.append((c0, min(256, FG0 - c0)))
                c0 += 256
            blocks.append((FG0, NFG))
            c0 = FG0 + NFG
            while c0 < DIN:
                blocks.append((c0, min(256, DIN - c0)))
                c0 += 256
            for (c0, ncols) in blocks:
                w, wk = load_w_block(win_d, c0, ncols)
                j0 = 0
                while j0 < ncols:
                    m = min(128, ncols - j0)
                    r0 = c0 + j0
                    rkey = f"projP_r{r0}"
                    for ti, (t0, n) in enumerate(tiles):
                        prompt = ti < NPTI
                        ps, pk = psr.next()
                        for c in range(KC):
                            p.op("pe", lambda e, c=c, j0=j0, m=m, t0=t0, n=n, ps=ps, w=w: e.matmul(ps[0:m, 0:n], lhsT=w[:, c, j0:j0 + m], rhs=hT[:, c, t0:t0 + n], start=(c == 0), stop=(c == KC - 1)), reads=[wk, f"hT{ti}"], writes=[pk])
                        o, ok = tmp.next()
                        p.op("act", lambda e, m=m, n=n, ps=ps, o=o: e.activation(out=o[0:m, 0:n], in_=ps[0:m, 0:n], func=AF.Copy), reads=[pk], writes=[ok])
                        if prompt:
                            p.dma("sp", "o_" + ok, projP[l][r0:r0 + m, t0:t0 + n], o[0:m, 0:n], reads=[ok], writes=[rkey + f"_{ti}"])
                        else:
                            p.dma("sp", "o_" + ok, projS[l][r0:r0 + m, :], o[0:m, 0:n], reads=[ok], writes=[])
                        if FOXW <= r0 < 3 * FOXW:
                            p.dma("sp", "o_" + ok, kv_d[l][r0 - FOXW:r0 - FOXW + m, t0:t0 + n], o[0:m, 0:n], reads=[ok], writes=[])
                        if c0 == FG0:
                            lt, lk = tmp.next()
                            p.op("act", lambda e, m=m, n=n, o=o, lt=lt: e.activation(out=lt[0:m, 0:n], in_=o[0:m, 0:n], func=AF.Exp, scale=-1.0, bias=negb[:, 0:1]), reads=[ok, "negb"], writes=[lk])
                            p.op("act", lambda e, m=m, n=n, lt=lt: e.activation(out=lt[0:m, 0:n], in_=lt[0:m, 0:n], func=AF.Ln, bias=1.0), reads=[lk], writes=[lk])
                            p.op("dve", lambda e, m=m, n=n, lt=lt: e.tensor_scalar(out=lt[0:m, 0:n], in0=lt[0:m, 0:n], scalar1=-1.0, scalar2=None, op0=ALU.mult), reads=[lk], writes=[lk])
                            p.dma("sp", "o_" + lk, lfo_d[l][:, t0:t0 + n], lt[0:m, 0:n], reads=[lk], writes=[])
                            if prompt:
                                p.dma("sp", "o_" + lk, lfP[l][:, t0:t0 + n], lt[0:m, 0:n], reads=[lk], writes=[f"lfP_{ti}"])
                            else:
                                p.dma("sp", "o_" + lk, lfS[l][:, :], lt[0:m, 0:n], reads=[lk], writes=[])
                    prk = [rkey + f"_{ti}" for ti in range(NPTI)]
                    if r0 < 3 * FOXW:
                        b = r0 // 128
                        allgather(f"ag1_{l}", projP[l][r0:r0 + 128, :], G1a[l][b], reads=prk, writes=[f"G1a_{b}"])
                    elif c0 == FG0:
                        allgather(f"ag1_{l}", lfP[l][:, :], G1l[l][:, :], reads=[f"lfP_{ti}" for ti in range(NPTI)], writes=["G1l"])
                    elif r0 < HG0:
                        b = (r0 - H0) // 128
                        allgather(f"ag1_{l}", projP[l][r0:r0 + 128, :], G1h[l][b], reads=prk, writes=[f"G1h_{b}"])
                    j0 += m
            for _ in range(2):
                allgather(f"ag1_{l}", fl_src[:, :], fl_dst[:, :], reads=[], writes=["flush"])

        if do_final:
            g = load_gain(Wd["norm_final"], "g_fin")
            rmsnorm_to_hT(g, "g_fin", out_fp32_dram=yT_d)

    def m_phase(l):
        p.arena_begin()
        g1a = G1a[l].rearrange("b r n -> (b r) n")
        g1h = G1h[l].rearrange("b r n -> (b r) n")
        g1l = G1l[l]
        pS = Ring(banks[0:2], "bank")
        pO = Ring(banks[2:4], "bankO")
        pH = Ring(banks[4:7], "bankH")
        pO.name = "bankO"
        stg = Ring([p.sb([128, FSEG], F32, f"stg{i}") for i in range(3)], "stg")
        cbr = Ring([p.sb([128, FSEG], BF16, f"cb{i}") for i in range(3)], "cb")
        hOne = p.sb([128, FSEG], F32, "hOne")
        p.op("dve", lambda e: e.memset(hOne[:], 1.0), writes=["hOne"])
        carry = p.sb([128, 1], F32, "carry")
        ptr = Ring([p.sb([128, 512], BF16, f"pt{i}") for i in range(3)], "pt")
        fo = Ring([p.sb([64, 512], F32, f"fo{i}") for i in range(2)], "fo")
        frl = Ring([p.sb([64, 512], F32, f"frl{i}") for i in range(1)], "frl")

        def mkrings(tag, nb, TKx, TQx):
            NK = (TKx + 127) // 128
            kr = Ring([p.sb([70, TKx], BF16, f"{tag}kaug{i}") for i in range(nb)], f"{tag}kaug")
            qr = Ring([p.sb([70, TQx], BF16, f"{tag}qaug{i}") for i in range(nb)], f"{tag}qaug")
            vr = Ring([p.sb([128, NK, 128], BF16, f"{tag}vaug{i}") for i in range(nb)], f"{tag}vaug")
            for i in range(nb):
                t = kr.tiles[i]
                p.op("dve", lambda e, t=t: e.memset(t[64:70, :], 1.0), writes=[f"{tag}kaug{i}c"])
                t = qr.tiles[i]
                p.op("dve", lambda e, t=t: e.memset(t[64:70, :], -1.0), writes=[f"{tag}qaug{i}c"])
                t = vr.tiles[i]
                p.op("dve", lambda e, t=t: e.memset(t[:, :, 64:128], 1.0), writes=[f"{tag}vaug{i}o"])
            return kr, qr, vr

        arena_mark = p.arena_off
        rings_s = mkrings("S", 2, TKS, TQS)
        stgA = Ring([p.sb([128, PAST], F32, f"stgA{i}") for i in range(4)], "stgA")
        qkS = p.sb([64, 3, 8, NS], F32, "qkS")
        vtb = p.sb([64, TQS], BF16, "vtb")
        lf = p.sb([NFS, TKS], F32, "fs_lf")
        cs = p.sb([NFS, 3, TKS], BF16, "fs_cs")

        def fox_core(ka, kak, qa_, qak, va, vak, TQ, TK, out_cb):
            P = TK - TQ
            W = min(512, TQ)
            for qt in range(TQ // W):
                q0 = qt * W
                po, pok = pO.next()
                last_kt = (P + q0 + W - 1) // 128
                pairs = []
                for kt in range(last_kt + 1):
                    nk = min(128, TK - kt * 128)
                    qa = max(0, kt * 128 - P - q0)
                    if qa >= W:
                        continue
                    pairs.append((kt, nk, qa, (P + q0 + qa) < (kt * 128 + nk - 1)))

                def mm2(kt, nk, qa, pt, ptk, first, last, po=po, pok=pok):
                    p.op("pe", lambda e: e.matmul(po[:, qa:W], lhsT=va[0:nk, kt, :], rhs=pt[0:nk, qa:W], start=first, stop=last), reads=[vak, vak + "o", ptk], writes=[pok])
                pend = None
                for i, (kt, nk, qa, diag) in enumerate(pairs):
                    ps, psk = pS.next()
                    p.op("pe", lambda e, ps=ps, kt=kt, nk=nk, qa=qa, diag=diag, q0=q0: e.matmul(ps[0:nk, qa:W], lhsT=ka[0:70, kt * 128:kt * 128 + nk], rhs=qa_[0:70, q0 + qa:q0 + W], start=True, stop=not diag), reads=[kak, kak + "c", qak, qak + "c"], writes=[psk])
                    if diag:
                        p.op("pe", lambda e, ps=ps, nk=nk, qa=qa: e.matmul(ps[0:nk, qa:W], lhsT=ident[0:nk, 0:nk], rhs=maskrow[0:nk, 0:W - qa], start=False, stop=True), reads=["ident", "maskrow"], writes=[psk])
                    pt, ptk = ptr.next()
                    p.op("act", lambda e, ps=ps, pt=pt, nk=nk, qa=qa: e.activation(out=pt[0:nk, qa:W], in_=ps[0:nk, qa:W], func=AF.Exp, scale=HD ** -0.5), reads=[psk], writes=[ptk])
                    if pend is not None:
                        mm2(*pend)
                    pend = (kt, nk, qa, pt, ptk, i == 0, i == len(pairs) - 1)
                mm2(*pend)
                rl, rlk = frl.next()
                o, ok = fo.next()
                p.op("dve", lambda e, po=po, rl=rl: e.reciprocal(out=rl[0:64, 0:W], in_=po[64:128, 0:W]), reads=[pok], writes=[rlk])
                p.op("dve", lambda e, po=po, rl=rl, o=o: e.tensor_tensor(out=o[0:64, 0:W], in0=po[0:64, 0:W], in1=rl[0:64, 0:W], op=ALU.mult), reads=[pok, rlk], writes=[ok])
                out_cb(o, ok, q0, W)
                yield

        def fox_core_small(ka, kak, qa_, qak, va, vak, TQ, TK, out_cb, strided=False):
            P = TK - TQ
            W = TQ
            NKT = (TK + 127) // 128
            assert NKT * W <= 512
            ps, psk = pS.next()
            for kt in range(NKT):
                nk = min(128, TK - kt * 128)
                diag = P < (kt * 128 + nk - 1)
                nfull_ = TK // 128
                kcols = slice(kt, nfull_ * 128, nfull_) if (strided and kt < nfull_) else slice(kt * 128, kt * 128 + nk)
                p.op("pe", lambda e, kt=kt, nk=nk, diag=diag, kcols=kcols: e.matmul(ps[0:nk, kt * W:(kt + 1) * W], lhsT=ka[0:70, kcols], rhs=qa_[0:70, 0:W], start=True, stop=not diag), reads=[kak, kak + "c", qak, qak + "c"], writes=[psk])
                if diag:
                    p.op("pe", lambda e, kt=kt, nk=nk: e.matmul(ps[0:nk, kt * W:(kt + 1) * W], lhsT=ident[0:nk, 0:nk], rhs=maskrow[0:nk, 0:W], start=False, stop=True), reads=["ident", "maskrow"], writes=[psk])
            pt, ptk = ptr.next()
            nfull = TK // 128
            rem = TK - nfull * 128
            p.op("act", lambda e: e.activation(out=pt[:, 0:nfull * W], in_=ps[:, 0:nfull * W], func=AF.Exp, scale=HD ** -0.5), reads=[psk], writes=[ptk])
            if rem:
                p.op("act", lambda e: e.activation(out=pt[0:rem, nfull * W:(nfull + 1) * W], in_=ps[0:rem, nfull * W:(nfull + 1) * W], func=AF.Exp, scale=HD ** -0.5), reads=[psk], writes=[ptk])
            po, pok = pO.next()
            for kt in range(NKT):
                nk = min(128, TK - kt * 128)
                p.op("pe", lambda e, kt=kt, nk=nk: e.matmul(po[:, 0:W], lhsT=va[0:nk, kt, :], rhs=pt[0:nk, kt * W:(kt + 1) * W], start=(kt == 0), stop=(kt == NKT - 1)), reads=[vak, vak + "o", ptk], writes=[pok])
            rl, rlk = frl.next()
            o, ok = fo.next()
            p.op("dve", lambda e: e.reciprocal(out=rl[0:64, 0:W], in_=po[64:128, 0:W]), reads=[pok], writes=[rlk])
            p.op("dve", lambda e: e.tensor_tensor(out=o[0:64, 0:W], in0=po[0:64, 0:W], in1=rl[0:64, 0:W], op=ALU.mult), reads=[pok, rlk], writes=[ok])
            out_cb(o, ok, 0, W)
            yield

        def csplit_seg(src, srck, dst, dstk, nu, n, emit_row):
            for r in range(3):
                cb, cbk = cbr.next()
                p.op("dve", lambda e, cb=cb, src=src: e.tensor_copy(out=cb[0:nu, 0:n], in_=src[0:nu, 0:n]), reads=[srck], writes=[cbk])
                if r < 2:
                    p.op("dve", lambda e, cb=cb, src=src, dst=dst: e.tensor_tensor(out=dst[0:nu, 0:n], in0=src[0:nu, 0:n], in1=cb[0:nu, 0:n], op=ALU.subtract), reads=[srck, cbk], writes=[dstk])
                emit_row(r, cb, cbk)
                src, srck, dst, dstk = dst, dstk, src, srck

        def fox_prompt_unit(u):
            kaug, qaug, vaug = rings_p
            ka, kak = kaug.next()
            qa_, qak = qaug.next()
            va, vak = vaug.next()
            for c0 in range(0, SEQ, FSEG):
                r, col = c0 // NPT, c0 % NPT
                s, sk = stg.next()
                gather(sk, s[0:64, 0:FSEG], g1a, FSEG, LI_F(1, u, r), 64, col, reads=["G1"], writes=[sk])
                p.op("dve", lambda e, s=s, c0=c0: e.tensor_copy(out=ka[0:64, c0:c0 + FSEG], in_=s[0:64, 0:FSEG]), reads=[sk], writes=[kak])
                s, sk = stg.next()
                gather(sk, s[0:64, 0:FSEG], g1a, FSEG, LI_F(0, u, r), 64, col, reads=["G1"], writes=[sk])
                p.op("act", lambda e, s=s, c0=c0: e.activation(out=qa_[0:64, c0:c0 + FSEG], in_=s[0:64, 0:FSEG], func=AF.Copy), reads=[sk], writes=[qak])
                s, sk = stg.next()
                gather(sk, s[0:64, 0:FSEG], g1a, FSEG, LI_F(2, u, r), 64, col, reads=["G1"], writes=[sk])
                vb, vbk = cbr.next()
                p.op("dve", lambda e, s=s, vb=vb: e.tensor_copy(out=vb[0:64, 0:FSEG], in_=s[0:64, 0:FSEG]), reads=[sk], writes=[vbk])
                for k8 in range(FSEG // 128):
                    p.op("pe", lambda e, vb=vb, k8=k8: e.transpose(pbf[:, k8 * 64:(k8 + 1) * 64], vb[0:64, k8 * 128:(k8 + 1) * 128], ident[0:64, 0:64]), reads=[vbk, "ident"], writes=["pbf"])
                kt0 = c0 // 128
                p.op("act", lambda e, kt0=kt0: e.activation(out=va[:, kt0:kt0 + FSEG // 128, 0:64], in_=pbf[:, 0:FSEG // 2].rearrange("p (t d) -> p t d", d=64), func=AF.Copy), reads=["pbf"], writes=[vak])
                lw, lwk = stg.next()
                gather(lwk, lw[0:2, 0:FSEG], g1l, FSEG, LI_FL(u, r), 2, col, reads=["G1"], writes=[lwk])
                lc, lck = stg.next()
                init = 0.0 if c0 == 0 else carry[0:1, 0:1]
                p.op("dve", lambda e, lw=lw, lc=lc, init=init: e.tensor_tensor_scan(out=lc[0:1, 0:FSEG], data0=hOne[0:1, 0:FSEG], data1=lw[0:1, 0:FSEG], initial=init, op0=ALU.mult, op1=ALU.add), reads=[lwk, "hOne", "carry"], writes=[lck])
                p.op("dve", lambda e, lc=lc: e.tensor_copy(out=carry[0:1, 0:1], in_=lc[0:1, FSEG - 1:FSEG]), reads=[lck], writes=["carry"])
                p.op("dve", lambda e, lc=lc: e.tensor_scalar(out=lc[0:1, 0:FSEG], in0=lc[0:1, 0:FSEG], scalar1=8.0, scalar2=None, op0=ALU.mult), reads=[lck], writes=[lck])

                def emit_row(r_, cb, cbk, c0=c0):
                    p.dma("sp", kak + "c", ka[67 + r_:68 + r_, c0:c0 + FSEG], cb[0:1, 0:FSEG], reads=[cbk, kak + "c"], writes=[kak + "c"])
                    p.dma("sp", qak + "c", qa_[64 + r_:65 + r_, c0:c0 + FSEG], cb[0:1, 0:FSEG], reads=[cbk, qak + "c"], writes=[qak + "c"])
                csplit_seg(lc, lck, lw, lwk, 1, FSEG, emit_row)
                yield

            def out_cb(o, ok, q0, W, u=u):
                for half in range(2):
                    p.dma("sp", "o_" + ok, S2[l][u * 2 + half, :, q0:q0 + W], o[half * 32:(half + 1) * 32, 0:W], reads=[ok], writes=[f"S2f{u}_{q0}_{half}"])
            yield from fox_core(ka, kak, qa_, qak, va, vak, SEQ, SEQ, out_cb)
            for half in range(2):
                b = u * 2 + half
                allgather(f"ag2_{l}", S2[l][b], G2[l][b], reads=[f"S2f{u}_{q0}_{half}" for q0 in range(0, SEQ, 512)], writes=[f"G2_{b}"])

        def fox_prompt_all():
            for u in range(2):
                yield from fox_prompt_unit(u)

        hA = hK = hB = hNB = hQ = hVf = hVt = QpT = KpT = hVb = hO = ebl = eqr = ekr = amr = ktr = S = Sb = St = lbt = rings_p = NCS = None
        def alloc_stage_b():
            nonlocal hA, hK, hB, hNB, hQ, hVf, hVt, QpT, KpT, hVb, hO, ebl, eqr, ekr, amr, ktr, S, Sb, St, lbt, rings_p, NCS
            hA = p.sb([128, HSEG], F32, "hA")
            hK = p.sb([128, HSEG], F32, "hK")
            hB = p.sb([128, HSEG], F32, "hB")
            hNB = p.sb([128, HSEG], F32, "hNB")
            hQ = p.sb([128, HSEG], F32, "hQ")
            hVf = p.sb([128, HSEG], F32, "hVf")
            hVt = p.sb([128, HSEG], BF16, "hVt")
            QpT = p.sb([128, HSEG], BF16, "QpT")
            KpT = p.sb([128, HSEG], BF16, "KpT")
            NCS = HSEG // CH
            hVb = p.sb([CH, NCS, 128], BF16, "hVb")
            hO = p.sb([128, HSEG], F32, "hO")
            ebl = p.sb([128, NCS], F32, "ebl")
            eqr = Ring([p.sb([128, CH], F32, f"eq{i}") for i in range(2)], "eq")
            ekr = Ring([p.sb([128, CH], F32, f"ek{i}") for i in range(2)], "ek")
            amr = Ring([p.sb([CH, CH], BF16, f"am{i}") for i in range(2)], "am")
            ktr = Ring([p.sb([CH, 128], BF16, f"kt{i}") for i in range(2)], "kt")
            S = p.sb([128, 128], F32, "S")
            Sb = p.sb([128, 128], BF16, "Sb")
            St = p.sb([128, 128], F32, "St")
            lbt = p.sb([128, 4], F32, "lbt")

            rings_p = mkrings("P", 1, SEQ, SEQ)

        def hgrn_unit(T, C, load_seg, lb_ap, s0_ap, out_seg, sfin_ap):
            p.dma("sp", "lbt", lbt[:, 0:3], lb_ap, reads=["lbt"], writes=["lbt"])
            p.op("dve", lambda e: e.tensor_tensor(out=lbt[:, 3:4], in0=lbt[:, 1:2], in1=lbt[:, 0:1], op=ALU.subtract), reads=["lbt"], writes=["lbt"])
            p.op("act", lambda e: e.activation(out=lbt[:, 3:4], in_=lbt[:, 3:4], func=AF.Sigmoid), reads=["lbt"], writes=["lbt"])
            p.op("dve", lambda e: e.tensor_tensor(out=lbt[:, 0:1], in0=lbt[:, 3:4], in1=lbt[:, 2:3], op=ALU.mult), reads=["lbt"], writes=["lbt"])
            p.op("dve", lambda e: e.tensor_scalar(out=lbt[:, 1:2], in0=lbt[:, 0:1], scalar1=-1.0, scalar2=1.0, op0=ALU.mult, op1=ALU.add), reads=["lbt"], writes=["lbt"])
            if s0_ap is None:
                p.op("dve", lambda e: e.memset(S[:], 0.0), reads=["S"], writes=["S"])
                p.op("dve", lambda e: e.memset(Sb[:], 0.0), reads=["Sb"], writes=["Sb"])
            else:
                p.dma("sp", "S0", S[:], s0_ap, reads=["S"], writes=["S"])
                p.op("dve", lambda e: e.tensor_copy(out=Sb[:], in_=S[:]), reads=["S"], writes=["Sb"])
            s0 = 0
            while s0 < T:
                n = min(HSEG, T - s0)
                nch = n // C
                load_seg(s0, n)
                p.op("dve", lambda e, n=n: e.tensor_copy(out=hVt[:, 0:n], in_=hVf[:, 0:n]), reads=["hVf"], writes=["hVt"])
                for ci in range(nch):
                    p.op("pe", lambda e, ci=ci: e.transpose(pbf[0:C, ci * 128:(ci + 1) * 128], hVt[:, ci * C:(ci + 1) * C], ident[:]), reads=["hVt", "ident"], writes=["pbf"])
                p.op("act", lambda e, nch=nch: e.activation(out=hVb[0:C, 0:nch, :], in_=pbf[0:C, 0:nch * 128].rearrange("p (t d) -> p t d", d=128), func=AF.Copy), reads=["pbf"], writes=["hVb"])
                p.op("act", lambda e, n=n: e.activation(out=hA[:, 0:n], in_=hA[:, 0:n], func=AF.Sigmoid), reads=["hA"], writes=["hA"])
                p.op("act", lambda e, n=n: e.activation(out=hQ[:, 0:n], in_=hQ[:, 0:n], func=AF.Silu), reads=["hQ"], writes=["hQ"])
                p.op("dve", lambda e, n=n: e.tensor_scalar(out=hA[:, 0:n], in0=hA[:, 0:n], scalar1=lbt[:, 1:2], scalar2=lbt[:, 0:1], op0=ALU.mult, op1=ALU.add), reads=["hA", "lbt"], writes=["hA"])
                p.op("dve", lambda e, n=n: e.tensor_scalar(out=hK[:, 0:n], in0=hA[:, 0:n], scalar1=-1.0, scalar2=1.0, op0=ALU.mult, op1=ALU.add), reads=["hA"], writes=["hK"])
                p.op("act", lambda e, n=n: e.activation(out=hA[:, 0:n], in_=hA[:, 0:n], func=AF.Ln), reads=["hA"], writes=["hA"])
                p.op("dve", lambda e, n=n: e.tensor_tensor_scan(out=hB[:, 0:n], data0=hOne[:, 0:n], data1=hA[:, 0:n], initial=0.0, op0=ALU.mult, op1=ALU.add), reads=["hA", "hOne"], writes=["hB"])
                p.op("dve", lambda e, n=n: e.tensor_scalar(out=hNB[:, 0:n], in0=hB[:, 0:n], scalar1=-1.0, scalar2=None, op0=ALU.mult), reads=["hB"], writes=["hNB"])
                for ci in range(nch):
                    c0 = ci * C
                    bq = zcol[:, 0:1] if ci == 0 else hNB[:, c0 - 1:c0]
                    bk = zcol[:, 0:1] if ci == 0 else hB[:, c0 - 1:c0]
                    eq, eqk = eqr.next()
                    ek, ekk = ekr.next()
                    p.op("act", lambda e, eq=eq, c0=c0, bq=bq: e.activation(out=eq[:, 0:C], in_=hB[:, c0:c0 + C], func=AF.Exp, bias=bq), reads=["hB", "hNB", "zcol"], writes=[eqk])
                    p.op("act", lambda e, ek=ek, c0=c0, bk=bk: e.activation(out=ek[:, 0:C], in_=hB[:, c0:c0 + C], func=AF.Exp, scale=-1.0, bias=bk), reads=["hB", "zcol"], writes=[ekk])
                    p.op("dve", lambda e, eq=eq, c0=c0: e.tensor_tensor(out=QpT[:, c0:c0 + C], in0=hQ[:, c0:c0 + C], in1=eq[:, 0:C], op=ALU.mult), reads=["hQ", eqk], writes=["QpT"])
                    p.op("dve", lambda e, ek=ek, c0=c0: e.tensor_tensor(out=KpT[:, c0:c0 + C], in0=hK[:, c0:c0 + C], in1=ek[:, 0:C], op=ALU.mult), reads=["hK", ekk], writes=["KpT"])
                    p.op("dve", lambda e, eq=eq, ci=ci: e.tensor_copy(out=ebl[:, ci:ci + 1], in_=eq[:, C - 1:C]), reads=[eqk], writes=["ebl"])
                for ci in range(nch):
                    c0 = ci * C
                    pa, pak = pH.next()
                    p.op("pe", lambda e, pa=pa, c0=c0: e.matmul(pa[0:C, 0:C], lhsT=KpT[:, c0:c0 + C], rhs=QpT[:, c0:c0 + C], start=True, stop=True), reads=["KpT", "QpT"], writes=[pak])
                    am, amk = amr.next()
                    p.op("dve", lambda e, pa=pa, am=am: e.tensor_tensor(out=am[0:C, 0:C], in0=pa[0:C, 0:C], in1=mask01[0:C, 0:C], op=ALU.mult), reads=[pak, "mask01"], writes=[amk])
                    p.op("pe", lambda e, c0=c0: e.transpose(pbf[0:C, 0:128], KpT[:, c0:c0 + C], ident[:]), reads=["KpT", "ident"], writes=["pbf"])
                    ktt, ktk = ktr.next()
                    p.op("act", lambda e, t=ktt: e.activation(out=t[0:C, :], in_=pbf[0:C, 0:128], func=AF.Copy), reads=["pbf"], writes=[ktk])
                    po, pok = pH.next()
                    p.op("pe", lambda e, po=po, am=am, ci=ci: e.matmul(po[:, 0:C], lhsT=hVb[0:C, ci, :], rhs=am[0:C, 0:C], start=True, stop=False), reads=["hVb", amk], writes=[pok])
                    p.op("pe", lambda e, po=po, c0=c0: e.matmul(po[:, 0:C], lhsT=Sb[:], rhs=QpT[:, c0:c0 + C], start=False, stop=True), reads=["Sb", "QpT"], writes=[pok])
                    p.op("act", lambda e, po=po, c0=c0: e.activation(out=hO[:, c0:c0 + C], in_=po[:, 0:C], func=AF.Copy), reads=[pok], writes=["hO"])
                    pu, puk = pH.next()
                    p.op("pe", lambda e, pu=pu, t=ktt, ci=ci: e.matmul(pu[:, 0:128], lhsT=t[0:C, :], rhs=hVb[0:C, ci, :], start=True, stop=True), reads=[ktk, "hVb"], writes=[puk])
                    p.op("dve", lambda e, pu=pu: e.tensor_tensor(out=St[:], in0=S[:], in1=pu[:, 0:128], op=ALU.add), reads=["S", puk], writes=["St"])
                    p.op("dve", lambda e, ci=ci: e.tensor_scalar(out=S[:], in0=St[:], scalar1=ebl[:, ci:ci + 1], scalar2=None, op0=ALU.mult), reads=["St", "ebl"], writes=["S"])
                    p.op("act", lambda e, ci=ci: e.activation(out=Sb[:], in_=St[:], func=AF.Copy, scale=ebl[:, ci:ci + 1]), reads=["St", "ebl"], writes=["Sb"])
                    if ci % 2 == 1:
                        yield
                out_seg(s0, n)
                yield
                s0 += n
            p.dma("sp", "Sout", sfin_ap, S[:], reads=["S"], writes=[])

        def hp_load(s0, n):
            r, col = s0 // NPT, s0 % NPT
            gather("hQ", hQ[:, 0:n], g1h, HSEG, LI_H(0, r), 128, col, reads=["G1", "hQ"], writes=["hQ"])
            gather("hA", hA[:, 0:n], g1h, HSEG, LI_H(1, r), 128, col, reads=["G1", "hA"], writes=["hA"])
            gather("hVf", hVf[:, 0:n], g1h, HSEG, LI_H(2, r), 128, col, reads=["G1", "hVf"], writes=["hVf"])

        def hp_out(s0, n):
            for q4 in range(4):
                p.dma("sp", "hO", S2[l][4 + q4, :, s0:s0 + n], hO[q4 * 32:(q4 + 1) * 32, 0:n], reads=["hO"], writes=[f"S2h_{s0}_{q4}"])
        def hgrn_prompt_all():
            yield from hgrn_unit(SEQ, CH, hp_load, hplb_d[l], None, hp_out, hpS_d[l])
            for q4 in range(4):
                allgather(f"ag2_{l}", S2[l][4 + q4], G2[l][4 + q4], reads=[f"S2h_{s0}_{q4}" for s0 in range(0, SEQ, HSEG)], writes=[f"G2_{4 + q4}"])

        def fox_sample_prep():
            for t3 in range(3):
                p.dma("sp", "qkS", qkS[:, t3, :, :], projS[l][t3 * FOXW:(t3 + 1) * FOXW, :].rearrange("(h d) n -> d h n", d=64), reads=["projS"], writes=["qkS"])
            p.dma("sp", "fslf", lf[:, 0:PAST], clf_d[l][:, :], writes=["fslf"])
            for bl in range(NSB):
                p.dma("sp", "fslf", lf[bl * 8:(bl + 1) * 8, PAST:TKS], lfS[l][:, bl * TQS:(bl + 1) * TQS], reads=["lfS"], writes=["fslf"])
            c0 = 0
            while c0 < TKS:
                n = min(FSEG, TKS - c0)
                lc, lck = stg.next()
                init = 0.0 if c0 == 0 else carry[0:NFS, 0:1]
                p.op("dve", lambda e, lc=lc, c0=c0, n=n, init=init: e.tensor_tensor_scan(out=lc[0:NFS, 0:n], data0=hOne[0:NFS, 0:n], data1=lf[:, c0:c0 + n], initial=init, op0=ALU.mult, op1=ALU.add), reads=["fslf", "hOne", "carry"], writes=[lck])
                p.op("dve", lambda e, lc=lc, n=n: e.tensor_copy(out=carry[0:NFS, 0:1], in_=lc[0:NFS, n - 1:n]), reads=[lck], writes=["carry"])
                p.op("dve", lambda e, lc=lc, n=n: e.tensor_scalar(out=lc[0:NFS, 0:n], in0=lc[0:NFS, 0:n], scalar1=8.0, scalar2=None, op0=ALU.mult), reads=[lck], writes=[lck])
                lw, lwk = stg.next()

                def emit_row(r_, cb, cbk, c0=c0, n=n):
                    p.op("dve", lambda e, cb=cb: e.tensor_copy(out=cs[:, r_, c0:c0 + n], in_=cb[0:NFS, 0:n]), reads=[cbk], writes=["fscs"])
                csplit_seg(lc, lck, lw, lwk, NFS, n, emit_row)
                c0 += n
        def fox_sample_unit(u):
            bl, h = u // 8, u % 8
            kaug, qaug, vaug = rings_s
            ka, kak = kaug.next()
            qa_, qak = qaug.next()
            va, vak = vaug.next()
            nfull = PAST // 128
            cols = slice(bl * TQS, (bl + 1) * TQS)
            s, sk = stgA.next()
            p.dma("sp", sk, s[0:64, 0:PAST], ckT_d[l][u, :, :], writes=[sk])
            p.op("dve", lambda e, s=s: e.tensor_copy(out=ka[0:64, 0:PAST], in_=s[0:64, 0:PAST]), reads=[sk], writes=[kak])
            p.op("dve", lambda e: e.tensor_copy(out=ka[0:64, PAST:TKS], in_=qkS[:, 1, h, cols]), reads=["qkS"], writes=[kak])
            p.op("act", lambda e: e.activation(out=qa_[0:64, 0:TQS], in_=qkS[:, 0, h, cols], func=AF.Copy), reads=["qkS"], writes=[qak])
            for r in range(3):
                p.dma("act", kak + "c", ka[67 + r:68 + r, 0:TKS], cs[u:u + 1, r, 0:TKS], reads=["fscs", kak + "c"], writes=[kak + "c"])
                p.dma("act", qak + "c", qa_[64 + r:65 + r, 0:TQS], cs[u:u + 1, r, PAST:TKS], reads=["fscs", qak + "c"], writes=[qak + "c"])
            s, sk = stgA.next()
            p.dma("sp", sk, s[:, 0:nfull * 64], cv_d[l][u].rearrange("(p t) d -> p (t d)", t=nfull), writes=[sk])
            p.op("act", lambda e, s=s: e.activation(out=va[:, 0:nfull, 0:64], in_=s[:, 0:nfull * 64].rearrange("p (t d) -> p t d", d=64), func=AF.Copy), reads=[sk], writes=[vak])
            p.op("dve", lambda e: e.tensor_copy(out=vtb[:, :], in_=qkS[:, 2, h, cols]), reads=["qkS"], writes=["vtb"])
            p.op("pe", lambda e: e.transpose(pbf[0:TQS, 0:64], vtb[:, :], ident[0:64, 0:64]), reads=["vtb", "ident"], writes=["pbf"])
            p.op("act", lambda e: e.activation(out=va[0:TQS, nfull, 0:64], in_=pbf[0:TQS, 0:64], func=AF.Copy), reads=["pbf"], writes=[vak])

            def out_cb(o, ok, q0, W, bl=bl, h=h):
                p.dma("sp", "o_" + ok, foxS[l][h * 64:(h + 1) * 64, bl * TQS:(bl + 1) * TQS], o[0:64, 0:W], reads=[ok], writes=[])
            yield from fox_core_small(ka, kak, qa_, qak, va, vak, TQS, TKS, out_cb, strided=True)

        def fox_sample_all():
            fox_sample_prep()
            for u in range(NFS):
                yield from fox_sample_unit(u)

        def hgrn_sample_unit(u):
            bl, hd = u // 4, u % 4

            def hs_load(s0, n, bl=bl, hd=hd):
                cols = slice(bl * TQS, (bl + 1) * TQS)
                p.dma("sp", "hQ", hQ[:, 0:n], projS[l][H0 + hd * 128:H0 + (hd + 1) * 128, cols], reads=["projS", "hQ"], writes=["hQ"])
                p.dma("sp", "hA", hA[:, 0:n], projS[l][H0 + HGW + hd * 128:H0 + HGW + (hd + 1) * 128, cols], reads=["projS", "hA"], writes=["hA"])
                p.dma("sp", "hVf", hVf[:, 0:n], projS[l][H0 + 2 * HGW + hd * 128:H0 + 2 * HGW + (hd + 1) * 128, cols], reads=["projS", "hVf"], writes=["hVf"])

            def hs_out(s0, n, bl=bl, hd=hd):
                p.dma("sp", "hO", hoS[l][hd * 128:(hd + 1) * 128, bl * TQS:(bl + 1) * TQS], hO[:, 0:n], reads=["hO"], writes=[])
            yield from hgrn_unit(TQS, min(CH, TQS), hs_load, hslb_d[l][u], st0_d[l][u], hs_out, hsS_d[l][u])

        def hgrn_sample_all():
            for u in range(NHS):
                yield from hgrn_sample_unit(u)

        def interleave(gens):
            gens = list(gens)
            while gens:
                for g in list(gens):
                    try:
                        next(g)
                    except StopIteration:
                        gens.remove(g)

        interleave([fox_sample_all()])
        p.fence()
        p.arena_off = arena_mark
        alloc_stage_b()

        def hgrn_chain():
            yield from hgrn_prompt_all()
            yield from hgrn_sample_all()
        interleave([fox_prompt_all(), hgrn_chain()])
        for _ in range(2):
            allgather(f"ag2_{l}", fl_src[:, :], fl_dst[:, :], reads=[], writes=["flush"])

    t_phase(None, 0, False, True)
    for l in range(L):
        p.fence(skip=(f"ag1_{l}",))
        m_phase(l)
        p.fence()
        if l + 1 < L:
            t_phase(l, l + 1, False, False)
        else:
            t_phase(l, None, True, False)
    p.emit()
    return nc, p


def make_idx(j, NPT):
    SEQ = 4 * NPT
    q1024, q512, s512 = NPT // 1024, NPT // 512, SEQ // 512
    idx = np.zeros((128, NLISTS), np.int32)
    d = np.arange(64)
    e = np.arange(128)
    for t in range(3):
        for u in range(2):
            for r in range(4):
                idx[0:64, LI_F(t, u, r)] = (((t * 4 + j) * 4 + r) * 128 + u * 64 + d) * q1024
    for u in range(2):
        for r in range(4):
            idx[0, LI_FL(u, r)] = (r * 8 + 2 * j + u) * q1024
            idx[1, LI_FL(u, r)] = (r * 8 + 2 * j + (1 - u)) * q1024
    for t in range(3):
        for r in range(4):
            idx[:, LI_H(t, r)] = (((t * 4 + j) * 4 + r) * 128 + e) * q512
    for c in range(4):
        idx[:, LI_P(0, c)] = (((e // 32) * 4 + c) * 32 + e % 32) * s512 + j * q512
        idx[:, LI_P(1, c)] = (((4 + e // 32) * 4 + c) * 32 + e % 32) * s512 + j * q512
    return idx


def run_fused(inp, SEQ, PAST, DFF, BATCH=2, DEC_BATCH=32, DEC_SEQ=16, D=1024, L=2):
    f32 = np.float32
    c_ = lambda a: np.ascontiguousarray(a, dtype=f32)
    NPT = SEQ // 4
    NSB = DEC_BATCH // 8
    H, HD, HH = 8, 64, 4
    nc, p = build_fused(NPT, NSB, DEC_SEQ, PAST, D, DFF, L)
    A = {k: np.asarray(v, f32) for k, v in inp.items()}
    shared = dict(norm_ffn1=c_(A["norm_ffn1"]), ffn1_wi=c_(A["ffn1_wi"]), ffn1_wo=c_(A["ffn1_wo"]), norm_mix=c_(A["norm_mix"]),
                  w_in=c_(A["w_in"]), b_fgate=c_(A["b_fgate"]), gnorm=c_(A["hgrn_gnorm"]), w_out=c_(A["w_out"]),
                  norm_ffn2=c_(A["norm_ffn2"]), ffn2_wi=c_(A["ffn2_wi"]), ffn2_wo=c_(A["ffn2_wo"]), norm_final=c_(A["norm_final"]))
    lbp = A["hgrn_lb"].reshape(L, HH, 128)
    maps = []
    for c in range(8):
        b, j = c // 4, c % 4
        bs = slice(NSB * c, NSB * (c + 1))
        xt = np.concatenate([A["x_prompt"][b, j * NPT:(j + 1) * NPT], A["x_sample"][bs].reshape(NSB * DEC_SEQ, D)], axis=0)
        lb3 = np.stack([np.stack([lbp[0], lbp[l], np.full((HH, 128), 1.0 if l > 0 else 0.0, f32)], axis=-1) for l in range(L)])
        m = dict(shared)
        m.update(xT=c_(xt.T),
                 cache_kT=c_(np.swapaxes(A["cache_k"][:, bs], -1, -2).reshape(L, NSB * H, HD, PAST)),
                 cache_v=c_(A["cache_v"][:, bs].reshape(L, NSB * H, PAST, HD)),
                 cache_logf=c_(A["cache_logf"][:, bs].reshape(L, NSB * H, PAST)),
                 state0=c_(A["state_hgrn"][:, bs].reshape(L, NSB * HH, 128, 128)),
                 hp_lb=c_(lb3[:, j]), hs_lb=c_(np.tile(lb3, (1, NSB, 1, 1))),
                 idx=make_idx(j, NPT))
        maps.append(m)
    res = run_bass_kernel_spmd(nc, maps, core_ids=list(range(8))).results
    NS = NSB * DEC_SEQ
    y_p = np.zeros((BATCH, SEQ, D), f32); y_s = np.zeros((DEC_BATCH, DEC_SEQ, D), f32)
    k_p = np.zeros((L, BATCH, H, SEQ, HD), f32); v_p = np.zeros_like(k_p)
    lf_p = np.zeros((L, BATCH, H, SEQ), f32)
    k_s = np.zeros((L, DEC_BATCH, H, DEC_SEQ, HD), f32); v_s = np.zeros_like(k_s)
    lf_s = np.zeros((L, DEC_BATCH, H, DEC_SEQ), f32)
    s_p = np.zeros((L, BATCH, HH, 128, 128), f32); s_s = np.zeros((L, DEC_BATCH, HH, 128, 128), f32)
    for c in range(8):
        b, j = c // 4, c % 4
        bs = slice(NSB * c, NSB * (c + 1)); ts = slice(j * NPT, (j + 1) * NPT)
        r = res[c]
        y = r["yT"].T
        y_p[b, ts] = y[:NPT]; y_s[bs] = y[NPT:].reshape(NSB, DEC_SEQ, D)
        kv = r["kvT"]
        k = kv[:, 0:512].reshape(L, H, HD, NPT + NS); v = kv[:, 512:1024].reshape(L, H, HD, NPT + NS)
        k_p[:, b, :, ts] = np.transpose(k[..., :NPT], (0, 1, 3, 2)); v_p[:, b, :, ts] = np.transpose(v[..., :NPT], (0, 1, 3, 2))
        k_s[:, bs] = np.transpose(k[..., NPT:].reshape(L, H, HD, NSB, DEC_SEQ), (0, 3, 1, 4, 2))
        v_s[:, bs] = np.transpose(v[..., NPT:].reshape(L, H, HD, NSB, DEC_SEQ), (0, 3, 1, 4, 2))
        lf = r["logfT"]
        lf_p[:, b, :, ts] = lf[..., :NPT]
        lf_s[:, bs] = np.transpose(lf[..., NPT:].reshape(L, H, NSB, DEC_SEQ), (0, 2, 1, 3))
        s_p[:, b, j] = r["hp_S"]
        s_s[:, bs] = r["hs_S"].reshape(L, NSB, HH, 128, 128)
    return (y_p, y_s, k_p, v_p, lf_p, s_p, k_s, v_s, lf_s, s_s)


def kernel(x_prompt, x_sample, cache_k, cache_v, cache_logf, state_hgrn,
           norm_ffn1, ffn1_wi, ffn1_wo, norm_mix, w_in, b_fgate, hgrn_lb, hgrn_gnorm,
           w_out, norm_ffn2, ffn2_wi, ffn2_wo, norm_final):
    inp = dict(x_prompt=x_prompt, x_sample=x_sample, cache_k=cache_k, cache_v=cache_v, cache_logf=cache_logf,
               state_hgrn=state_hgrn, norm_ffn1=norm_ffn1, ffn1_wi=ffn1_wi, ffn1_wo=ffn1_wo, norm_mix=norm_mix,
               w_in=w_in, b_fgate=b_fgate, hgrn_lb=hgrn_lb, hgrn_gnorm=hgrn_gnorm, w_out=w_out,
               norm_ffn2=norm_ffn2, ffn2_wi=ffn2_wi, ffn2_wo=ffn2_wo, norm_final=norm_final)
    return run_fused(inp, SEQ=8192, PAST=2048, DFF=2816)
```

```python
from concourse.bass_utils import run_bass_kernel_spmd
import numpy as np
import concourse.bass as bass
import concourse.mybir as mybir

F32 = mybir.dt.float32
BF16 = mybir.dt.bfloat16
I32 = mybir.dt.int32
AF = mybir.ActivationFunctionType
ALU = mybir.AluOpType
AX = mybir.AxisListType

ENGS = ["pe", "act", "dve", "pool", "sp"]
SEM_ROLL = 30000


class Prog:
    def __init__(self, nc):
        self.nc = nc
        self.q = {e: [] for e in ENGS}
        self.buf = {}
        self.dma_cnt = {}
        self.n_tensors = 0

    arena_base = None
    arena_off = 0
    arena_peak = 0

    def arena_begin(self):
        if self.arena_base is None:
            nc = self.nc
            self.arena_base = (nc.SBUF_PARTITION_SIZE_BYTES - nc.sbuf_bytes_remaining + 63) // 64 * 64
            self.arena_limit = nc.SBUF_PARTITION_SIZE_BYTES
        self.arena_off = self.arena_base

    def sb(self, shape, dtype, name=None):
        self.n_tensors += 1
        nm = "sb_" + (name or "t") + f"_{self.n_tensors}"
        if self.arena_base is None:
            return self.nc.alloc_sbuf_tensor(nm, list(shape), dtype)
        esz = 4 if dtype in (F32, I32) else 2
        nbytes = esz
        for d in shape[1:]:
            nbytes *= d
        nbytes = (nbytes + 63) // 64 * 64
        off = self.arena_off
        assert off + nbytes <= self.arena_limit, f"SBUF arena overflow: {off + nbytes} > {self.arena_limit} ({nm})"
        self.arena_off = off + nbytes
        self.arena_peak = max(self.arena_peak, self.arena_off)
        return self.nc.alloc_sbuf_tensor_at(nm, list(shape), dtype, offset=off)

    def ps(self, shape, dtype=F32, name=None):
        self.n_tensors += 1
        return self.nc.alloc_psum_tensor("ps_" + (name or "t") + f"_{self.n_tensors}", list(shape), dtype)

    def _add(self, eng, fn, reads, writes, dma_key=None, inc=16):
        idx = len(self.q[eng])
        deps = set()
        for b in reads:
            st = self.buf.get(b)
            if st is not None and st[0] is not None:
                deps.add(st[0])
        for b in writes:
            st = self.buf.get(b)
            if st is not None:
                if st[0] is not None:
                    deps.add(st[0])
                deps.update(st[1])
        me = (eng, idx)
        deps.discard(me)
        ins = {"fn": fn, "deps": deps, "dma_key": dma_key, "sig": False, "dma_val": None}
        if dma_key is not None:
            c = self.dma_cnt.get(dma_key, 0) + inc
            self.dma_cnt[dma_key] = c
            ins["dma_val"] = c
            ins["inc"] = inc
        self.q[eng].append(ins)
        for b in reads:
            st = self.buf.get(b)
            if st is None:
                self.buf[b] = [None, [me]]
            else:
                st[1].append(me)
        for b in writes:
            self.buf[b] = [me, []]
        return me

    def op(self, eng, fn, reads=(), writes=()):
        return self._add(eng, fn, tuple(reads), tuple(writes))

    def dma(self, eng, key, out, in_, reads=(), writes=(), **kw):
        return self._add(eng, lambda e: e.dma_start(out=out, in_=in_, **kw), tuple(reads), tuple(writes), dma_key=key)

    def custom_dma(self, eng, key, fn, reads=(), writes=(), inc=16):
        me = self._add(eng, fn, tuple(reads), tuple(writes), dma_key=key, inc=inc)
        return me

    def wait_keys(self, engs, keys):
        snap = {k: self.dma_cnt[k] for k in keys if k in self.dma_cnt}
        for e in engs:
            self.q[e].append({"fn": None, "deps": set(), "dma_key": None, "sig": False, "dma_val": None, "fence": snap})

    def fence(self, skip=()):
        last = set()
        for e in ENGS:
            k = len(self.q[e]) - 1
            while k >= 0 and self.q[e][k]["dma_key"] in skip and self.q[e][k]["dma_key"] is not None:
                k -= 1
            if k >= 0:
                last.add((e, k))
        snap = {k: v for k, v in self.dma_cnt.items() if k not in skip}
        for e in ENGS:
            self.q[e].append({"fn": None, "deps": set(x for x in last if x[0] != e), "dma_key": None, "sig": False,
                              "dma_val": None, "fence": snap})
        self.buf = {}

    def emit(self, final_wait_bufs=()):
        nc = self.nc
        fin = set()
        for b in final_wait_bufs:
            st = self.buf.get(b)
            if st is not None and st[0] is not None:
                fin.add(st[0])
        self.q["sp"].append({"fn": None, "deps": fin, "dma_key": None, "sig": False, "dma_val": None, "final": True})
        for e in ENGS:
            for ins in self.q[e]:
                nd = set()
                for (de, di) in ins["deps"]:
                    d = self.q[de][di]
                    if d["dma_key"] is None:
                        if de == "pe" and e == "pe":
                            continue
                        if d["fn"] is None:
                            k2 = di
                            while k2 >= 0 and (self.q[de][k2]["fn"] is None or self.q[de][k2]["dma_key"] is not None):
                                k2 -= 1
                            if k2 < 0:
                                continue
                            di = k2
                            d = self.q[de][di]
                        d["sig"] = True
                    nd.add((de, di))
                ins["deps"] = nd
        sems = {}
        for e in ENGS:
            cur = nc.alloc_semaphore(f"s_{e}_0")
            n = 0
            k = 0
            for ins in self.q[e]:
                if ins["sig"] and ins["dma_key"] is None:
                    if n >= SEM_ROLL:
                        k += 1
                        cur = nc.alloc_semaphore(f"s_{e}_{k}")
                        n = 0
                    n += 1
                    ins["sem"] = (cur, n)
        dsem = {}
        for key in self.dma_cnt:
            dsem[key] = nc.alloc_semaphore(f"d_{len(dsem)}")
        self.stats = {e: len(self.q[e]) for e in ENGS}

        def run(eng_name, eng):
            known = {}
            nwait = 0
            for ins in self.q[eng_name]:
                need = {}
                for (de, di) in ins["deps"]:
                    d = self.q[de][di]
                    if d["dma_key"] is not None:
                        s, v = dsem[d["dma_key"]], d["dma_val"]
                    else:
                        s, v = d["sem"]
                    kk = id(s)
                    if known.get(kk, 0) >= v:
                        continue
                    if kk not in need or need[kk][1] < v:
                        need[kk] = (s, v)
                for kk, (s, v) in need.items():
                    eng.wait_ge(s, v)
                    known[kk] = v
                    nwait += 1
                if ins["fn"] is None:
                    if ins.get("final"):
                        for key, cnt in self.dma_cnt.items():
                            eng.wait_ge(dsem[key], cnt)
                    if ins.get("fence") is not None:
                        for key, cnt in ins["fence"].items():
                            kk = id(dsem[key])
                            if known.get(kk, 0) < cnt:
                                eng.wait_ge(dsem[key], cnt)
                                known[kk] = cnt
                    continue
                r = ins["fn"](eng)
                if ins["dma_key"] is not None:
                    r.then_inc(dsem[ins["dma_key"]], ins.get("inc", 16))
                elif ins["sig"]:
                    r.then_inc(ins["sem"][0], 1)
            self.stats[eng_name + "_waits"] = nwait

        with nc.Block() as block:
            @block.tensor
            def _(e):
                run("pe", e)

            @block.scalar
            def _(e):
                run("act", e)

            @block.vector
            def _(e):
                run("dve", e)

            @block.gpsimd
            def _(e):
                run("pool", e)

            @block.sync
            def _(e):
                run("sp", e)


EPS = 1e-6
NEG = -30000.0
GROUPS4 = [[0, 1, 2, 3], [4, 5, 6, 7]]


class Ring:
    def __init__(self, tiles, name):
        self.tiles = tiles
        self.name = name
        self.i = 0

    def next(self):
        k = self.i % len(self.tiles)
        self.i += 1
        return self.tiles[k], f"{self.name}{k}"


def LI_F(t, u, r):
    return (t * 2 + u) * 4 + r


def LI_FL(u, r):
    return 24 + u * 4 + r


def LI_H(t, r):
    return 32 + t * 4 + r


def LI_P(t, c):
    return 44 + t * 4 + c


NLISTS = 52
FSEG = 1024
HSEG = 512
TW = 512


def build_fused(NPT, NSB, TQS, PAST, D, DFF, L=2):
    SEQ = 4 * NPT
    NS = NSB * TQS
    NTOK = NPT + NS
    TKS = PAST + TQS
    FOXW, HGW, NFG, HD = 512, 512, 8, 64
    DIN = 3 * FOXW + NFG + 4 * HGW
    KC = D // 128
    FC = DFF // 128
    HG0 = 3 * FOXW + NFG + 3 * HGW
    H0 = 3 * FOXW + NFG
    NFS = NSB * 8
    NHS = NSB * 4
    CH = 64
    assert NPT % TW == 0 and NPT % FSEG == 0 and NS <= TW

    nc = bass.Bass("TRN2", target_bir_lowering=False)
    p = Prog(nc)

    def din(name, shape, dt=F32):
        return nc.dram_tensor(name, list(shape), dt, kind="ExternalInput").ap()

    def dout(name, shape, dt=F32):
        return nc.dram_tensor(name, list(shape), dt, kind="ExternalOutput").ap()

    def dint(name, shape, dt=F32):
        return nc.dram_tensor(name, list(shape), dt).ap()

    xT_d = din("xT", [D, NTOK])
    Wd = dict(
        norm_ffn1=din("norm_ffn1", [L, D]), ffn1_wi=din("ffn1_wi", [L, D, 2 * DFF]), ffn1_wo=din("ffn1_wo", [L, DFF, D]),
        norm_mix=din("norm_mix", [L, D]), w_in=din("w_in", [L, D, DIN]), b_fgate=din("b_fgate", [L, NFG]),
        gnorm=din("gnorm", [L, 128]), w_out=din("w_out", [L, D, D]),
        norm_ffn2=din("norm_ffn2", [L, D]), ffn2_wi=din("ffn2_wi", [L, D, 2 * DFF]), ffn2_wo=din("ffn2_wo", [L, DFF, D]),
        norm_final=din("norm_final", [D]))
    ckT_d = din("cache_kT", [L, NFS, HD, PAST])
    cv_d = din("cache_v", [L, NFS, PAST, HD])
    clf_d = din("cache_logf", [L, NFS, PAST])
    st0_d = din("state0", [L, NHS, 128, 128])
    hplb_d = din("hp_lb", [L, 128, 3])
    hslb_d = din("hs_lb", [L, NHS, 128, 3])
    idx_d = din("idx", [128, NLISTS], I32)
    yT_d = dout("yT", [D, NTOK])
    kv_d = dout("kvT", [L, 2 * FOXW, NTOK])
    lfo_d = dout("logfT", [L, NFG, NTOK])
    hpS_d = dout("hp_S", [L, 128, 128])
    hsS_d = dout("hs_S", [L, NHS, 128, 128])
    projP = [dint(f"projP{l}", [DIN, NPT]) for l in range(L)]
    projS = [dint(f"projS{l}", [DIN, NS]) for l in range(L)]
    lfP = [dint(f"lfP{l}", [NFG, NPT]) for l in range(L)]
    lfS = [dint(f"lfS{l}", [NFG, NS]) for l in range(L)]
    G1a = [dint(f"G1a{l}", [12, 4 * 128, NPT]) for l in range(L)]
    G1h = [dint(f"G1h{l}", [12, 4 * 128, NPT]) for l in range(L)]
    G1l = [dint(f"G1l{l}", [4 * NFG, NPT]) for l in range(L)]
    S2 = [dint(f"S2_{l}", [8, 32, SEQ]) for l in range(L)]
    G2 = [dint(f"G2_{l}", [8, 4 * 32, SEQ]) for l in range(L)]
    foxS = [dint(f"foxS{l}", [FOXW, NS]) for l in range(L)]
    fl_src = dint("flush_src", [8, 64])
    fl_dst = dint("flush_dst", [4 * 8, 64])
    hoS = [dint(f"hoS{l}", [HGW, NS]) for l in range(L)]

    xT = p.sb([128, KC, NTOK], F32, "xT")
    ones_bf = p.sb([128, 128], BF16, "ones_bf")
    ident = p.sb([128, 128], BF16, "ident")
    maskrow = p.sb([128, 512], BF16, "maskrow")
    mask01 = p.sb([128, 128], F32, "mask01")
    zcol = p.sb([128, 1], F32, "zcol")
    idx_sb = p.sb([128, NLISTS], I32, "idx")
    banks = [p.ps([128, 512], F32, f"bank{i}") for i in range(7)]
    pbf = p.ps([128, 1024], BF16, "pbf")

    def consts():
        p.op("pool", lambda e: e.memset(ones_bf[:], 1.0), writes=["ones_bf"])
        p.op("pool", lambda e: e.memset(ident[:], 0.0), writes=["ident"])
        p.op("pool", lambda e: e.affine_select(out=ident[:], in_=ident[:], pattern=[[-1, 128]], compare_op=ALU.not_equal, fill=1.0, base=0, channel_multiplier=1), reads=["ident"], writes=["ident"])
        p.op("pool", lambda e: e.memset(maskrow[:], 0.0), writes=["maskrow"])
        p.op("pool", lambda e: e.affine_select(out=maskrow[:, 0:128], in_=maskrow[:, 0:128], pattern=[[1, 128]], compare_op=ALU.is_ge, fill=NEG, base=0, channel_multiplier=-1), reads=["maskrow"], writes=["maskrow"])
        p.op("pool", lambda e: e.memset(mask01[:], 1.0), writes=["mask01"])
        p.op("pool", lambda e: e.affine_select(out=mask01[:], in_=mask01[:], pattern=[[1, 128]], compare_op=ALU.is_ge, fill=0.0, base=0, channel_multiplier=-1), reads=["mask01"], writes=["mask01"])
        p.op("pool", lambda e: e.memset(zcol[:], 0.0), writes=["zcol"])
        p.dma("sp", "idx", idx_sb[:], idx_d[:, :], writes=["idx"])

    tiles = [(t0, TW) for t0 in range(0, NPT, TW)] + [(NPT, NS)]
    NPTI = NPT // TW

    def gather(key, dst, src2d, n, li, npart, eoff, reads, writes):
        view = src2d.rearrange("r (a n) -> (r a) n", n=n)
        p.custom_dma("pool", key, lambda e: e.indirect_dma_start(out=dst, out_offset=None, in_=view[:, :], in_offset=bass.IndirectOffsetOnAxis(ap=idx_sb[0:npart, li:li + 1], axis=0), element_offset=eoff), reads=list(reads) + ["idx"], writes=writes)

    def allgather(key, src, dst, reads, writes):
        p.custom_dma("pool", key, lambda e: e.collective_compute("AllGather", ALU.bypass, replica_groups=GROUPS4, ins=[src.opt()], outs=[dst.opt()]), reads=reads, writes=writes, inc=1)

    def t_phase(l_post, l_ffn1, do_final, first):
        p.arena_begin()
        hT = p.sb([128, KC, NTOK], BF16, "hT")
        GMAX = 6
        gT = p.sb([128, GMAX, NTOK], BF16, "gT")
        psr = Ring(banks, "bank")
        wst = Ring([p.sb([128, KC, 256], F32, f"wst{i}") for i in range(2)], "wst")
        wbf = Ring([p.sb([128, KC, 256], BF16, f"wbf{i}") for i in range(2)], "wbf")
        wost = Ring([p.sb([128, GMAX, 128], F32, f"wost{i}") for i in range(2)], "wost")
        wobf = Ring([p.sb([128, GMAX, 128], BF16, f"wobf{i}") for i in range(2)], "wobf")
        sq = p.sb([128, KC, 512], BF16, "sq")
        tmp = Ring([p.sb([128, 512], F32, f"tmp{i}") for i in range(12)], "tmp")
        finr = Ring([p.sb([128, 512], F32, f"fin{i}") for i in range(3)], "fin") if do_final else None
        if first:
            consts()
            for ti, (t0, n) in enumerate(tiles):
                p.dma("sp", f"xin{ti}", xT[:, :, t0:t0 + n], xT_d[:, t0:t0 + n].rearrange("(c p) n -> p c n", p=128), writes=[f"xT{ti}"])

        def load_gain(g_d, key):
            g = p.sb([128, KC], F32, key)
            p.dma("sp", key, g[:], g_d.rearrange("(c p) -> p c", p=128), writes=[key], allow_slow_non_contiguous=True)
            return g

        def rmsnorm_to_hT(g, gkey, out_fp32_dram=None):
            for ti, (t0, n) in enumerate(tiles):
                p.op("act", lambda e, t0=t0, n=n: e.activation(out=sq[:, :, 0:n], in_=xT[:, :, t0:t0 + n], func=AF.Square), reads=[f"xT{ti}"], writes=["sq"])
                ps, pk = psr.next()
                for c in range(KC):
                    p.op("pe", lambda e, c=c, n=n, ps=ps: e.matmul(ps[:, 0:n], lhsT=ones_bf[:], rhs=sq[:, c, 0:n], start=(c == 0), stop=(c == KC - 1)), reads=["sq", "ones_bf"], writes=[pk])
                sd, sk = tmp.next()
                p.op("act", lambda e, n=n, ps=ps, sd=sd: e.activation(out=sd[:, 0:n], in_=ps[:, 0:n], func=AF.Sqrt, scale=1.0 / D, bias=EPS), reads=[pk], writes=[sk])
                p.op("dve", lambda e, n=n, sd=sd: e.reciprocal(out=sd[:, 0:n], in_=sd[:, 0:n]), reads=[sk], writes=[sk])
                for c in range(KC):
                    if out_fp32_dram is None:
                        p.op("dve", lambda e, c=c, t0=t0, n=n, sd=sd: e.scalar_tensor_tensor(out=hT[:, c, t0:t0 + n], in0=xT[:, c, t0:t0 + n], scalar=g[:, c:c + 1], in1=sd[:, 0:n], op0=ALU.mult, op1=ALU.mult), reads=[f"xT{ti}", sk, gkey], writes=[f"hT{ti}"])
                    else:
                        o, ok = finr.next()
                        p.op("dve", lambda e, c=c, t0=t0, n=n, sd=sd, o=o: e.scalar_tensor_tensor(out=o[:, 0:n], in0=xT[:, c, t0:t0 + n], scalar=g[:, c:c + 1], in1=sd[:, 0:n], op0=ALU.mult, op1=ALU.mult), reads=[f"xT{ti}", sk, gkey], writes=[ok])
                        p.dma("sp", "o_" + ok, out_fp32_dram[c * 128:(c + 1) * 128, t0:t0 + n], o[:, 0:n], reads=[ok], writes=[])

        def load_w_block(w_d, c0, ncols):
            st, stk = wst.next()
            p.dma("sp", stk, st[:, :, 0:ncols], w_d[:, c0:c0 + ncols].rearrange("(c p) n -> p c n", p=128), writes=[stk])
            wb, wbk = wbf.next()
            p.op("act", lambda e: e.activation(out=wb[:, :, 0:ncols], in_=st[:, :, 0:ncols], func=AF.Copy), reads=[stk], writes=[wbk])
            return wb, wbk

        def ffn(pref, l):
            g = load_gain(Wd["norm_" + pref][l], f"g_{pref}{l}")
            wi_d = Wd[pref + "_wi"][l]
            wo_d = Wd[pref + "_wo"][l]
            rmsnorm_to_hT(g, f"g_{pref}{l}")
            f = 0
            while f < FC:
                gsz = min(GMAX, FC - f)
                f0 = f
                fl = 0
                while fl < gsz:
                    nb = min(2, gsz - fl)
                    wa, wak = load_w_block(wi_d, (f0 + fl) * 128, nb * 128)
                    wb_, wbk_ = load_w_block(wi_d, DFF + (f0 + fl) * 128, nb * 128)
                    for j in range(nb):
                        for ti, (t0, n) in enumerate(tiles):
                            pa, pak = psr.next()
                            pb, pbk = psr.next()
                            for c in range(KC):
                                p.op("pe", lambda e, c=c, j=j, t0=t0, n=n, pa=pa, wa=wa: e.matmul(pa[:, 0:n], lhsT=wa[:, c, j * 128:(j + 1) * 128], rhs=hT[:, c, t0:t0 + n], start=(c == 0), stop=(c == KC - 1)), reads=[wak, f"hT{ti}"], writes=[pak])
                            for c in range(KC):
                                p.op("pe", lambda e, c=c, j=j, t0=t0, n=n, pb=pb, wb_=wb_: e.matmul(pb[:, 0:n], lhsT=wb_[:, c, j * 128:(j + 1) * 128], rhs=hT[:, c, t0:t0 + n], start=(c == 0), stop=(c == KC - 1)), reads=[wbk_, f"hT{ti}"], writes=[pbk])
                            sa, sak = tmp.next()
                            p.op("act", lambda e, n=n, pa=pa, sa=sa: e.activation(out=sa[:, 0:n], in_=pa[:, 0:n], func=AF.Silu), reads=[pak], writes=[sak])
                            p.op("dve", lambda e, n=n, t0=t0, pb=pb, sa=sa, fi=fl + j: e.tensor_tensor(out=gT[:, fi, t0:t0 + n], in0=sa[:, 0:n], in1=pb[:, 0:n], op=ALU.mult), reads=[sak, pbk], writes=[f"gT{ti}"])
                    fl += nb
                for oc in range(KC):
                    st, stk = wost.next()
                    p.dma("sp", stk, st[:, 0:gsz, :], wo_d[f0 * 128:(f0 + gsz) * 128, oc * 128:(oc + 1) * 128].rearrange("(c p) n -> p c n", p=128), writes=[stk])
                    wo, wok = wobf.next()
                    p.op("dve", lambda e, st=st, wo=wo, gsz=gsz: e.tensor_copy(out=wo[:, 0:gsz, :], in_=st[:, 0:gsz, :]), reads=[stk], writes=[wok])
                    for ti, (t0, n) in enumerate(tiles):
                        ps, pk = psr.next()
                        for fi in range(gsz):
                            p.op("pe", lambda e, fi=fi, t0=t0, n=n, ps=ps, wo=wo: e.matmul(ps[:, 0:n], lhsT=wo[:, fi, :], rhs=gT[:, fi, t0:t0 + n], start=(fi == 0), stop=(fi == gsz - 1)), reads=[wok, f"gT{ti}"], writes=[pk])
                        p.op("dve", lambda e, oc=oc, t0=t0, n=n, ps=ps: e.scalar_tensor_tensor(out=xT[:, oc, t0:t0 + n], in0=ps[:, 0:n], scalar=0.5, in1=xT[:, oc, t0:t0 + n], op0=ALU.mult, op1=ALU.add), reads=[pk, f"xT{ti}"], writes=[f"xT{ti}"])
                f += gsz

        if l_post is not None:
            l = l_post
            NH = HGW // 128
            FXC = FOXW // 128
            gn = p.sb([128, 1], F32, "gn")
            p.dma("sp", "gn", gn[:], Wd["gnorm"][l].rearrange("(p o) -> p o", o=1), writes=["gn"])
            g2flat = G2[l].rearrange("b r s -> (b r) s")
            for ti, (t0, n) in enumerate(tiles):
                prompt = ti < NPTI
                for c in range(FXC):
                    a, ak = tmp.next()
                    if prompt:
                        gather(ak, a[:, 0:n], g2flat, TW, LI_P(0, c), 128, t0, reads=["G2"], writes=[ak])
                    else:
                        p.dma("sp", ak, a[:, 0:n], foxS[l][c * 128:(c + 1) * 128, :], reads=["foxS"], writes=[ak])
                    p.op("pool", lambda e, a=a, c=c, t0=t0, n=n: e.tensor_copy(out=hT[:, c, t0:t0 + n], in_=a[:, 0:n]), reads=[ak], writes=[f"hT{ti}"])
                for hd in range(NH):
                    a, ak = tmp.next()
                    b, bk = tmp.next()
                    if prompt:
                        gather(ak, a[:, 0:n], g2flat, TW, LI_P(1, hd), 128, t0, reads=["G2"], writes=[ak])
                        p.dma("sp", bk, b[:, 0:n], projP[l][HG0 + hd * 128:HG0 + (hd + 1) * 128, t0:t0 + n], reads=["projP"], writes=[bk])
                    else:
                        p.dma("sp", ak, a[:, 0:n], hoS[l][hd * 128:(hd + 1) * 128, :], reads=["hoS"], writes=[ak])
                        p.dma("sp", bk, b[:, 0:n], projS[l][HG0 + hd * 128:HG0 + (hd + 1) * 128, :], reads=["projS"], writes=[bk])
                    p.op("act", lambda e, a=a, n=n: e.activation(out=sq[:, 0, 0:n], in_=a[:, 0:n], func=AF.Square), reads=[ak], writes=["sq"])
                    ps, pk = psr.next()
                    p.op("pe", lambda e, n=n, ps=ps: e.matmul(ps[:, 0:n], lhsT=ones_bf[:], rhs=sq[:, 0, 0:n], start=True, stop=True), reads=["sq", "ones_bf"], writes=[pk])
                    sd, sk = tmp.next()
                    p.op("act", lambda e, n=n, ps=ps, sd=sd: e.activation(out=sd[:, 0:n], in_=ps[:, 0:n], func=AF.Sqrt, scale=1.0 / 128, bias=EPS), reads=[pk], writes=[sk])
                    p.op("dve", lambda e, n=n, sd=sd: e.reciprocal(out=sd[:, 0:n], in_=sd[:, 0:n]), reads=[sk], writes=[sk])
                    p.op("act", lambda e, b=b, n=n: e.activation(out=b[:, 0:n], in_=b[:, 0:n], func=AF.Silu), reads=[bk], writes=[bk])
                    p.op("dve", lambda e, a=a, sd=sd, n=n: e.scalar_tensor_tensor(out=a[:, 0:n], in0=a[:, 0:n], scalar=gn[:, 0:1], in1=sd[:, 0:n], op0=ALU.mult, op1=ALU.mult), reads=[ak, sk, "gn"], writes=[ak])
                    p.op("dve", lambda e, a=a, b=b, hd=hd, t0=t0, n=n: e.tensor_tensor(out=hT[:, FXC + hd, t0:t0 + n], in0=a[:, 0:n], in1=b[:, 0:n], op=ALU.mult), reads=[ak, bk], writes=[f"hT{ti}"])
            for oc in range(0, KC, 2):
                w, wk = load_w_block(Wd["w_out"][l], oc * 128, 256)
                for j in range(2):
                    for ti, (t0, n) in enumerate(tiles):
                        ps, pk = psr.next()
                        for c in range(KC):
                            p.op("pe", lambda e, c=c, j=j, t0=t0, n=n, ps=ps, w=w: e.matmul(ps[:, 0:n], lhsT=w[:, c, j * 128:(j + 1) * 128], rhs=hT[:, c, t0:t0 + n], start=(c == 0), stop=(c == KC - 1)), reads=[wk, f"hT{ti}"], writes=[pk])
                        p.op("dve", lambda e, oc=oc + j, t0=t0, n=n, ps=ps: e.tensor_tensor(out=xT[:, oc, t0:t0 + n], in0=ps[:, 0:n], in1=xT[:, oc, t0:t0 + n], op=ALU.add), reads=[pk, f"xT{ti}"], writes=[f"xT{ti}"])
            ffn("ffn2", l)

        if l_ffn1 is not None:
            l = l_ffn1
            ffn("ffn1", l)
            g = load_gain(Wd["norm_mix"][l], f"g_mix{l}")
            win_d = Wd["w_in"][l]
            negb = p.sb([NFG, 1], F32, "negb")
            p.dma("sp", "negb", negb[:], Wd["b_fgate"][l].rearrange("(p o) -> p o", o=1), writes=["negb"])
            p.op("dve", lambda e: e.tensor_scalar(out=negb[:], in0=negb[:], scalar1=-1.0, scalar2=None, op0=ALU.mult), reads=["negb"], writes=["negb"])
            rmsnorm_to_hT(g, f"g_mix{l}")
            FG0 = 3 * FOXW
            blocks = []
            c0 = 0
            while c0 < FG0:
                blocks.append((c0, min(256, FG0 - c0)))
                c0 += 256
            blocks.append((FG0, NFG))
            c0 = FG0 + NFG
            while c0 < DIN:
                blocks.append((c0, min(256, DIN - c0)))
                c0 += 256
            for (c0, ncols) in blocks:
                w, wk = load_w_block(win_d, c0, ncols)
                j0 = 0
                while j0 < ncols:
                    m = min(128, ncols - j0)
                    r0 = c0 + j0
                    rkey = f"projP_r{r0}"
                    for ti, (t0, n) in enumerate(tiles):
                        prompt = ti < NPTI
                        ps, pk = psr.next()
                        for c in range(KC):
                            p.op("pe", lambda e, c=c, j0=j0, m=m, t0=t0, n=n, ps=ps, w=w: e.matmul(ps[0:m, 0:n], lhsT=w[:, c, j0:j0 + m], rhs=hT[:, c, t0:t0 + n], start=(c == 0), stop=(c == KC - 1)), reads=[wk, f"hT{ti}"], writes=[pk])
                        o, ok = tmp.next()
                        p.op("act", lambda e, m=m, n=n, ps=ps, o=o: e.activation(out=o[0:m, 0:n], in_=ps[0:m, 0:n], func=AF.Copy), reads=[pk], writes=[ok])
                        if prompt:
                            p.dma("act", "o_" + ok, projP[l][r0:r0 + m, t0:t0 + n], o[0:m, 0:n], reads=[ok], writes=[rkey + f"_{ti}"])
                        else:
                            p.dma("act", "o_" + ok, projS[l][r0:r0 + m, :], o[0:m, 0:n], reads=[ok], writes=[])
                        if FOXW <= r0 < 3 * FOXW:
                            p.dma("act", "o_" + ok, kv_d[l][r0 - FOXW:r0 - FOXW + m, t0:t0 + n], o[0:m, 0:n], reads=[ok], writes=[])
                        if c0 == FG0:
                            lt, lk = tmp.next()
                            p.op("act", lambda e, m=m, n=n, o=o, lt=lt: e.activation(out=lt[0:m, 0:n], in_=o[0:m, 0:n], func=AF.Exp, scale=-1.0, bias=negb[:, 0:1]), reads=[ok, "negb"], writes=[lk])
                            p.op("act", lambda e, m=m, n=n, lt=lt: e.activation(out=lt[0:m, 0:n], in_=lt[0:m, 0:n], func=AF.Ln, bias=1.0), reads=[lk], writes=[lk])
                            p.op("dve", lambda e, m=m, n=n, lt=lt: e.tensor_scalar(out=lt[0:m, 0:n], in0=lt[0:m, 0:n], scalar1=-1.0, scalar2=None, op0=ALU.mult), reads=[lk], writes=[lk])
                            p.dma("act", "o_" + lk, lfo_d[l][:, t0:t0 + n], lt[0:m, 0:n], reads=[lk], writes=[])
                            if prompt:
                                p.dma("act", "o_" + lk, lfP[l][:, t0:t0 + n], lt[0:m, 0:n], reads=[lk], writes=[f"lfP_{ti}"])
                            else:
                                p.dma("act", "o_" + lk, lfS[l][:, :], lt[0:m, 0:n], reads=[lk], writes=[])
                    prk = [rkey + f"_{ti}" for ti in range(NPTI)]
                    if r0 < 3 * FOXW:
                        b = r0 // 128
                        allgather(f"ag1_{l}", projP[l][r0:r0 + 128, :], G1a[l][b], reads=prk, writes=[f"G1a_{b}"])
                    elif c0 == FG0:
                        allgather(f"ag1_{l}", lfP[l][:, :], G1l[l][:, :], reads=[f"lfP_{ti}" for ti in range(NPTI)], writes=["G1l"])
                    elif r0 < HG0:
                        b = (r0 - H0) // 128
                        allgather(f"ag1_{l}", projP[l][r0:r0 + 128, :], G1h[l][b], reads=prk, writes=[f"G1h_{b}"])
                    j0 += m
            for _ in range(2):
                allgather(f"ag1_{l}", fl_src[:, :], fl_dst[:, :], reads=[], writes=["flush"])

        if do_final:
            g = load_gain(Wd["norm_final"], "g_fin")
            rmsnorm_to_hT(g, "g_fin", out_fp32_dram=yT_d)

    def m_phase(l):
        p.arena_begin()
        g1a = G1a[l].rearrange("b r n -> (b r) n")
        g1h = G1h[l].rearrange("b r n -> (b r) n")
        g1l = G1l[l]
        pS = Ring(banks[0:2], "bank")
        pO = Ring(banks[2:4], "bankO")
        pH = Ring(banks[4:7], "bankH")
        pO.name = "bankO"
        stg = Ring([p.sb([128, FSEG], F32, f"stg{i}") for i in range(3)], "stg")
        cbr = Ring([p.sb([128, FSEG], BF16, f"cb{i}") for i in range(3)], "cb")
        hOne = p.sb([128, FSEG], F32, "hOne")
        p.op("dve", lambda e: e.memset(hOne[:], 1.0), writes=["hOne"])
        carry = p.sb([128, 1], F32, "carry")
        ptr = Ring([p.sb([128, 512], BF16, f"pt{i}") for i in range(3)], "pt")
        fo = Ring([p.sb([64, 512], F32, f"fo{i}") for i in range(2)], "fo")
        frl = Ring([p.sb([64, 512], F32, f"frl{i}") for i in range(1)], "frl")

        def mkrings(tag, nb, TKx, TQx):
            NK = (TKx + 127) // 128
            kr = Ring([p.sb([70, TKx], BF16, f"{tag}kaug{i}") for i in range(nb)], f"{tag}kaug")
            qr = Ring([p.sb([70, TQx], BF16, f"{tag}qaug{i}") for i in range(nb)], f"{tag}qaug")
            vr = Ring([p.sb([128, NK, 128], BF16, f"{tag}vaug{i}") for i in range(nb)], f"{tag}vaug")
            for i in range(nb):
                t = kr.tiles[i]
                p.op("dve", lambda e, t=t: e.memset(t[64:70, :], 1.0), writes=[f"{tag}kaug{i}c"])
                t = qr.tiles[i]
                p.op("dve", lambda e, t=t: e.memset(t[64:70, :], -1.0), writes=[f"{tag}qaug{i}c"])
                t = vr.tiles[i]
                p.op("dve", lambda e, t=t: e.memset(t[:, :, 64:128], 1.0), writes=[f"{tag}vaug{i}o"])
            return kr, qr, vr

        arena_mark = p.arena_off
        rings_s = mkrings("S", 2, TKS, TQS)
        stgA = Ring([p.sb([128, PAST], F32, f"stgA{i}") for i in range(4)], "stgA")
        qkS = p.sb([64, 3, 8, NS], F32, "qkS")
        vtb = p.sb([64, TQS], BF16, "vtb")
        lf = p.sb([NFS, TKS], F32, "fs_lf")
        cs = p.sb([NFS, 3, TKS], BF16, "fs_cs")

        def fox_core(ka, kak, qa_, qak, va, vak, TQ, TK, out_cb):
            P = TK - TQ
            W = min(512, TQ)
            for qt in range(TQ // W):
                q0 = qt * W
                po, pok = pO.next()
                last_kt = (P + q0 + W - 1) // 128
                pairs = []
                for kt in range(last_kt + 1):
                    nk = min(128, TK - kt * 128)
                    qa = max(0, kt * 128 - P - q0)
                    if qa >= W:
                        continue
                    pairs.append((kt, nk, qa, (P + q0 + qa) < (kt * 128 + nk - 1)))

                def mm2(kt, nk, qa, pt, ptk, first, last, po=po, pok=pok):
                    p.op("pe", lambda e: e.matmul(po[:, qa:W], lhsT=va[0:nk, kt, :], rhs=pt[0:nk, qa:W], start=first, stop=last), reads=[vak, vak + "o", ptk], writes=[pok])
                pend = None
                for i, (kt, nk, qa, diag) in enumerate(pairs):
                    ps, psk = pS.next()
                    p.op("pe", lambda e, ps=ps, kt=kt, nk=nk, qa=qa, diag=diag, q0=q0: e.matmul(ps[0:nk, qa:W], lhsT=ka[0:70, kt * 128:kt * 128 + nk], rhs=qa_[0:70, q0 + qa:q0 + W], start=True, stop=not diag), reads=[kak, kak + "c", qak, qak + "c"], writes=[psk])
                    if diag:
                        p.op("pe", lambda e, ps=ps, nk=nk, qa=qa: e.matmul(ps[0:nk, qa:W], lhsT=ident[0:nk, 0:nk], rhs=maskrow[0:nk, 0:W - qa], start=False, stop=True), reads=["ident", "maskrow"], writes=[psk])
                    pt, ptk = ptr.next()
                    p.op("act", lambda e, ps=ps, pt=pt, nk=nk, qa=qa: e.activation(out=pt[0:nk, qa:W], in_=ps[0:nk, qa:W], func=AF.Exp, scale=HD ** -0.5), reads=[psk], writes=[ptk])
                    if pend is not None:
                        mm2(*pend)
                    pend = (kt, nk, qa, pt, ptk, i == 0, i == len(pairs) - 1)
                mm2(*pend)
                rl, rlk = frl.next()
                o, ok = fo.next()
                p.op("dve", lambda e, po=po, rl=rl: e.reciprocal(out=rl[0:64, 0:W], in_=po[64:128, 0:W]), reads=[pok], writes=[rlk])
                p.op("dve", lambda e, po=po, rl=rl, o=o: e.tensor_tensor(out=o[0:64, 0:W], in0=po[0:64, 0:W], in1=rl[0:64, 0:W], op=ALU.mult), reads=[pok, rlk], writes=[ok])
                out_cb(o, ok, q0, W)
                yield

        def fox_core_small(ka, kak, qa_, qak, va, vak, TQ, TK, out_cb, strided=False):
            P = TK - TQ
            W = TQ
            NKT = (TK + 127) // 128
            assert NKT * W <= 512
            ps, psk = pS.next()
            for kt in range(NKT):
                nk = min(128, TK - kt * 128)
                diag = P < (kt * 128 + nk - 1)
                nfull_ = TK // 128
                kcols = slice(kt, nfull_ * 128, nfull_) if (strided and kt < nfull_) else slice(kt * 128, kt * 128 + nk)
                p.op("pe", lambda e, kt=kt, nk=nk, diag=diag, kcols=kcols: e.matmul(ps[0:nk, kt * W:(kt + 1) * W], lhsT=ka[0:70, kcols], rhs=qa_[0:70, 0:W], start=True, stop=not diag), reads=[kak, kak + "c", qak, qak + "c"], writes=[psk])
                if diag:
                    p.op("pe", lambda e, kt=kt, nk=nk: e.matmul(ps[0:nk, kt * W:(kt + 1) * W], lhsT=ident[0:nk, 0:nk], rhs=maskrow[0:nk, 0:W], start=False, stop=True), reads=["ident", "maskrow"], writes=[psk])
            pt, ptk = ptr.next()
            nfull = TK // 128
            rem = TK - nfull * 128
            p.op("act", lambda e: e.activation(out=pt[:, 0:nfull * W], in_=ps[:, 0:nfull * W], func=AF.Exp, scale=HD ** -0.5), reads=[psk], writes=[ptk])
            if rem:
                p.op("act", lambda e: e.activation(out=pt[0:rem, nfull * W:(nfull + 1) * W], in_=ps[0:rem, nfull * W:(nfull + 1) * W], func=AF.Exp, scale=HD ** -0.5), reads=[psk], writes=[ptk])
            po, pok = pO.next()
            for kt in range(NKT):
                nk = min(128, TK - kt * 128)
                p.op("pe", lambda e, kt=kt, nk=nk: e.matmul(po[:, 0:W], lhsT=va[0:nk, kt, :], rhs=pt[0:nk, kt * W:(kt + 1) * W], start=(kt == 0), stop=(kt == NKT - 1)), reads=[vak, vak + "o", ptk], writes=[pok])
            rl, rlk = frl.next()
            o, ok = fo.next()
            p.op("dve", lambda e: e.reciprocal(out=rl[0:64, 0:W], in_=po[64:128, 0:W]), reads=[pok], writes=[rlk])
            p.op("dve", lambda e: e.tensor_tensor(out=o[0:64, 0:W], in0=po[0:64, 0:W], in1=rl[0:64, 0:W], op=ALU.mult), reads=[pok, rlk], writes=[ok])
            out_cb(o, ok, 0, W)
            yield

        def csplit_seg(src, srck, dst, dstk, nu, n, emit_row):
            for r in range(3):
                cb, cbk = cbr.next()
                p.op("dve", lambda e, cb=cb, src=src: e.tensor_copy(out=cb[0:nu, 0:n], in_=src[0:nu, 0:n]), reads=[srck], writes=[cbk])
                if r < 2:
                    p.op("dve", lambda e, cb=cb, src=src, dst=dst: e.tensor_tensor(out=dst[0:nu, 0:n], in0=src[0:nu, 0:n], in1=cb[0:nu, 0:n], op=ALU.subtract), reads=[srck, cbk], writes=[dstk])
                emit_row(r, cb, cbk)
                src, srck, dst, dstk = dst, dstk, src, srck

        def fox_prompt_unit(u):
            kaug, qaug, vaug = rings_p
            ka, kak = kaug.next()
            qa_, qak = qaug.next()
            va, vak = vaug.next()
            for c0 in range(0, SEQ, FSEG):
                r, col = c0 // NPT, c0 % NPT
                s, sk = stg.next()
                gather(sk, s[0:64, 0:FSEG], g1a, FSEG, LI_F(1, u, r), 64, col, reads=["G1"], writes=[sk])
                p.op("dve", lambda e, s=s, c0=c0: e.tensor_copy(out=ka[0:64, c0:c0 + FSEG], in_=s[0:64, 0:FSEG]), reads=[sk], writes=[kak])
                s, sk = stg.next()
                gather(sk, s[0:64, 0:FSEG], g1a, FSEG, LI_F(0, u, r), 64, col, reads=["G1"], writes=[sk])
                p.op("act", lambda e, s=s, c0=c0: e.activation(out=qa_[0:64, c0:c0 + FSEG], in_=s[0:64, 0:FSEG], func=AF.Copy), reads=[sk], writes=[qak])
                s, sk = stg.next()
                gather(sk, s[0:64, 0:FSEG], g1a, FSEG, LI_F(2, u, r), 64, col, reads=["G1"], writes=[sk])
                vb, vbk = cbr.next()
                p.op("dve", lambda e, s=s, vb=vb: e.tensor_copy(out=vb[0:64, 0:FSEG], in_=s[0:64, 0:FSEG]), reads=[sk], writes=[vbk])
                for k8 in range(FSEG // 128):
                    p.op("pe", lambda e, vb=vb, k8=k8: e.transpose(pbf[:, k8 * 64:(k8 + 1) * 64], vb[0:64, k8 * 128:(k8 + 1) * 128], ident[0:64, 0:64]), reads=[vbk, "ident"], writes=["pbf"])
                kt0 = c0 // 128
                p.op("act", lambda e, kt0=kt0: e.activation(out=va[:, kt0:kt0 + FSEG // 128, 0:64], in_=pbf[:, 0:FSEG // 2].rearrange("p (t d) -> p t d", d=64), func=AF.Copy), reads=["pbf"], writes=[vak])
                lw, lwk = stg.next()
                gather(lwk, lw[0:2, 0:FSEG], g1l, FSEG, LI_FL(u, r), 2, col, reads=["G1"], writes=[lwk])
                lc, lck = stg.next()
                init = 0.0 if c0 == 0 else carry[0:1, 0:1]
                p.op("dve", lambda e, lw=lw, lc=lc, init=init: e.tensor_tensor_scan(out=lc[0:1, 0:FSEG], data0=hOne[0:1, 0:FSEG], data1=lw[0:1, 0:FSEG], initial=init, op0=ALU.mult, op1=ALU.add), reads=[lwk, "hOne", "carry"], writes=[lck])
                p.op("dve", lambda e, lc=lc: e.tensor_copy(out=carry[0:1, 0:1], in_=lc[0:1, FSEG - 1:FSEG]), reads=[lck], writes=["carry"])
                p.op("dve", lambda e, lc=lc: e.tensor_scalar(out=lc[0:1, 0:FSEG], in0=lc[0:1, 0:FSEG], scalar1=8.0, scalar2=None, op0=ALU.mult), reads=[lck], writes=[lck])

                def emit_row(r_, cb, cbk, c0=c0):
                    p.dma("sp", kak + "c", ka[67 + r_:68 + r_, c0:c0 + FSEG], cb[0:1, 0:FSEG], reads=[cbk, kak + "c"], writes=[kak + "c"])
                    p.dma("sp", qak + "c", qa_[64 + r_:65 + r_, c0:c0 + FSEG], cb[0:1, 0:FSEG], reads=[cbk, qak + "c"], writes=[qak + "c"])
                csplit_seg(lc, lck, lw, lwk, 1, FSEG, emit_row)
                yield

            def out_cb(o, ok, q0, W, u=u):
                for half in range(2):
                    p.dma("sp", "o_" + ok, S2[l][u * 2 + half, :, q0:q0 + W], o[half * 32:(half + 1) * 32, 0:W], reads=[ok], writes=[f"S2f{u}_{q0}_{half}"])
            yield from fox_core(ka, kak, qa_, qak, va, vak, SEQ, SEQ, out_cb)
            for half in range(2):
                b = u * 2 + half
                allgather(f"ag2_{l}", S2[l][b], G2[l][b], reads=[f"S2f{u}_{q0}_{half}" for q0 in range(0, SEQ, 512)], writes=[f"G2_{b}"])

        def fox_prompt_all():
            for u in range(2):
                yield from fox_prompt_unit(u)

        hA = hK = hB = hNB = hQ = hVf = hVt = QpT = KpT = hVb = hO = ebl = eqr = ekr = amr = ktr = S = Sb = St = lbt = rings_p = NCS = None
        def alloc_stage_b():
            nonlocal hA, hK, hB, hNB, hQ, hVf, hVt, QpT, KpT, hVb, hO, ebl, eqr, ekr, amr, ktr, S, Sb, St, lbt, rings_p, NCS
            hA = p.sb([128, HSEG], F32, "hA")
            hK = p.sb([128, HSEG], F32, "hK")
            hB = p.sb([128, HSEG], F32, "hB")
            hNB = p.sb([128, HSEG], F32, "hNB")
            hQ = p.sb([128, HSEG], F32, "hQ")
            hVf = p.sb([128, HSEG], F32, "hVf")
            hVt = p.sb([128, HSEG], BF16, "hVt")
            QpT = p.sb([128, HSEG], BF16, "QpT")
            KpT = p.sb([128, HSEG], BF16, "KpT")
            NCS = HSEG // CH
            hVb = p.sb([CH, NCS, 128], BF16, "hVb")
            hO = p.sb([128, HSEG], F32, "hO")
            ebl = p.sb([128, NCS], F32, "ebl")
            eqr = Ring([p.sb([128, CH], F32, f"eq{i}") for i in range(2)], "eq")
            ekr = Ring([p.sb([128, CH], F32, f"ek{i}") for i in range(2)], "ek")
            amr = Ring([p.sb([CH, CH], BF16, f"am{i}") for i in range(2)], "am")
            ktr = Ring([p.sb([CH, 128], BF16, f"kt{i}") for i in range(2)], "kt")
            S = p.sb([128, 128], F32, "S")
            Sb = p.sb([128, 128], BF16, "Sb")
            St = p.sb([128, 128], F32, "St")
            lbt = p.sb([128, 4], F32, "lbt")

            rings_p = mkrings("P", 1, SEQ, SEQ)

        def hgrn_unit(T, C, load_seg, lb_ap, s0_ap, out_seg, sfin_ap):
            p.dma("sp", "lbt", lbt[:, 0:3], lb_ap, reads=["lbt"], writes=["lbt"])
            p.op("dve", lambda e: e.tensor_tensor(out=lbt[:, 3:4], in0=lbt[:, 1:2], in1=lbt[:, 0:1], op=ALU.subtract), reads=["lbt"], writes=["lbt"])
            p.op("act", lambda e: e.activation(out=lbt[:, 3:4], in_=lbt[:, 3:4], func=AF.Sigmoid), reads=["lbt"], writes=["lbt"])
            p.op("dve", lambda e: e.tensor_tensor(out=lbt[:, 0:1], in0=lbt[:, 3:4], in1=lbt[:, 2:3], op=ALU.mult), reads=["lbt"], writes=["lbt"])
            p.op("dve", lambda e: e.tensor_scalar(out=lbt[:, 1:2], in0=lbt[:, 0:1], scalar1=-1.0, scalar2=1.0, op0=ALU.mult, op1=ALU.add), reads=["lbt"], writes=["lbt"])
            if s0_ap is None:
                p.op("dve", lambda e: e.memset(S[:], 0.0), reads=["S"], writes=["S"])
                p.op("dve", lambda e: e.memset(Sb[:], 0.0), reads=["Sb"], writes=["Sb"])
            else:
                p.dma("sp", "S0", S[:], s0_ap, reads=["S"], writes=["S"])
                p.op("dve", lambda e: e.tensor_copy(out=Sb[:], in_=S[:]), reads=["S"], writes=["Sb"])
            s0 = 0
            while s0 < T:
                n = min(HSEG, T - s0)
                nch = n // C
                load_seg(s0, n)
                p.op("dve", lambda e, n=n: e.tensor_copy(out=hVt[:, 0:n], in_=hVf[:, 0:n]), reads=["hVf"], writes=["hVt"])
                for ci in range(nch):
                    p.op("pe", lambda e, ci=ci: e.transpose(pbf[0:C, ci * 128:(ci + 1) * 128], hVt[:, ci * C:(ci + 1) * C], ident[:]), reads=["hVt", "ident"], writes=["pbf"])
                p.op("act", lambda e, nch=nch: e.activation(out=hVb[0:C, 0:nch, :], in_=pbf[0:C, 0:nch * 128].rearrange("p (t d) -> p t d", d=128), func=AF.Copy), reads=["pbf"], writes=["hVb"])
                p.op("act", lambda e, n=n: e.activation(out=hA[:, 0:n], in_=hA[:, 0:n], func=AF.Sigmoid), reads=["hA"], writes=["hA"])
                p.op("act", lambda e, n=n: e.activation(out=hQ[:, 0:n], in_=hQ[:, 0:n], func=AF.Silu), reads=["hQ"], writes=["hQ"])
                p.op("dve", lambda e, n=n: e.tensor_scalar(out=hA[:, 0:n], in0=hA[:, 0:n], scalar1=lbt[:, 1:2], scalar2=lbt[:, 0:1], op0=ALU.mult, op1=ALU.add), reads=["hA", "lbt"], writes=["hA"])
                p.op("dve", lambda e, n=n: e.tensor_scalar(out=hK[:, 0:n], in0=hA[:, 0:n], scalar1=-1.0, scalar2=1.0, op0=ALU.mult, op1=ALU.add), reads=["hA"], writes=["hK"])
                p.op("act", lambda e, n=n: e.activation(out=hA[:, 0:n], in_=hA[:, 0:n], func=AF.Ln), reads=["hA"], writes=["hA"])
                p.op("dve", lambda e, n=n: e.tensor_tensor_scan(out=hB[:, 0:n], data0=hOne[:, 0:n], data1=hA[:, 0:n], initial=0.0, op0=ALU.mult, op1=ALU.add), reads=["hA", "hOne"], writes=["hB"])
                p.op("dve", lambda e, n=n: e.tensor_scalar(out=hNB[:, 0:n], in0=hB[:, 0:n], scalar1=-1.0, scalar2=None, op0=ALU.mult), reads=["hB"], writes=["hNB"])
                for ci in range(nch):
                    c0 = ci * C
                    bq = zcol[:, 0:1] if ci == 0 else hNB[:, c0 - 1:c0]
                    bk = zcol[:, 0:1] if ci == 0 else hB[:, c0 - 1:c0]
                    eq, eqk = eqr.next()
                    ek, ekk = ekr.next()
                    p.op("act", lambda e, eq=eq, c0=c0, bq=bq: e.activation(out=eq[:, 0:C], in_=hB[:, c0:c0 + C], func=AF.Exp, bias=bq), reads=["hB", "hNB", "zcol"], writes=[eqk])
                    p.op("act", lambda e, ek=ek, c0=c0, bk=bk: e.activation(out=ek[:, 0:C], in_=hB[:, c0:c0 + C], func=AF.Exp, scale=-1.0, bias=bk), reads=["hB", "zcol"], writes=[ekk])
                    p.op("dve", lambda e, eq=eq, c0=c0: e.tensor_tensor(out=QpT[:, c0:c0 + C], in0=hQ[:, c0:c0 + C], in1=eq[:, 0:C], op=ALU.mult), reads=["hQ", eqk], writes=["QpT"])
                    p.op("dve", lambda e, ek=ek, c0=c0: e.tensor_tensor(out=KpT[:, c0:c0 + C], in0=hK[:, c0:c0 + C], in1=ek[:, 0:C], op=ALU.mult), reads=["hK", ekk], writes=["KpT"])
                    p.op("dve", lambda e, eq=eq, ci=ci: e.tensor_copy(out=ebl[:, ci:ci + 1], in_=eq[:, C - 1:C]), reads=[eqk], writes=["ebl"])
                for ci in range(nch):
                    c0 = ci * C
                    pa, pak = pH.next()
                    p.op("pe", lambda e, pa=pa, c0=c0: e.matmul(pa[0:C, 0:C], lhsT=KpT[:, c0:c0 + C], rhs=QpT[:, c0:c0 + C], start=True, stop=True), reads=["KpT", "QpT"], writes=[pak])
                    am, amk = amr.next()
                    p.op("dve", lambda e, pa=pa, am=am: e.tensor_tensor(out=am[0:C, 0:C], in0=pa[0:C, 0:C], in1=mask01[0:C, 0:C], op=ALU.mult), reads=[pak, "mask01"], writes=[amk])
                    p.op("pe", lambda e, c0=c0: e.transpose(pbf[0:C, 0:128], KpT[:, c0:c0 + C], ident[:]), reads=["KpT", "ident"], writes=["pbf"])
                    ktt, ktk = ktr.next()
                    p.op("act", lambda e, t=ktt: e.activation(out=t[0:C, :], in_=pbf[0:C, 0:128], func=AF.Copy), reads=["pbf"], writes=[ktk])
                    po, pok = pH.next()
                    p.op("pe", lambda e, po=po, am=am, ci=ci: e.matmul(po[:, 0:C], lhsT=hVb[0:C, ci, :], rhs=am[0:C, 0:C], start=True, stop=False), reads=["hVb", amk], writes=[pok])
                    p.op("pe", lambda e, po=po, c0=c0: e.matmul(po[:, 0:C], lhsT=Sb[:], rhs=QpT[:, c0:c0 + C], start=False, stop=True), reads=["Sb", "QpT"], writes=[pok])
                    p.op("act", lambda e, po=po, c0=c0: e.activation(out=hO[:, c0:c0 + C], in_=po[:, 0:C], func=AF.Copy), reads=[pok], writes=["hO"])
                    pu, puk = pH.next()
                    p.op("pe", lambda e, pu=pu, t=ktt, ci=ci: e.matmul(pu[:, 0:128], lhsT=t[0:C, :], rhs=hVb[0:C, ci, :], start=True, stop=True), reads=[ktk, "hVb"], writes=[puk])
                    p.op("dve", lambda e, pu=pu: e.tensor_tensor(out=St[:], in0=S[:], in1=pu[:, 0:128], op=ALU.add), reads=["S", puk], writes=["St"])
                    p.op("dve", lambda e, ci=ci: e.tensor_scalar(out=S[:], in0=St[:], scalar1=ebl[:, ci:ci + 1], scalar2=None, op0=ALU.mult), reads=["St", "ebl"], writes=["S"])
                    p.op("act", lambda e, ci=ci: e.activation(out=Sb[:], in_=St[:], func=AF.Copy, scale=ebl[:, ci:ci + 1]), reads=["St", "ebl"], writes=["Sb"])
                    if ci % 2 == 1:
                        yield
                out_seg(s0, n)
                yield
                s0 += n
            p.dma("sp", "Sout", sfin_ap, S[:], reads=["S"], writes=[])

        def hp_load(s0, n):
            r, col = s0 // NPT, s0 % NPT
            gather("hQ", hQ[:, 0:n], g1h, HSEG, LI_H(0, r), 128, col, reads=["G1", "hQ"], writes=["hQ"])
            gather("hA", hA[:, 0:n], g1h, HSEG, LI_H(1, r), 128, col, reads=["G1", "hA"], writes=["hA"])
            gather("hVf", hVf[:, 0:n], g1h, HSEG, LI_H(2, r), 128, col, reads=["G1", "hVf"], writes=["hVf"])

        def hp_out(s0, n):
            for q4 in range(4):
                p.dma("sp", "hO", S2[l][4 + q4, :, s0:s0 + n], hO[q4 * 32:(q4 + 1) * 32, 0:n], reads=["hO"], writes=[f"S2h_{s0}_{q4}"])
        def hgrn_prompt_all():
            yield from hgrn_unit(SEQ, CH, hp_load, hplb_d[l], None, hp_out, hpS_d[l])
            for q4 in range(4):
                allgather(f"ag2_{l}", S2[l][4 + q4], G2[l][4 + q4], reads=[f"S2h_{s0}_{q4}" for s0 in range(0, SEQ, HSEG)], writes=[f"G2_{4 + q4}"])

        def fox_sample_prep():
            for t3 in range(3):
                p.dma("sp", "qkS", qkS[:, t3, :, :], projS[l][t3 * FOXW:(t3 + 1) * FOXW, :].rearrange("(h d) n -> d h n", d=64), reads=["projS"], writes=["qkS"])
            p.dma("sp", "fslf", lf[:, 0:PAST], clf_d[l][:, :], writes=["fslf"])
            for bl in range(NSB):
                p.dma("sp", "fslf", lf[bl * 8:(bl + 1) * 8, PAST:TKS], lfS[l][:, bl * TQS:(bl + 1) * TQS], reads=["lfS"], writes=["fslf"])
            c0 = 0
            while c0 < TKS:
                n = min(FSEG, TKS - c0)
                lc, lck = stg.next()
                init = 0.0 if c0 == 0 else carry[0:NFS, 0:1]
                p.op("dve", lambda e, lc=lc, c0=c0, n=n, init=init: e.tensor_tensor_scan(out=lc[0:NFS, 0:n], data0=hOne[0:NFS, 0:n], data1=lf[:, c0:c0 + n], initial=init, op0=ALU.mult, op1=ALU.add), reads=["fslf", "hOne", "carry"], writes=[lck])
                p.op("dve", lambda e, lc=lc, n=n: e.tensor_copy(out=carry[0:NFS, 0:1], in_=lc[0:NFS, n - 1:n]), reads=[lck], writes=["carry"])
                p.op("dve", lambda e, lc=lc, n=n: e.tensor_scalar(out=lc[0:NFS, 0:n], in0=lc[0:NFS, 0:n], scalar1=8.0, scalar2=None, op0=ALU.mult), reads=[lck], writes=[lck])
                lw, lwk = stg.next()

                def emit_row(r_, cb, cbk, c0=c0, n=n):
                    p.op("dve", lambda e, cb=cb: e.tensor_copy(out=cs[:, r_, c0:c0 + n], in_=cb[0:NFS, 0:n]), reads=[cbk], writes=["fscs"])
                csplit_seg(lc, lck, lw, lwk, NFS, n, emit_row)
                c0 += n
        def fox_sample_unit(u):
            bl, h = u // 8, u % 8
            kaug, qaug, vaug = rings_s
            ka, kak = kaug.next()
            qa_, qak = qaug.next()
            va, vak = vaug.next()
            nfull = PAST // 128
            cols = slice(bl * TQS, (bl + 1) * TQS)
            s, sk = stgA.next()
            p.dma("sp", sk, s[0:64, 0:PAST], ckT_d[l][u, :, :], writes=[sk])
            p.op("dve", lambda e, s=s: e.tensor_copy(out=ka[0:64, 0:PAST], in_=s[0:64, 0:PAST]), reads=[sk], writes=[kak])
            p.op("dve", lambda e: e.tensor_copy(out=ka[0:64, PAST:TKS], in_=qkS[:, 1, h, cols]), reads=["qkS"], writes=[kak])
            p.op("act", lambda e: e.activation(out=qa_[0:64, 0:TQS], in_=qkS[:, 0, h, cols], func=AF.Copy), reads=["qkS"], writes=[qak])
            for r in range(3):
                p.dma("act", kak + "c", ka[67 + r:68 + r, 0:TKS], cs[u:u + 1, r, 0:TKS], reads=["fscs", kak + "c"], writes=[kak + "c"])
                p.dma("act", qak + "c", qa_[64 + r:65 + r, 0:TQS], cs[u:u + 1, r, PAST:TKS], reads=["fscs", qak + "c"], writes=[qak + "c"])
            s, sk = stgA.next()
            p.dma("sp", sk, s[:, 0:nfull * 64], cv_d[l][u].rearrange("(p t) d -> p (t d)", t=nfull), writes=[sk])
            p.op("act", lambda e, s=s: e.activation(out=va[:, 0:nfull, 0:64], in_=s[:, 0:nfull * 64].rearrange("p (t d) -> p t d", d=64), func=AF.Copy), reads=[sk], writes=[vak])
            p.op("dve", lambda e: e.tensor_copy(out=vtb[:, :], in_=qkS[:, 2, h, cols]), reads=["qkS"], writes=["vtb"])
            p.op("pe", lambda e: e.transpose(pbf[0:TQS, 0:64], vtb[:, :], ident[0:64, 0:64]), reads=["vtb", "ident"], writes=["pbf"])
            p.op("act", lambda e: e.activation(out=va[0:TQS, nfull, 0:64], in_=pbf[0:TQS, 0:64], func=AF.Copy), reads=["pbf"], writes=[vak])

            def out_cb(o, ok, q0, W, bl=bl, h=h):
                p.dma("sp", "o_" + ok, foxS[l][h * 64:(h + 1) * 64, bl * TQS:(bl + 1) * TQS], o[0:64, 0:W], reads=[ok], writes=[])
            yield from fox_core_small(ka, kak, qa_, qak, va, vak, TQS, TKS, out_cb, strided=True)

        def fox_sample_all():
            fox_sample_prep()
            for u in range(NFS):
                yield from fox_sample_unit(u)

        def hgrn_sample_unit(u):
            bl, hd = u // 4, u % 4

            def hs_load(s0, n, bl=bl, hd=hd):
                cols = slice(bl * TQS, (bl + 1) * TQS)
                p.dma("sp", "hQ", hQ[:, 0:n], projS[l][H0 + hd * 128:H0 + (hd + 1) * 128, cols], reads=["projS", "hQ"], writes=["hQ"])
                p.dma("sp", "hA", hA[:, 0:n], projS[l][H0 + HGW + hd * 128:H0 + HGW + (hd + 1) * 128, cols], reads=["projS", "hA"], writes=["hA"])
                p.dma("sp", "hVf", hVf[:, 0:n], projS[l][H0 + 2 * HGW + hd * 128:H0 + 2 * HGW + (hd + 1) * 128, cols], reads=["projS", "hVf"], writes=["hVf"])

            def hs_out(s0, n, bl=bl, hd=hd):
                p.dma("sp", "hO", hoS[l][hd * 128:(hd + 1) * 128, bl * TQS:(bl + 1) * TQS], hO[:, 0:n], reads=["hO"], writes=[])
            yield from hgrn_unit(TQS, min(CH, TQS), hs_load, hslb_d[l][u], st0_d[l][u], hs_out, hsS_d[l][u])

        def hgrn_sample_all():
            for u in range(NHS):
                yield from hgrn_sample_unit(u)

        def interleave(gens):
            gens = list(gens)
            while gens:
                for g in list(gens):
                    try:
                        next(g)
                    except StopIteration:
                        gens.remove(g)

        interleave([fox_sample_all()])
        p.fence()
        p.arena_off = arena_mark
        alloc_stage_b()

        def hgrn_chain():
            yield from hgrn_prompt_all()
            yield from hgrn_sample_all()
        interleave([fox_prompt_all(), hgrn_chain()])
        for _ in range(2):
            allgather(f"ag2_{l}", fl_src[:, :], fl_dst[:, :], reads=[], writes=["flush"])

    t_phase(None, 0, False, True)
    for l in range(L):
        p.fence(skip=(f"ag1_{l}",))
        m_phase(l)
        p.fence()
        if l + 1 < L:
            t_phase(l, l + 1, False, False)
        else:
            t_phase(l, None, True, False)
    p.emit()
    return nc, p


def make_idx(j, NPT):
    SEQ = 4 * NPT
    q1024, q512, s512 = NPT // 1024, NPT // 512, SEQ // 512
    idx = np.zeros((128, NLISTS), np.int32)
    d = np.arange(64)
    e = np.arange(128)
    for t in range(3):
        for u in range(2):
            for r in range(4):
                idx[0:64, LI_F(t, u, r)] = (((t * 4 + j) * 4 + r) * 128 + u * 64 + d) * q1024
    for u in range(2):
        for r in range(4):
            idx[0, LI_FL(u, r)] = (r * 8 + 2 * j + u) * q1024
            idx[1, LI_FL(u, r)] = (r * 8 + 2 * j + (1 - u)) * q1024
    for t in range(3):
        for r in range(4):
            idx[:, LI_H(t, r)] = (((t * 4 + j) * 4 + r) * 128 + e) * q512
    for c in range(4):
        idx[:, LI_P(0, c)] = (((e // 32) * 4 + c) * 32 + e % 32) * s512 + j * q512
        idx[:, LI_P(1, c)] = (((4 + e // 32) * 4 + c) * 32 + e % 32) * s512 + j * q512
    return idx


def run_fused(inp, SEQ, PAST, DFF, BATCH=2, DEC_BATCH=32, DEC_SEQ=16, D=1024, L=2):
    f32 = np.float32
    c_ = lambda a: np.ascontiguousarray(a, dtype=f32)
    NPT = SEQ // 4
    NSB = DEC_BATCH // 8
    H, HD, HH = 8, 64, 4
    nc, p = build_fused(NPT, NSB, DEC_SEQ, PAST, D, DFF, L)
    A = {k: np.asarray(v, f32) for k, v in inp.items()}
    shared = dict(norm_ffn1=c_(A["norm_ffn1"]), ffn1_wi=c_(A["ffn1_wi"]), ffn1_wo=c_(A["ffn1_wo"]), norm_mix=c_(A["norm_mix"]),
                  w_in=c_(A["w_in"]), b_fgate=c_(A["b_fgate"]), gnorm=c_(A["hgrn_gnorm"]), w_out=c_(A["w_out"]),
                  norm_ffn2=c_(A["norm_ffn2"]), ffn2_wi=c_(A["ffn2_wi"]), ffn2_wo=c_(A["ffn2_wo"]), norm_final=c_(A["norm_final"]))
    lbp = A["hgrn_lb"].reshape(L, HH, 128)
    maps = []
    for c in range(8):
        b, j = c // 4, c % 4
        bs = slice(NSB * c, NSB * (c + 1))
        xt = np.concatenate([A["x_prompt"][b, j * NPT:(j + 1) * NPT], A["x_sample"][bs].reshape(NSB * DEC_SEQ, D)], axis=0)
        lb3 = np.stack([np.stack([lbp[0], lbp[l], np.full((HH, 128), 1.0 if l > 0 else 0.0, f32)], axis=-1) for l in range(L)])
        m = dict(shared)
        m.update(xT=c_(xt.T),
                 cache_kT=c_(np.swapaxes(A["cache_k"][:, bs], -1, -2).reshape(L, NSB * H, HD, PAST)),
                 cache_v=c_(A["cache_v"][:, bs].reshape(L, NSB * H, PAST, HD)),
                 cache_logf=c_(A["cache_logf"][:, bs].reshape(L, NSB * H, PAST)),
                 state0=c_(A["state_hgrn"][:, bs].reshape(L, NSB * HH, 128, 128)),
                 hp_lb=c_(lb3[:, j]), hs_lb=c_(np.tile(lb3, (1, NSB, 1, 1))),
                 idx=make_idx(j, NPT))
        maps.append(m)
    res = run_bass_kernel_spmd(nc, maps, core_ids=list(range(8))).results
    NS = NSB * DEC_SEQ
    y_p = np.zeros((BATCH, SEQ, D), f32); y_s = np.zeros((DEC_BATCH, DEC_SEQ, D), f32)
    k_p = np.zeros((L, BATCH, H, SEQ, HD), f32); v_p = np.zeros_like(k_p)
    lf_p = np.zeros((L, BATCH, H, SEQ), f32)
    k_s = np.zeros((L, DEC_BATCH, H, DEC_SEQ, HD), f32); v_s = np.zeros_like(k_s)
    lf_s = np.zeros((L, DEC_BATCH, H, DEC_SEQ), f32)
    s_p = np.zeros((L, BATCH, HH, 128, 128), f32); s_s = np.zeros((L, DEC_BATCH, HH, 128, 128), f32)
    for c in range(8):
        b, j = c // 4, c % 4
        bs = slice(NSB * c, NSB * (c + 1)); ts = slice(j * NPT, (j + 1) * NPT)
        r = res[c]
        y = r["yT"].T
        y_p[b, ts] = y[:NPT]; y_s[bs] = y[NPT:].reshape(NSB, DEC_SEQ, D)
        kv = r["kvT"]
        k = kv[:, 0:512].reshape(L, H, HD, NPT + NS); v = kv[:, 512:1024].reshape(L, H, HD, NPT + NS)
        k_p[:, b, :, ts] = np.transpose(k[..., :NPT], (0, 1, 3, 2)); v_p[:, b, :, ts] = np.transpose(v[..., :NPT], (0, 1, 3, 2))
        k_s[:, bs] = np.transpose(k[..., NPT:].reshape(L, H, HD, NSB, DEC_SEQ), (0, 3, 1, 4, 2))
        v_s[:, bs] = np.transpose(v[..., NPT:].reshape(L, H, HD, NSB, DEC_SEQ), (0, 3, 1, 4, 2))
        lf = r["logfT"]
        lf_p[:, b, :, ts] = lf[..., :NPT]
        lf_s[:, bs] = np.transpose(lf[..., NPT:].reshape(L, H, NSB, DEC_SEQ), (0, 2, 1, 3))
        s_p[:, b, j] = r["hp_S"]
        s_s[:, bs] = r["hs_S"].reshape(L, NSB, HH, 128, 128)
    return (y_p, y_s, k_p, v_p, lf_p, s_p, k_s, v_s, lf_s, s_s)


def kernel(x_prompt, x_sample, cache_k, cache_v, cache_logf, state_hgrn,
           norm_ffn1, ffn1_wi, ffn1_wo, norm_mix, w_in, b_fgate, hgrn_lb, hgrn_gnorm,
           w_out, norm_ffn2, ffn2_wi, ffn2_wo, norm_final):
    inp = dict(x_prompt=x_prompt, x_sample=x_sample, cache_k=cache_k, cache_v=cache_v, cache_logf=cache_logf,
               state_hgrn=state_hgrn, norm_ffn1=norm_ffn1, ffn1_wi=ffn1_wi, ffn1_wo=ffn1_wo, norm_mix=norm_mix,
               w_in=w_in, b_fgate=b_fgate, hgrn_lb=hgrn_lb, hgrn_gnorm=hgrn_gnorm, w_out=w_out,
               norm_ffn2=norm_ffn2, ffn2_wi=ffn2_wi, ffn2_wo=ffn2_wo, norm_final=norm_final)
    return run_fused(inp, SEQ=8192, PAST=2048, DFF=2816)
```

```python
from concourse.bass_utils import run_bass_kernel_spmd
import numpy as np
import concourse.bass as bass
import concourse.mybir as mybir

F32 = mybir.dt.float32
BF16 = mybir.dt.bfloat16
I32 = mybir.dt.int32
AF = mybir.ActivationFunctionType
ALU = mybir.AluOpType
AX = mybir.AxisListType

ENGS = ["pe", "act", "dve", "pool", "sp"]
SEM_ROLL = 30000


class Prog:
    def __init__(self, nc):
        self.nc = nc
        self.q = {e: [] for e in ENGS}
        self.buf = {}
        self.dma_cnt = {}
        self.n_tensors = 0

    arena_base = None
    arena_off = 0
    arena_peak = 0

    def arena_begin(self):
        if self.arena_base is None:
            nc = self.nc
            self.arena_base = (nc.SBUF_PARTITION_SIZE_BYTES - nc.sbuf_bytes_remaining + 63) // 64 * 64
            self.arena_limit = nc.SBUF_PARTITION_SIZE_BYTES
        self.arena_off = self.arena_base

    def sb(self, shape, dtype, name=None):
        self.n_tensors += 1
        nm = "sb_" + (name or "t") + f"_{self.n_tensors}"
        if self.arena_base is None:
            return self.nc.alloc_sbuf_tensor(nm, list(shape), dtype)
        esz = 4 if dtype in (F32, I32) else 2
        nbytes = esz
        for d in shape[1:]:
            nbytes *= d
        nbytes = (nbytes + 63) // 64 * 64
        off = self.arena_off
        assert off + nbytes <= self.arena_limit, f"SBUF arena overflow: {off + nbytes} > {self.arena_limit} ({nm})"
        self.arena_off = off + nbytes
        self.arena_peak = max(self.arena_peak, self.arena_off)
        return self.nc.alloc_sbuf_tensor_at(nm, list(shape), dtype, offset=off)

    def ps(self, shape, dtype=F32, name=None):
        self.n_tensors += 1
        return self.nc.alloc_psum_tensor("ps_" + (name or "t") + f"_{self.n_tensors}", list(shape), dtype)

    def _add(self, eng, fn, reads, writes, dma_key=None, inc=16):
        idx = len(self.q[eng])
        deps = set()
        for b in reads:
            st = self.buf.get(b)
            if st is not None and st[0] is not None:
                deps.add(st[0])
        for b in writes:
            st = self.buf.get(b)
            if st is not None:
                if st[0] is not None:
                    deps.add(st[0])
                deps.update(st[1])
        me = (eng, idx)
        deps.discard(me)
        ins = {"fn": fn, "deps": deps, "dma_key": dma_key, "sig": False, "dma_val": None}
        if dma_key is not None:
            c = self.dma_cnt.get(dma_key, 0) + inc
            self.dma_cnt[dma_key] = c
            ins["dma_val"] = c
            ins["inc"] = inc
        self.q[eng].append(ins)
        for b in reads:
            st = self.buf.get(b)
            if st is None:
                self.buf[b] = [None, [me]]
            else:
                st[1].append(me)
        for b in writes:
            self.buf[b] = [me, []]
        return me

    def op(self, eng, fn, reads=(), writes=()):
        return self._add(eng, fn, tuple(reads), tuple(writes))

    def dma(self, eng, key, out, in_, reads=(), writes=(), **kw):
        return self._add(eng, lambda e: e.dma_start(out=out, in_=in_, **kw), tuple(reads), tuple(writes), dma_key=key)

    def custom_dma(self, eng, key, fn, reads=(), writes=(), inc=16):
        me = self._add(eng, fn, tuple(reads), tuple(writes), dma_key=key, inc=inc)
        return me

    def wait_keys(self, engs, keys):
        snap = {k: self.dma_cnt[k] for k in keys if k in self.dma_cnt}
        for e in engs:
            self.q[e].append({"fn": None, "deps": set(), "dma_key": None, "sig": False, "dma_val": None, "fence": snap})

    def fence(self, skip=()):
        last = set()
        for e in ENGS:
            k = len(self.q[e]) - 1
            while k >= 0 and self.q[e][k]["dma_key"] in skip and self.q[e][k]["dma_key"] is not None:
                k -= 1
            if k >= 0:
                last.add((e, k))
        snap = {k: v for k, v in self.dma_cnt.items() if k not in skip}
        for e in ENGS:
            self.q[e].append({"fn": None, "deps": set(x for x in last if x[0] != e), "dma_key": None, "sig": False,
                              "dma_val": None, "fence": snap})
        self.buf = {}

    def emit(self, final_wait_bufs=()):
        nc = self.nc
        fin = set()
        for b in final_wait_bufs:
            st = self.buf.get(b)
            if st is not None and st[0] is not None:
                fin.add(st[0])
        self.q["sp"].append({"fn": None, "deps": fin, "dma_key": None, "sig": False, "dma_val": None, "final": True})
        for e in ENGS:
            for ins in self.q[e]:
                nd = set()
                for (de, di) in ins["deps"]:
                    d = self.q[de][di]
                    if d["dma_key"] is None:
                        if de == "pe" and e == "pe":
                            continue
                        if d["fn"] is None:
                            k2 = di
                            while k2 >= 0 and (self.q[de][k2]["fn"] is None or self.q[de][k2]["dma_key"] is not None):
                                k2 -= 1
                            if k2 < 0:
                                continue
                            di = k2
                            d = self.q[de][di]
                        d["sig"] = True
                    nd.add((de, di))
                ins["deps"] = nd
        sems = {}
        for e in ENGS:
            cur = nc.alloc_semaphore(f"s_{e}_0")
            n = 0
            k = 0
            for ins in self.q[e]:
                if ins["sig"] and ins["dma_key"] is None:
                    if n >= SEM_ROLL:
                        k += 1
                        cur = nc.alloc_semaphore(f"s_{e}_{k}")
                        n = 0
                    n += 1
                    ins["sem"] = (cur, n)
        dsem = {}
        for key in self.dma_cnt:
            dsem[key] = nc.alloc_semaphore(f"d_{len(dsem)}")
        self.stats = {e: len(self.q[e]) for e in ENGS}

        def run(eng_name, eng):
            known = {}
            nwait = 0
            for ins in self.q[eng_name]:
                need = {}
                for (de, di) in ins["deps"]:
                    d = self.q[de][di]
                    if d["dma_key"] is not None:
                        s, v = dsem[d["dma_key"]], d["dma_val"]
                    else:
                        s, v = d["sem"]
                    kk = id(s)
                    if known.get(kk, 0) >= v:
                        continue
                    if kk not in need or need[kk][1] < v:
                        need[kk] = (s, v)
                for kk, (s, v) in need.items():
                    eng.wait_ge(s, v)
                    known[kk] = v
                    nwait += 1
                if ins["fn"] is None:
                    if ins.get("final"):
                        for key, cnt in self.dma_cnt.items():
                            eng.wait_ge(dsem[key], cnt)
                    if ins.get("fence") is not None:
                        for key, cnt in ins["fence"].items():
                            kk = id(dsem[key])
                            if known.get(kk, 0) < cnt:
                                eng.wait_ge(dsem[key], cnt)
                                known[kk] = cnt
                    continue
                r = ins["fn"](eng)
                if ins["dma_key"] is not None:
                    r.then_inc(dsem[ins["dma_key"]], ins.get("inc", 16))
                elif ins["sig"]:
                    r.then_inc(ins["sem"][0], 1)
            self.stats[eng_name + "_waits"] = nwait

        with nc.Block() as block:
            @block.tensor
            def _(e):
                run("pe", e)

            @block.scalar
            def _(e):
                run("act", e)

            @block.vector
            def _(e):
                run("dve", e)

            @block.gpsimd
            def _(e):
                run("pool", e)

            @block.sync
            def _(e):
                run("sp", e)


EPS = 1e-6
NEG = -30000.0
GROUPS4 = [[0, 1, 2, 3], [4, 5, 6, 7]]


class Ring:
    def __init__(self, tiles, name):
        self.tiles = tiles
        self.name = name
        self.i = 0

    def next(self):
        k = self.i % len(self.tiles)
        self.i += 1
        return self.tiles[k], f"{self.name}{k}"


def LI_F(t, u, r):
    return (t * 2 + u) * 4 + r


def LI_FL(u, r):
    return 24 + u * 4 + r


def LI_H(t, r):
    return 32 + t * 4 + r


def LI_P(t, c):
    return 44 + t * 4 + c


NLISTS = 52
FSEG = 1024
HSEG = 512
TW = 512


def build_fused(NPT, NSB, TQS, PAST, D, DFF, L=2):
    SEQ = 4 * NPT
    NS = NSB * TQS
    NTOK = NPT + NS
    TKS = PAST + TQS
    FOXW, HGW, NFG, HD = 512, 512, 8, 64
    DIN = 3 * FOXW + NFG + 4 * HGW
    KC = D // 128
    FC = DFF // 128
    HG0 = 3 * FOXW + NFG + 3 * HGW
    H0 = 3 * FOXW + NFG
    NFS = NSB * 8
    NHS = NSB * 4
    CH = 64
    assert NPT % TW == 0 and NPT % FSEG == 0 and NS <= TW

    nc = bass.Bass("TRN2", target_bir_lowering=False)
    p = Prog(nc)

    def din(name, shape, dt=F32):
        return nc.dram_tensor(name, list(shape), dt, kind="ExternalInput").ap()

    def dout(name, shape, dt=F32):
        return nc.dram_tensor(name, list(shape), dt, kind="ExternalOutput").ap()

    def dint(name, shape, dt=F32):
        return nc.dram_tensor(name, list(shape), dt).ap()

    xT_d = din("xT", [D, NTOK])
    Wd = dict(
        norm_ffn1=din("norm_ffn1", [L, D]), ffn1_wi=din("ffn1_wi", [L, D, 2 * DFF]), ffn1_wo=din("ffn1_wo", [L, DFF, D]),
        norm_mix=din("norm_mix", [L, D]), w_in=din("w_in", [L, D, DIN]), b_fgate=din("b_fgate", [L, NFG]),
        gnorm=din("gnorm", [L, 128]), w_out=din("w_out", [L, D, D]),
        norm_ffn2=din("norm_ffn2", [L, D]), ffn2_wi=din("ffn2_wi", [L, D, 2 * DFF]), ffn2_wo=din("ffn2_wo", [L, DFF, D]),
        norm_final=din("norm_final", [D]))
    ckT_d = din("cache_kT", [L, NFS, HD, PAST])
    cv_d = din("cache_v", [L, NFS, PAST, HD])
    clf_d = din("cache_logf", [L, NFS, PAST])
    st0_d = din("state0", [L, NHS, 128, 128])
    hplb_d = din("hp_lb", [L, 128, 3])
    hslb_d = din("hs_lb", [L, NHS, 128, 3])
    idx_d = din("idx", [128, NLISTS], I32)
    yT_d = dout("yT", [D, NTOK])
    kv_d = dout("kvT", [L, 2 * FOXW, NTOK])
    lfo_d = dout("logfT", [L, NFG, NTOK])
    hpS_d = dout("hp_S", [L, 128, 128])
    hsS_d = dout("hs_S", [L, NHS, 128, 128])
    projP = [dint(f"projP{l}", [DIN, NPT]) for l in range(L)]
    projS = [dint(f"projS{l}", [DIN, NS]) for l in range(L)]
    lfP = [dint(f"lfP{l}", [NFG, NPT]) for l in range(L)]
    lfS = [dint(f"lfS{l}", [NFG, NS]) for l in range(L)]
    G1h = [dint(f"G1h{l}", [12, 4 * 128, NPT]) for l in range(L)]
    G1l = [dint(f"G1l{l}", [4 * NFG, NPT]) for l in range(L)]
    P32 = [dint(f"P32_{l}", [5 * FOXW, NPT // 2]) for l in range(L)]
    G32 = [dint(f"G32_{l}", [10, 4 * 256, NPT // 2]) for l in range(L)]
    S2 = [dint(f"S2_{l}", [8, 32, SEQ]) for l in range(L)]
    G2 = [dint(f"G2_{l}", [8, 4 * 32, SEQ]) for l in range(L)]
    foxS = [dint(f"foxS{l}", [FOXW, NS]) for l in range(L)]
    fl_src = dint("flush_src", [8, 64])
    fl_dst = dint("flush_dst", [4 * 8, 64])
    hoS = [dint(f"hoS{l}", [HGW, NS]) for l in range(L)]

    xT = p.sb([128, KC, NTOK], F32, "xT")
    ones_bf = p.sb([128, 128], BF16, "ones_bf")
    ident = p.sb([128, 128], BF16, "ident")
    maskrow = p.sb([128, 512], BF16, "maskrow")
    mask01 = p.sb([128, 128], F32, "mask01")
    zcol = p.sb([128, 1], F32, "zcol")
    idx_sb = p.sb([128, NLISTS], I32, "idx")
    banks = [p.ps([128, 512], F32, f"bank{i}") for i in range(7)]
    pbf = p.ps([128, 1024], BF16, "pbf")

    def consts():
        p.op("pool", lambda e: e.memset(ones_bf[:], 1.0), writes=["ones_bf"])
        p.op("pool", lambda e: e.memset(ident[:], 0.0), writes=["ident"])
        p.op("pool", lambda e: e.affine_select(out=ident[:], in_=ident[:], pattern=[[-1, 128]], compare_op=ALU.not_equal, fill=1.0, base=0, channel_multiplier=1), reads=["ident"], writes=["ident"])
        p.op("pool", lambda e: e.memset(maskrow[:], 0.0), writes=["maskrow"])
        p.op("pool", lambda e: e.affine_select(out=maskrow[:, 0:128], in_=maskrow[:, 0:128], pattern=[[1, 128]], compare_op=ALU.is_ge, fill=NEG, base=0, channel_multiplier=-1), reads=["maskrow"], writes=["maskrow"])
        p.op("pool", lambda e: e.memset(mask01[:], 1.0), writes=["mask01"])
        p.op("pool", lambda e: e.affine_select(out=mask01[:], in_=mask01[:], pattern=[[1, 128]], compare_op=ALU.is_ge, fill=0.0, base=0, channel_multiplier=-1), reads=["mask01"], writes=["mask01"])
        p.op("pool", lambda e: e.memset(zcol[:], 0.0), writes=["zcol"])
        p.dma("sp", "idx", idx_sb[:], idx_d[:, :], writes=["idx"])

    tiles = [(t0, TW) for t0 in range(0, NPT, TW)] + [(NPT, NS)]
    NPTI = NPT // TW

    def gather(key, dst, src2d, n, li, npart, eoff, reads, writes):
        view = src2d.rearrange("r (a n) -> (r a) n", n=n)
        p.custom_dma("pool", key, lambda e: e.indirect_dma_start(out=dst, out_offset=None, in_=view[:, :], in_offset=bass.IndirectOffsetOnAxis(ap=idx_sb[0:npart, li:li + 1], axis=0), element_offset=eoff), reads=list(reads) + ["idx"], writes=writes)

    def allgather(key, src, dst, reads, writes):
        p.custom_dma("pool", key, lambda e: e.collective_compute("AllGather", ALU.bypass, replica_groups=GROUPS4, ins=[src.opt()], outs=[dst.opt()]), reads=reads, writes=writes, inc=1)

    def t_phase(l_post, l_ffn1, do_final, first):
        p.arena_begin()
        hT = p.sb([128, KC, NTOK], BF16, "hT")
        GMAX = 6
        gT = p.sb([128, GMAX, NTOK], BF16, "gT")
        psr = Ring(banks, "bank")
        wst = Ring([p.sb([128, KC, 256], F32, f"wst{i}") for i in range(2)], "wst")
        wbf = Ring([p.sb([128, KC, 256], BF16, f"wbf{i}") for i in range(2)], "wbf")
        wost = Ring([p.sb([128, GMAX, 128], F32, f"wost{i}") for i in range(2)], "wost")
        wobf = Ring([p.sb([128, GMAX, 128], BF16, f"wobf{i}") for i in range(2)], "wobf")
        sq = p.sb([128, KC, 512], BF16, "sq")
        tmp = Ring([p.sb([128, 512], F32, f"tmp{i}") for i in range(12)], "tmp")
        finr = Ring([p.sb([128, 512], F32, f"fin{i}") for i in range(3)], "fin") if do_final else None
        if first:
            consts()
            for ti, (t0, n) in enumerate(tiles):
                p.dma("sp", f"xin{ti}", xT[:, :, t0:t0 + n], xT_d[:, t0:t0 + n].rearrange("(c p) n -> p c n", p=128), writes=[f"xT{ti}"])

        def load_gain(g_d, key):
            g = p.sb([128, KC], F32, key)
            p.dma("sp", key, g[:], g_d.rearrange("(c p) -> p c", p=128), writes=[key], allow_slow_non_contiguous=True)
            return g

        def rmsnorm_to_hT(g, gkey, out_fp32_dram=None):
            for ti, (t0, n) in enumerate(tiles):
                p.op("act", lambda e, t0=t0, n=n: e.activation(out=sq[:, :, 0:n], in_=xT[:, :, t0:t0 + n], func=AF.Square), reads=[f"xT{ti}"], writes=["sq"])
                ps, pk = psr.next()
                for c in range(KC):
                    p.op("pe", lambda e, c=c, n=n, ps=ps: e.matmul(ps[:, 0:n], lhsT=ones_bf[:], rhs=sq[:, c, 0:n], start=(c == 0), stop=(c == KC - 1)), reads=["sq", "ones_bf"], writes=[pk])
                sd, sk = tmp.next()
                p.op("act", lambda e, n=n, ps=ps, sd=sd: e.activation(out=sd[:, 0:n], in_=ps[:, 0:n], func=AF.Sqrt, scale=1.0 / D, bias=EPS), reads=[pk], writes=[sk])
                p.op("dve", lambda e, n=n, sd=sd: e.reciprocal(out=sd[:, 0:n], in_=sd[:, 0:n]), reads=[sk], writes=[sk])
                for c in range(KC):
                    if out_fp32_dram is None:
                        p.op("dve", lambda e, c=c, t0=t0, n=n, sd=sd: e.scalar_tensor_tensor(out=hT[:, c, t0:t0 + n], in0=xT[:, c, t0:t0 + n], scalar=g[:, c:c + 1], in1=sd[:, 0:n], op0=ALU.mult, op1=ALU.mult), reads=[f"xT{ti}", sk, gkey], writes=[f"hT{ti}"])
                    else:
                        o, ok = finr.next()
                        p.op("dve", lambda e, c=c, t0=t0, n=n, sd=sd, o=o: e.scalar_tensor_tensor(out=o[:, 0:n], in0=xT[:, c, t0:t0 + n], scalar=g[:, c:c + 1], in1=sd[:, 0:n], op0=ALU.mult, op1=ALU.mult), reads=[f"xT{ti}", sk, gkey], writes=[ok])
                        p.dma("sp", "o_" + ok, out_fp32_dram[c * 128:(c + 1) * 128, t0:t0 + n], o[:, 0:n], reads=[ok], writes=[])

        def load_w_block(w_d, c0, ncols):
            st, stk = wst.next()
            p.dma("sp", stk, st[:, :, 0:ncols], w_d[:, c0:c0 + ncols].rearrange("(c p) n -> p c n", p=128), writes=[stk])
            wb, wbk = wbf.next()
            p.op("act", lambda e: e.activation(out=wb[:, :, 0:ncols], in_=st[:, :, 0:ncols], func=AF.Copy), reads=[stk], writes=[wbk])
            return wb, wbk

        def ffn(pref, l):
            g = load_gain(Wd["norm_" + pref][l], f"g_{pref}{l}")
            wi_d = Wd[pref + "_wi"][l]
            wo_d = Wd[pref + "_wo"][l]
            rmsnorm_to_hT(g, f"g_{pref}{l}")
            f = 0
            while f < FC:
                gsz = min(GMAX, FC - f)
                f0 = f
                fl = 0
                while fl < gsz:
                    nb = min(2, gsz - fl)
                    wa, wak = load_w_block(wi_d, (f0 + fl) * 128, nb * 128)
                    wb_, wbk_ = load_w_block(wi_d, DFF + (f0 + fl) * 128, nb * 128)
                    for j in range(nb):
                        for ti, (t0, n) in enumerate(tiles):
                            pa, pak = psr.next()
                            pb, pbk = psr.next()
                            for c in range(KC):
                                p.op("pe", lambda e, c=c, j=j, t0=t0, n=n, pa=pa, wa=wa: e.matmul(pa[:, 0:n], lhsT=wa[:, c, j * 128:(j + 1) * 128], rhs=hT[:, c, t0:t0 + n], start=(c == 0), stop=(c == KC - 1)), reads=[wak, f"hT{ti}"], writes=[pak])
                            for c in range(KC):
                                p.op("pe", lambda e, c=c, j=j, t0=t0, n=n, pb=pb, wb_=wb_: e.matmul(pb[:, 0:n], lhsT=wb_[:, c, j * 128:(j + 1) * 128], rhs=hT[:, c, t0:t0 + n], start=(c == 0), stop=(c == KC - 1)), reads=[wbk_, f"hT{ti}"], writes=[pbk])
                            sa, sak = tmp.next()
                            p.op("act", lambda e, n=n, pa=pa, sa=sa: e.activation(out=sa[:, 0:n], in_=pa[:, 0:n], func=AF.Silu), reads=[pak], writes=[sak])
                            p.op("dve", lambda e, n=n, t0=t0, pb=pb, sa=sa, fi=fl + j: e.tensor_tensor(out=gT[:, fi, t0:t0 + n], in0=sa[:, 0:n], in1=pb[:, 0:n], op=ALU.mult), reads=[sak, pbk], writes=[f"gT{ti}"])
                    fl += nb
                for oc in range(KC):
                    st, stk = wost.next()
                    p.dma("sp", stk, st[:, 0:gsz, :], wo_d[f0 * 128:(f0 + gsz) * 128, oc * 128:(oc + 1) * 128].rearrange("(c p) n -> p c n", p=128), writes=[stk])
                    wo, wok = wobf.next()
                    p.op("dve", lambda e, st=st, wo=wo, gsz=gsz: e.tensor_copy(out=wo[:, 0:gsz, :], in_=st[:, 0:gsz, :]), reads=[stk], writes=[wok])
                    for ti, (t0, n) in enumerate(tiles):
                        ps, pk = psr.next()
                        for fi in range(gsz):
                            p.op("pe", lambda e, fi=fi, t0=t0, n=n, ps=ps, wo=wo: e.matmul(ps[:, 0:n], lhsT=wo[:, fi, :], rhs=gT[:, fi, t0:t0 + n], start=(fi == 0), stop=(fi == gsz - 1)), reads=[wok, f"gT{ti}"], writes=[pk])
                        p.op("dve", lambda e, oc=oc, t0=t0, n=n, ps=ps: e.scalar_tensor_tensor(out=xT[:, oc, t0:t0 + n], in0=ps[:, 0:n], scalar=0.5, in1=xT[:, oc, t0:t0 + n], op0=ALU.mult, op1=ALU.add), reads=[pk, f"xT{ti}"], writes=[f"xT{ti}"])
                f += gsz

        if l_post is not None:
            l = l_post
            NH = HGW // 128
            FXC = FOXW // 128
            gn = p.sb([128, 1], F32, "gn")
            p.dma("sp", "gn", gn[:], Wd["gnorm"][l].rearrange("(p o) -> p o", o=1), writes=["gn"])
            g2flat = G2[l].rearrange("b r s -> (b r) s")
            for ti, (t0, n) in enumerate(tiles):
                prompt = ti < NPTI
                for c in range(FXC):
                    a, ak = tmp.next()
                    if prompt:
                        gather(ak, a[:, 0:n], g2flat, TW, LI_P(0, c), 128, t0, reads=["G2"], writes=[ak])
                    else:
                        p.dma("sp", ak, a[:, 0:n], foxS[l][c * 128:(c + 1) * 128, :], reads=["foxS"], writes=[ak])
                    p.op("pool", lambda e, a=a, c=c, t0=t0, n=n: e.tensor_copy(out=hT[:, c, t0:t0 + n], in_=a[:, 0:n]), reads=[ak], writes=[f"hT{ti}"])
                for hd in range(NH):
                    a, ak = tmp.next()
                    b, bk = tmp.next()
                    if prompt:
                        gather(ak, a[:, 0:n], g2flat, TW, LI_P(1, hd), 128, t0, reads=["G2"], writes=[ak])
                        p.dma("sp", bk, b[:, 0:n], projP[l][HG0 + hd * 128:HG0 + (hd + 1) * 128, t0:t0 + n], reads=["projP"], writes=[bk])
                    else:
                        p.dma("sp", ak, a[:, 0:n], hoS[l][hd * 128:(hd + 1) * 128, :], reads=["hoS"], writes=[ak])
                        p.dma("sp", bk, b[:, 0:n], projS[l][HG0 + hd * 128:HG0 + (hd + 1) * 128, :], reads=["projS"], writes=[bk])
                    p.op("act", lambda e, a=a, n=n: e.activation(out=sq[:, 0, 0:n], in_=a[:, 0:n], func=AF.Square), reads=[ak], writes=["sq"])
                    ps, pk = psr.next()
                    p.op("pe", lambda e, n=n, ps=ps: e.matmul(ps[:, 0:n], lhsT=ones_bf[:], rhs=sq[:, 0, 0:n], start=True, stop=True), reads=["sq", "ones_bf"], writes=[pk])
                    sd, sk = tmp.next()
                    p.op("act", lambda e, n=n, ps=ps, sd=sd: e.activation(out=sd[:, 0:n], in_=ps[:, 0:n], func=AF.Sqrt, scale=1.0 / 128, bias=EPS), reads=[pk], writes=[sk])
                    p.op("dve", lambda e, n=n, sd=sd: e.reciprocal(out=sd[:, 0:n], in_=sd[:, 0:n]), reads=[sk], writes=[sk])
                    p.op("act", lambda e, b=b, n=n: e.activation(out=b[:, 0:n], in_=b[:, 0:n], func=AF.Silu), reads=[bk], writes=[bk])
                    p.op("dve", lambda e, a=a, sd=sd, n=n: e.scalar_tensor_tensor(out=a[:, 0:n], in0=a[:, 0:n], scalar=gn[:, 0:1], in1=sd[:, 0:n], op0=ALU.mult, op1=ALU.mult), reads=[ak, sk, "gn"], writes=[ak])
                    p.op("dve", lambda e, a=a, b=b, hd=hd, t0=t0, n=n: e.tensor_tensor(out=hT[:, FXC + hd, t0:t0 + n], in0=a[:, 0:n], in1=b[:, 0:n], op=ALU.mult), reads=[ak, bk], writes=[f"hT{ti}"])
            for oc in range(0, KC, 2):
                w, wk = load_w_block(Wd["w_out"][l], oc * 128, 256)
                for j in range(2):
                    for ti, (t0, n) in enumerate(tiles):
                        ps, pk = psr.next()
                        for c in range(KC):
                            p.op("pe", lambda e, c=c, j=j, t0=t0, n=n, ps=ps, w=w: e.matmul(ps[:, 0:n], lhsT=w[:, c, j * 128:(j + 1) * 128], rhs=hT[:, c, t0:t0 + n], start=(c == 0), stop=(c == KC - 1)), reads=[wk, f"hT{ti}"], writes=[pk])
                        p.op("dve", lambda e, oc=oc + j, t0=t0, n=n, ps=ps: e.tensor_tensor(out=xT[:, oc, t0:t0 + n], in0=ps[:, 0:n], in1=xT[:, oc, t0:t0 + n], op=ALU.add), reads=[pk, f"xT{ti}"], writes=[f"xT{ti}"])
            ffn("ffn2", l)

        if l_ffn1 is not None:
            l = l_ffn1
            ffn("ffn1", l)
            g = load_gain(Wd["norm_mix"][l], f"g_mix{l}")
            win_d = Wd["w_in"][l]
            negb = p.sb([NFG, 1], F32, "negb")
            p.dma("sp", "negb", negb[:], Wd["b_fgate"][l].rearrange("(p o) -> p o", o=1), writes=["negb"])
            p.op("dve", lambda e: e.tensor_scalar(out=negb[:], in0=negb[:], scalar1=-1.0, scalar2=None, op0=ALU.mult), reads=["negb"], writes=["negb"])
            rmsnorm_to_hT(g, f"g_mix{l}")
            FG0 = 3 * FOXW
            blocks = []
            c0 = 0
            while c0 < FG0:
                blocks.append((c0, min(256, FG0 - c0)))
                c0 += 256
            blocks.append((FG0, NFG))
            c0 = FG0 + NFG
            while c0 < DIN:
                blocks.append((c0, min(256, DIN - c0)))
                c0 += 256
            o16r = Ring([p.sb([128, 512], BF16, f"o16_{i}") for i in range(4)], "o16")
            for (c0, ncols) in blocks:
                w, wk = load_w_block(win_d, c0, ncols)
                if c0 < 3 * FOXW:
                    row16 = c0
                elif H0 <= c0 < H0 + HGW:
                    row16 = 3 * FOXW + (c0 - H0)
                elif H0 + 2 * HGW <= c0 < HG0:
                    row16 = 4 * FOXW + (c0 - H0 - 2 * HGW)
                else:
                    row16 = None
                p16keys = []
                j0 = 0
                while j0 < ncols:
                    m = min(128, ncols - j0)
                    r0 = c0 + j0
                    rkey = f"projP_r{r0}"
                    for ti, (t0, n) in enumerate(tiles):
                        prompt = ti < NPTI
                        ps, pk = psr.next()
                        for c in range(KC):
                            p.op("pe", lambda e, c=c, j0=j0, m=m, t0=t0, n=n, ps=ps, w=w: e.matmul(ps[0:m, 0:n], lhsT=w[:, c, j0:j0 + m], rhs=hT[:, c, t0:t0 + n], start=(c == 0), stop=(c == KC - 1)), reads=[wk, f"hT{ti}"], writes=[pk])
                        o, ok = tmp.next()
                        p.op("act", lambda e, m=m, n=n, ps=ps, o=o: e.activation(out=o[0:m, 0:n], in_=ps[0:m, 0:n], func=AF.Copy), reads=[pk], writes=[ok])
                        if prompt and row16 is not None:
                            o16, o16k = o16r.next()
                            p.op("act", lambda e, m=m, n=n, ps=ps, o16=o16: e.activation(out=o16[0:m, 0:n], in_=ps[0:m, 0:n], func=AF.Copy), reads=[pk], writes=[o16k])
                            kk = f"p16_{r0}_{ti}"
                            p16keys.append(kk)
                            p.dma("act", "o_" + o16k, P32[l][row16 + j0:row16 + j0 + m, t0 // 2:(t0 + n) // 2], o16[0:m, 0:n].bitcast(F32), reads=[o16k], writes=[kk])
                        elif prompt:
                            p.dma("act", "o_" + ok, projP[l][r0:r0 + m, t0:t0 + n], o[0:m, 0:n], reads=[ok], writes=[rkey + f"_{ti}"])
                        else:
                            p.dma("act", "o_" + ok, projS[l][r0:r0 + m, :], o[0:m, 0:n], reads=[ok], writes=[])
                        if FOXW <= r0 < 3 * FOXW:
                            p.dma("act", "o_" + ok, kv_d[l][r0 - FOXW:r0 - FOXW + m, t0:t0 + n], o[0:m, 0:n], reads=[ok], writes=[])
                        if c0 == FG0:
                            lt, lk = tmp.next()
                            p.op("act", lambda e, m=m, n=n, o=o, lt=lt: e.activation(out=lt[0:m, 0:n], in_=o[0:m, 0:n], func=AF.Exp, scale=-1.0, bias=negb[:, 0:1]), reads=[ok, "negb"], writes=[lk])
                            p.op("act", lambda e, m=m, n=n, lt=lt: e.activation(out=lt[0:m, 0:n], in_=lt[0:m, 0:n], func=AF.Ln, bias=1.0), reads=[lk], writes=[lk])
                            p.op("dve", lambda e, m=m, n=n, lt=lt: e.tensor_scalar(out=lt[0:m, 0:n], in0=lt[0:m, 0:n], scalar1=-1.0, scalar2=None, op0=ALU.mult), reads=[lk], writes=[lk])
                            p.dma("act", "o_" + lk, lfo_d[l][:, t0:t0 + n], lt[0:m, 0:n], reads=[lk], writes=[])
                            if prompt:
                                p.dma("act", "o_" + lk, lfP[l][:, t0:t0 + n], lt[0:m, 0:n], reads=[lk], writes=[f"lfP_{ti}"])
                            else:
                                p.dma("act", "o_" + lk, lfS[l][:, :], lt[0:m, 0:n], reads=[lk], writes=[])
                    prk = [rkey + f"_{ti}" for ti in range(NPTI)]
                    if c0 == FG0:
                        allgather(f"ag1_{l}", lfP[l][:, :], G1l[l][:, :], reads=[f"lfP_{ti}" for ti in range(NPTI)], writes=["G1l"])
                    elif row16 is None and r0 < HG0:
                        b = (r0 - H0) // 128
                        allgather(f"ag1_{l}", projP[l][r0:r0 + 128, :], G1h[l][b], reads=prk, writes=[f"G1h_{b}"])
                    j0 += m
                if row16 is not None:
                    b16 = row16 // 256
                    allgather(f"ag1_{l}", P32[l][row16:row16 + 256, :], G32[l][b16], reads=p16keys, writes=[f"G32_{b16}"])
            for _ in range(2):
                allgather(f"ag1_{l}", fl_src[:, :], fl_dst[:, :], reads=[], writes=["flush"])

        if do_final:
            g = load_gain(Wd["norm_final"], "g_fin")
            rmsnorm_to_hT(g, "g_fin", out_fp32_dram=yT_d)

    def m_phase(l):
        p.arena_begin()
        g32 = G32[l].rearrange("b r n -> (b r) n")
        g1h = G1h[l].rearrange("b r n -> (b r) n")
        g1l = G1l[l]
        pS = Ring(banks[0:2], "bank")
        pO = Ring(banks[2:4], "bankO")
        pH = Ring(banks[4:7], "bankH")
        pO.name = "bankO"
        stg = Ring([p.sb([128, FSEG], F32, f"stg{i}") for i in range(3)], "stg")
        cbr = Ring([p.sb([128, FSEG], BF16, f"cb{i}") for i in range(6)], "cb")
        hOne = p.sb([128, FSEG], F32, "hOne")
        p.op("dve", lambda e: e.memset(hOne[:], 1.0), writes=["hOne"])
        carry = p.sb([128, 1], F32, "carry")
        ptr = Ring([p.sb([128, 512], BF16, f"pt{i}") for i in range(3)], "pt")
        fo = Ring([p.sb([64, 512], F32, f"fo{i}") for i in range(2)], "fo")
        frl = Ring([p.sb([64, 512], F32, f"frl{i}") for i in range(1)], "frl")

        def mkrings(tag, nb, TKx, TQx):
            NK = (TKx + 127) // 128
            kr = Ring([p.sb([70, TKx], BF16, f"{tag}kaug{i}") for i in range(nb)], f"{tag}kaug")
            qr = Ring([p.sb([70, TQx], BF16, f"{tag}qaug{i}") for i in range(nb)], f"{tag}qaug")
            vr = Ring([p.sb([128, NK, 128], BF16, f"{tag}vaug{i}") for i in range(nb)], f"{tag}vaug")
            for i in range(nb):
                t = kr.tiles[i]
                p.op("dve", lambda e, t=t: e.memset(t[64:70, :], 1.0), writes=[f"{tag}kaug{i}c"])
                t = qr.tiles[i]
                p.op("dve", lambda e, t=t: e.memset(t[64:70, :], -1.0), writes=[f"{tag}qaug{i}c"])
                t = vr.tiles[i]
                p.op("dve", lambda e, t=t: e.memset(t[:, :, 64:128], 1.0), writes=[f"{tag}vaug{i}o"])
            return kr, qr, vr

        hA = p.sb([128, HSEG], F32, "hA")
        hK = p.sb([128, HSEG], F32, "hK")
        hB = p.sb([128, HSEG], F32, "hB")
        hNB = p.sb([128, HSEG], F32, "hNB")
        hQ = p.sb([128, HSEG], F32, "hQ")
        hVf = p.sb([128, HSEG], F32, "hVf")
        hVt = p.sb([128, HSEG], BF16, "hVt")
        hQ16 = p.sb([128, HSEG], BF16, "hQ16")
        QpT = p.sb([128, HSEG], BF16, "QpT")
        KpT = p.sb([128, HSEG], BF16, "KpT")
        NCS = HSEG // CH
        hVb = p.sb([CH, NCS, 128], BF16, "hVb")
        hO = p.sb([128, HSEG], F32, "hO")
        ebl = p.sb([128, NCS], F32, "ebl")
        eqr = Ring([p.sb([128, CH], F32, f"eq{i}") for i in range(2)], "eq")
        ekr = Ring([p.sb([128, CH], F32, f"ek{i}") for i in range(2)], "ek")
        amr = Ring([p.sb([CH, CH], BF16, f"am{i}") for i in range(2)], "am")
        ktr = Ring([p.sb([CH, 128], BF16, f"kt{i}") for i in range(2)], "kt")
        S = p.sb([128, 128], F32, "S")
        Sb = p.sb([128, 128], BF16, "Sb")
        St = p.sb([128, 128], F32, "St")
        lbt = p.sb([128, 4], F32, "lbt")


        arena_mark = p.arena_off
        rings_s = mkrings("S", 2, TKS, TQS)
        stgA = Ring([p.sb([128, PAST], F32, f"stgA{i}") for i in range(4)], "stgA")
        qkS = p.sb([64, 3, 8, NS], F32, "qkS")
        vtb = p.sb([64, TQS], BF16, "vtb")
        lf = p.sb([NFS, TKS], F32, "fs_lf")
        cs = p.sb([NFS, 3, TKS], BF16, "fs_cs")

        def fox_core(ka, kak, qa_, qak, va, vak, TQ, TK, out_cb, extra=()):
            P = TK - TQ
            W = min(512, TQ)
            for qt in range(TQ // W):
                q0 = qt * W
                po, pok = pO.next()
                last_kt = (P + q0 + W - 1) // 128
                pairs = []
                for kt in range(last_kt + 1):
                    nk = min(128, TK - kt * 128)
                    qa = max(0, kt * 128 - P - q0)
                    if qa >= W:
                        continue
                    pairs.append((kt, nk, qa, (P + q0 + qa) < (kt * 128 + nk - 1)))

                def mm2(kt, nk, qa, pt, ptk, first, last, po=po, pok=pok):
                    p.op("pe", lambda e: e.matmul(po[:, qa:W], lhsT=va[0:nk, kt, :], rhs=pt[0:nk, qa:W], start=first, stop=last), reads=[vak, vak + "o", ptk], writes=[pok])
                pend = None
                for i, (kt, nk, qa, diag) in enumerate(pairs):
                    ps, psk = pS.next()
                    p.op("pe", lambda e, ps=ps, kt=kt, nk=nk, qa=qa, diag=diag, q0=q0: e.matmul(ps[0:nk, qa:W], lhsT=ka[0:70, kt * 128:kt * 128 + nk], rhs=qa_[0:70, q0 + qa:q0 + W], start=True, stop=not diag), reads=[kak, kak + "c", qak, qak + "c"] + list(extra), writes=[psk])
                    if diag:
                        p.op("pe", lambda e, ps=ps, nk=nk, qa=qa: e.matmul(ps[0:nk, qa:W], lhsT=ident[0:nk, 0:nk], rhs=maskrow[0:nk, 0:W - qa], start=False, stop=True), reads=["ident", "maskrow"], writes=[psk])
                    pt, ptk = ptr.next()
                    p.op("act", lambda e, ps=ps, pt=pt, nk=nk, qa=qa: e.activation(out=pt[0:nk, qa:W], in_=ps[0:nk, qa:W], func=AF.Exp, scale=HD ** -0.5), reads=[psk], writes=[ptk])
                    if pend is not None:
                        mm2(*pend)
                    pend = (kt, nk, qa, pt, ptk, i == 0, i == len(pairs) - 1)
                mm2(*pend)
                rl, rlk = frl.next()
                o, ok = fo.next()
                p.op("dve", lambda e, po=po, rl=rl: e.reciprocal(out=rl[0:64, 0:W], in_=po[64:128, 0:W]), reads=[pok], writes=[rlk])
                p.op("dve", lambda e, po=po, rl=rl, o=o: e.tensor_tensor(out=o[0:64, 0:W], in0=po[0:64, 0:W], in1=rl[0:64, 0:W], op=ALU.mult), reads=[pok, rlk], writes=[ok])
                out_cb(o, ok, q0, W)
                yield

        def fox_core_small(ka, kak, qa_, qak, va, vak, TQ, TK, out_cb, strided=False):
            P = TK - TQ
            W = TQ
            NKT = (TK + 127) // 128
            assert NKT * W <= 512
            ps, psk = pS.next()
            for kt in range(NKT):
                nk = min(128, TK - kt * 128)
                diag = P < (kt * 128 + nk - 1)
                nfull_ = TK // 128
                kcols = slice(kt, nfull_ * 128, nfull_) if (strided and kt < nfull_) else slice(kt * 128, kt * 128 + nk)
                p.op("pe", lambda e, kt=kt, nk=nk, diag=diag, kcols=kcols: e.matmul(ps[0:nk, kt * W:(kt + 1) * W], lhsT=ka[0:70, kcols], rhs=qa_[0:70, 0:W], start=True, stop=not diag), reads=[kak, kak + "c", qak, qak + "c"], writes=[psk])
                if diag:
                    p.op("pe", lambda e, kt=kt, nk=nk: e.matmul(ps[0:nk, kt * W:(kt + 1) * W], lhsT=ident[0:nk, 0:nk], rhs=maskrow[0:nk, 0:W], start=False, stop=True), reads=["ident", "maskrow"], writes=[psk])
            pt, ptk = ptr.next()
            nfull = TK // 128
            rem = TK - nfull * 128
            p.op("act", lambda e: e.activation(out=pt[:, 0:nfull * W], in_=ps[:, 0:nfull * W], func=AF.Exp, scale=HD ** -0.5), reads=[psk], writes=[ptk])
            if rem:
                p.op("act", lambda e: e.activation(out=pt[0:rem, nfull * W:(nfull + 1) * W], in_=ps[0:rem, nfull * W:(nfull + 1) * W], func=AF.Exp, scale=HD ** -0.5), reads=[psk], writes=[ptk])
            po, pok = pO.next()
            for kt in range(NKT):
                nk = min(128, TK - kt * 128)
                p.op("pe", lambda e, kt=kt, nk=nk: e.matmul(po[:, 0:W], lhsT=va[0:nk, kt, :], rhs=pt[0:nk, kt * W:(kt + 1) * W], start=(kt == 0), stop=(kt == NKT - 1)), reads=[vak, vak + "o", ptk], writes=[pok])
            rl, rlk = frl.next()
            o, ok = fo.next()
            p.op("dve", lambda e: e.reciprocal(out=rl[0:64, 0:W], in_=po[64:128, 0:W]), reads=[pok], writes=[rlk])
            p.op("dve", lambda e: e.tensor_tensor(out=o[0:64, 0:W], in0=po[0:64, 0:W], in1=rl[0:64, 0:W], op=ALU.mult), reads=[pok, rlk], writes=[ok])
            out_cb(o, ok, 0, W)
            yield

        def csplit_seg(src, srck, dst, dstk, nu, n, emit_row):
            for r in range(3):
                cb, cbk = cbr.next()
                p.op("dve", lambda e, cb=cb, src=src: e.tensor_copy(out=cb[0:nu, 0:n], in_=src[0:nu, 0:n]), reads=[srck], writes=[cbk])
                if r < 2:
                    p.op("dve", lambda e, cb=cb, src=src, dst=dst: e.tensor_tensor(out=dst[0:nu, 0:n], in0=src[0:nu, 0:n], in1=cb[0:nu, 0:n], op=ALU.subtract), reads=[srck, cbk], writes=[dstk])
                emit_row(r, cb, cbk)
                src, srck, dst, dstk = dst, dstk, src, srck

        def fox_prompt_unit(u):
            kaug, qaug, vaug = rings_p
            ka, kak = kaug.next()
            qa_, qak = qaug.next()
            va, vak = vaug.next()
            for c0 in range(0, SEQ, FSEG):
                r, col = c0 // NPT, c0 % NPT
                kb, kbk = cbr.next()
                gather(kbk, kb[0:64, 0:FSEG].bitcast(F32), g32, FSEG // 2, LI_F(1, u, r), 64, col // 2, reads=["G1"], writes=[kbk])
                p.op("dve", lambda e, kb=kb, c0=c0: e.tensor_copy(out=ka[0:64, c0:c0 + FSEG], in_=kb[0:64, 0:FSEG]), reads=[kbk], writes=[f"{kak}_s{c0}"])
                qb, qbk = cbr.next()
                gather(qbk, qb[0:64, 0:FSEG].bitcast(F32), g32, FSEG // 2, LI_F(0, u, r), 64, col // 2, reads=["G1"], writes=[qbk])
                p.op("dve", lambda e, qb=qb, c0=c0: e.tensor_copy(out=qa_[0:64, c0:c0 + FSEG], in_=qb[0:64, 0:FSEG]), reads=[qbk], writes=[f"{qak}_s{c0}"])
                vb, vbk = cbr.next()
                gather(vbk, vb[0:64, 0:FSEG].bitcast(F32), g32, FSEG // 2, LI_F(2, u, r), 64, col // 2, reads=["G1"], writes=[vbk])
                for k8 in range(FSEG // 128):
                    p.op("pe", lambda e, vb=vb, k8=k8: e.transpose(pbf[:, k8 * 64:(k8 + 1) * 64], vb[0:64, k8 * 128:(k8 + 1) * 128], ident[0:64, 0:64]), reads=[vbk, "ident"], writes=["pbf"])
                kt0 = c0 // 128
                p.op("act", lambda e, kt0=kt0: e.activation(out=va[:, kt0:kt0 + FSEG // 128, 0:64], in_=pbf[:, 0:FSEG // 2].rearrange("p (t d) -> p t d", d=64), func=AF.Copy), reads=["pbf"], writes=[vak])
                lw, lwk = stg.next()
                gather(lwk, lw[0:2, 0:FSEG], g1l, FSEG, LI_FL(u, r), 2, col, reads=["G1"], writes=[lwk])
                lc, lck = stg.next()
                init = 0.0 if c0 == 0 else carry[0:1, 0:1]
                p.op("dve", lambda e, lw=lw, lc=lc, init=init: e.tensor_tensor_scan(out=lc[0:1, 0:FSEG], data0=hOne[0:1, 0:FSEG], data1=lw[0:1, 0:FSEG], initial=init, op0=ALU.mult, op1=ALU.add), reads=[lwk, "hOne", "carry"], writes=[lck])
                p.op("dve", lambda e, lc=lc: e.tensor_copy(out=carry[0:1, 0:1], in_=lc[0:1, FSEG - 1:FSEG]), reads=[lck], writes=["carry"])
                p.op("dve", lambda e, lc=lc: e.tensor_scalar(out=lc[0:1, 0:FSEG], in0=lc[0:1, 0:FSEG], scalar1=8.0, scalar2=None, op0=ALU.mult), reads=[lck], writes=[lck])

                def emit_row(r_, cb, cbk, c0=c0):
                    p.dma("sp", kak + "c", ka[67 + r_:68 + r_, c0:c0 + FSEG], cb[0:1, 0:FSEG], reads=[cbk, kak + "c"], writes=[kak + "c"])
                    p.dma("sp", qak + "c", qa_[64 + r_:65 + r_, c0:c0 + FSEG], cb[0:1, 0:FSEG], reads=[cbk, qak + "c"], writes=[qak + "c"])
                csplit_seg(lc, lck, lw, lwk, 1, FSEG, emit_row)
                yield

            def out_cb(o, ok, q0, W, u=u):
                for half in range(2):
                    p.dma("sp", "o_" + ok, S2[l][u * 2 + half, :, q0:q0 + W], o[half * 32:(half + 1) * 32, 0:W], reads=[ok], writes=[f"S2f{u}_{q0}_{half}"])
            segk = [f"{kak}_s{c0}" for c0 in range(0, SEQ, FSEG)] + [f"{qak}_s{c0}" for c0 in range(0, SEQ, FSEG)]
            yield from fox_core(ka, kak, qa_, qak, va, vak, SEQ, SEQ, out_cb, extra=segk)
            for half in range(2):
                b = u * 2 + half
                allgather(f"ag2_{l}", S2[l][b], G2[l][b], reads=[f"S2f{u}_{q0}_{half}" for q0 in range(0, SEQ, 512)], writes=[f"G2_{b}"])

        def fox_prompt_all():
            for u in range(2):
                yield from fox_prompt_unit(u)

        rings_p = None
        def alloc_stage_b():
            nonlocal rings_p
            rings_p = mkrings("P", 1, SEQ, SEQ)

        def hgrn_unit(T, C, load_seg, lb_ap, s0_ap, out_seg, sfin_ap):
            p.dma("sp", "lbt", lbt[:, 0:3], lb_ap, reads=["lbt"], writes=["lbt"])
            p.op("dve", lambda e: e.tensor_tensor(out=lbt[:, 3:4], in0=lbt[:, 1:2], in1=lbt[:, 0:1], op=ALU.subtract), reads=["lbt"], writes=["lbt"])
            p.op("act", lambda e: e.activation(out=lbt[:, 3:4], in_=lbt[:, 3:4], func=AF.Sigmoid), reads=["lbt"], writes=["lbt"])
            p.op("dve", lambda e: e.tensor_tensor(out=lbt[:, 0:1], in0=lbt[:, 3:4], in1=lbt[:, 2:3], op=ALU.mult), reads=["lbt"], writes=["lbt"])
            p.op("dve", lambda e: e.tensor_scalar(out=lbt[:, 1:2], in0=lbt[:, 0:1], scalar1=-1.0, scalar2=1.0, op0=ALU.mult, op1=ALU.add), reads=["lbt"], writes=["lbt"])
            if s0_ap is None:
                p.op("dve", lambda e: e.memset(S[:], 0.0), reads=["S"], writes=["S"])
                p.op("dve", lambda e: e.memset(Sb[:], 0.0), reads=["Sb"], writes=["Sb"])
            else:
                p.dma("sp", "S0", S[:], s0_ap, reads=["S"], writes=["S"])
                p.op("dve", lambda e: e.tensor_copy(out=Sb[:], in_=S[:]), reads=["S"], writes=["Sb"])
            s0 = 0
            while s0 < T:
                n = min(HSEG, T - s0)
                nch = n // C
                have_bf = load_seg(s0, n)
                if not have_bf:
                    p.op("dve", lambda e, n=n: e.tensor_copy(out=hVt[:, 0:n], in_=hVf[:, 0:n]), reads=["hVf"], writes=["hVt"])
                for ci in range(nch):
                    p.op("pe", lambda e, ci=ci: e.transpose(pbf[0:C, ci * 128:(ci + 1) * 128], hVt[:, ci * C:(ci + 1) * C], ident[:]), reads=["hVt", "ident"], writes=["pbf"])
                p.op("act", lambda e, nch=nch: e.activation(out=hVb[0:C, 0:nch, :], in_=pbf[0:C, 0:nch * 128].rearrange("p (t d) -> p t d", d=128), func=AF.Copy), reads=["pbf"], writes=["hVb"])
                p.op("act", lambda e, n=n: e.activation(out=hA[:, 0:n], in_=hA[:, 0:n], func=AF.Sigmoid), reads=["hA"], writes=["hA"])
                p.op("act", lambda e, n=n: e.activation(out=hQ[:, 0:n], in_=hQ[:, 0:n], func=AF.Silu), reads=["hQ"], writes=["hQ"])
                p.op("dve", lambda e, n=n: e.tensor_scalar(out=hA[:, 0:n], in0=hA[:, 0:n], scalar1=lbt[:, 1:2], scalar2=lbt[:, 0:1], op0=ALU.mult, op1=ALU.add), reads=["hA", "lbt"], writes=["hA"])
                p.op("dve", lambda e, n=n: e.tensor_scalar(out=hK[:, 0:n], in0=hA[:, 0:n], scalar1=-1.0, scalar2=1.0, op0=ALU.mult, op1=ALU.add), reads=["hA"], writes=["hK"])
                p.op("act", lambda e, n=n: e.activation(out=hA[:, 0:n], in_=hA[:, 0:n], func=AF.Ln), reads=["hA"], writes=["hA"])
                p.op("dve", lambda e, n=n: e.tensor_tensor_scan(out=hB[:, 0:n], data0=hOne[:, 0:n], data1=hA[:, 0:n], initial=0.0, op0=ALU.mult, op1=ALU.add), reads=["hA", "hOne"], writes=["hB"])
                p.op("dve", lambda e, n=n: e.tensor_scalar(out=hNB[:, 0:n], in0=hB[:, 0:n], scalar1=-1.0, scalar2=None, op0=ALU.mult), reads=["hB"], writes=["hNB"])
                for ci in range(nch):
                    c0 = ci * C
                    bq = zcol[:, 0:1] if ci == 0 else hNB[:, c0 - 1:c0]
                    bk = zcol[:, 0:1] if ci == 0 else hB[:, c0 - 1:c0]
                    eq, eqk = eqr.next()
                    ek, ekk = ekr.next()
                    p.op("act", lambda e, eq=eq, c0=c0, bq=bq: e.activation(out=eq[:, 0:C], in_=hB[:, c0:c0 + C], func=AF.Exp, bias=bq), reads=["hB", "hNB", "zcol"], writes=[eqk])
                    p.op("act", lambda e, ek=ek, c0=c0, bk=bk: e.activation(out=ek[:, 0:C], in_=hB[:, c0:c0 + C], func=AF.Exp, scale=-1.0, bias=bk), reads=["hB", "zcol"], writes=[ekk])
                    p.op("dve", lambda e, eq=eq, c0=c0: e.tensor_tensor(out=QpT[:, c0:c0 + C], in0=hQ[:, c0:c0 + C], in1=eq[:, 0:C], op=ALU.mult), reads=["hQ", eqk], writes=["QpT"])
                    p.op("dve", lambda e, ek=ek, c0=c0: e.tensor_tensor(out=KpT[:, c0:c0 + C], in0=hK[:, c0:c0 + C], in1=ek[:, 0:C], op=ALU.mult), reads=["hK", ekk], writes=["KpT"])
                    p.op("dve", lambda e, eq=eq, ci=ci: e.tensor_copy(out=ebl[:, ci:ci + 1], in_=eq[:, C - 1:C]), reads=[eqk], writes=["ebl"])
                for ci in range(nch):
                    c0 = ci * C
                    pa, pak = pH.next()
                    p.op("pe", lambda e, pa=pa, c0=c0: e.matmul(pa[0:C, 0:C], lhsT=KpT[:, c0:c0 + C], rhs=QpT[:, c0:c0 + C], start=True, stop=True), reads=["KpT", "QpT"], writes=[pak])
                    am, amk = amr.next()
                    p.op("dve", lambda e, pa=pa, am=am: e.tensor_tensor(out=am[0:C, 0:C], in0=pa[0:C, 0:C], in1=mask01[0:C, 0:C], op=ALU.mult), reads=[pak, "mask01"], writes=[amk])
                    p.op("pe", lambda e, c0=c0: e.transpose(pbf[0:C, 0:128], KpT[:, c0:c0 + C], ident[:]), reads=["KpT", "ident"], writes=["pbf"])
                    ktt, ktk = ktr.next()
                    p.op("act", lambda e, t=ktt: e.activation(out=t[0:C, :], in_=pbf[0:C, 0:128], func=AF.Copy), reads=["pbf"], writes=[ktk])
                    po, pok = pH.next()
                    p.op("pe", lambda e, po=po, am=am, ci=ci: e.matmul(po[:, 0:C], lhsT=hVb[0:C, ci, :], rhs=am[0:C, 0:C], start=True, stop=False), reads=["hVb", amk], writes=[pok])
                    p.op("pe", lambda e, po=po, c0=c0: e.matmul(po[:, 0:C], lhsT=Sb[:], rhs=QpT[:, c0:c0 + C], start=False, stop=True), reads=["Sb", "QpT"], writes=[pok])
                    p.op("act", lambda e, po=po, c0=c0: e.activation(out=hO[:, c0:c0 + C], in_=po[:, 0:C], func=AF.Copy), reads=[pok], writes=["hO"])
                    pu, puk = pH.next()
                    p.op("pe", lambda e, pu=pu, t=ktt, ci=ci: e.matmul(pu[:, 0:128], lhsT=t[0:C, :], rhs=hVb[0:C, ci, :], start=True, stop=True), reads=[ktk, "hVb"], writes=[puk])
                    p.op("dve", lambda e, pu=pu: e.tensor_tensor(out=St[:], in0=S[:], in1=pu[:, 0:128], op=ALU.add), reads=["S", puk], writes=["St"])
                    p.op("dve", lambda e, ci=ci: e.tensor_scalar(out=S[:], in0=St[:], scalar1=ebl[:, ci:ci + 1], scalar2=None, op0=ALU.mult), reads=["St", "ebl"], writes=["S"])
                    p.op("act", lambda e, ci=ci: e.activation(out=Sb[:], in_=St[:], func=AF.Copy, scale=ebl[:, ci:ci + 1]), reads=["St", "ebl"], writes=["Sb"])
                    if ci % 2 == 1:
                        yield
                out_seg(s0, n)
                yield
                s0 += n
            p.dma("sp", "Sout", sfin_ap, S[:], reads=["S"], writes=[])

        def hp_load(s0, n):
            r, col = s0 // NPT, s0 % NPT
            gather("hQ16", hQ16[:, 0:n].bitcast(F32), g32, HSEG // 2, LI_H(0, r), 128, col // 2, reads=["G1", "hQ16"], writes=["hQ16"])
            p.op("act", lambda e, n=n: e.activation(out=hQ[:, 0:n], in_=hQ16[:, 0:n], func=AF.Copy), reads=["hQ16", "hQ"], writes=["hQ"])
            gather("hA", hA[:, 0:n], g1h, HSEG, LI_H(1, r), 128, col, reads=["G1", "hA"], writes=["hA"])
            gather("hVt", hVt[:, 0:n].bitcast(F32), g32, HSEG // 2, LI_H(2, r), 128, col // 2, reads=["G1", "hVt"], writes=["hVt"])
            return True

        def hp_out(s0, n):
            for q4 in range(4):
                p.dma("sp", "hO", S2[l][4 + q4, :, s0:s0 + n], hO[q4 * 32:(q4 + 1) * 32, 0:n], reads=["hO"], writes=[f"S2h_{s0}_{q4}"])
        def hgrn_prompt_all():
            yield from hgrn_unit(SEQ, CH, hp_load, hplb_d[l], None, hp_out, hpS_d[l])
            for q4 in range(4):
                allgather(f"ag2_{l}", S2[l][4 + q4], G2[l][4 + q4], reads=[f"S2h_{s0}_{q4}" for s0 in range(0, SEQ, HSEG)], writes=[f"G2_{4 + q4}"])

        def fox_sample_prep():
            for t3 in range(3):
                p.dma("sp", "qkS", qkS[:, t3, :, :], projS[l][t3 * FOXW:(t3 + 1) * FOXW, :].rearrange("(h d) n -> d h n", d=64), reads=["projS"], writes=["qkS"])
            p.dma("sp", "fslf", lf[:, 0:PAST], clf_d[l][:, :], writes=["fslf"])
            for bl in range(NSB):
                p.dma("sp", "fslf", lf[bl * 8:(bl + 1) * 8, PAST:TKS], lfS[l][:, bl * TQS:(bl + 1) * TQS], reads=["lfS"], writes=["fslf"])
            c0 = 0
            while c0 < TKS:
                n = min(FSEG, TKS - c0)
                lc, lck = stg.next()
                init = 0.0 if c0 == 0 else carry[0:NFS, 0:1]
                p.op("dve", lambda e, lc=lc, c0=c0, n=n, init=init: e.tensor_tensor_scan(out=lc[0:NFS, 0:n], data0=hOne[0:NFS, 0:n], data1=lf[:, c0:c0 + n], initial=init, op0=ALU.mult, op1=ALU.add), reads=["fslf", "hOne", "carry"], writes=[lck])
                p.op("dve", lambda e, lc=lc, n=n: e.tensor_copy(out=carry[0:NFS, 0:1], in_=lc[0:NFS, n - 1:n]), reads=[lck], writes=["carry"])
                p.op("dve", lambda e, lc=lc, n=n: e.tensor_scalar(out=lc[0:NFS, 0:n], in0=lc[0:NFS, 0:n], scalar1=8.0, scalar2=None, op0=ALU.mult), reads=[lck], writes=[lck])
                lw, lwk = stg.next()

                def emit_row(r_, cb, cbk, c0=c0, n=n):
                    p.op("dve", lambda e, cb=cb: e.tensor_copy(out=cs[:, r_, c0:c0 + n], in_=cb[0:NFS, 0:n]), reads=[cbk], writes=["fscs"])
                csplit_seg(lc, lck, lw, lwk, NFS, n, emit_row)
                c0 += n
        def fox_sample_unit(u):
            bl, h = u // 8, u % 8
            kaug, qaug, vaug = rings_s
            ka, kak = kaug.next()
            qa_, qak = qaug.next()
            va, vak = vaug.next()
            nfull = PAST // 128
            cols = slice(bl * TQS, (bl + 1) * TQS)
            s, sk = stgA.next()
            p.dma("sp", sk, s[0:64, 0:PAST], ckT_d[l][u, :, :], writes=[sk])
            p.op("dve", lambda e, s=s: e.tensor_copy(out=ka[0:64, 0:PAST], in_=s[0:64, 0:PAST]), reads=[sk], writes=[kak])
            p.op("dve", lambda e: e.tensor_copy(out=ka[0:64, PAST:TKS], in_=qkS[:, 1, h, cols]), reads=["qkS"], writes=[kak])
            p.op("act", lambda e: e.activation(out=qa_[0:64, 0:TQS], in_=qkS[:, 0, h, cols], func=AF.Copy), reads=["qkS"], writes=[qak])
            for r in range(3):
                p.dma("act", kak + "c", ka[67 + r:68 + r, 0:TKS], cs[u:u + 1, r, 0:TKS], reads=["fscs", kak + "c"], writes=[kak + "c"])
                p.dma("act", qak + "c", qa_[64 + r:65 + r, 0:TQS], cs[u:u + 1, r, PAST:TKS], reads=["fscs", qak + "c"], writes=[qak + "c"])
            s, sk = stgA.next()
            p.dma("sp", sk, s[:, 0:nfull * 64], cv_d[l][u].rearrange("(p t) d -> p (t d)", t=nfull), writes=[sk])
            p.op("act", lambda e, s=s: e.activation(out=va[:, 0:nfull, 0:64], in_=s[:, 0:nfull * 64].rearrange("p (t d) -> p t d", d=64), func=AF.Copy), reads=[sk], writes=[vak])
            p.op("dve", lambda e: e.tensor_copy(out=vtb[:, :], in_=qkS[:, 2, h, cols]), reads=["qkS"], writes=["vtb"])
            p.op("pe", lambda e: e.transpose(pbf[0:TQS, 0:64], vtb[:, :], ident[0:64, 0:64]), reads=["vtb", "ident"], writes=["pbf"])
            p.op("act", lambda e: e.activation(out=va[0:TQS, nfull, 0:64], in_=pbf[0:TQS, 0:64], func=AF.Copy), reads=["pbf"], writes=[vak])

            def out_cb(o, ok, q0, W, bl=bl, h=h):
                p.dma("sp", "o_" + ok, foxS[l][h * 64:(h + 1) * 64, bl * TQS:(bl + 1) * TQS], o[0:64, 0:W], reads=[ok], writes=[])
            yield from fox_core_small(ka, kak, qa_, qak, va, vak, TQS, TKS, out_cb, strided=True)

        def fox_sample_all():
            fox_sample_prep()
            for u in range(NFS):
                yield from fox_sample_unit(u)

        def hgrn_sample_unit(u):
            bl, hd = u // 4, u % 4

            def hs_load(s0, n, bl=bl, hd=hd):
                cols = slice(bl * TQS, (bl + 1) * TQS)
                p.dma("sp", "hQ", hQ[:, 0:n], projS[l][H0 + hd * 128:H0 + (hd + 1) * 128, cols], reads=["projS", "hQ"], writes=["hQ"])
                p.dma("sp", "hA", hA[:, 0:n], projS[l][H0 + HGW + hd * 128:H0 + HGW + (hd + 1) * 128, cols], reads=["projS", "hA"], writes=["hA"])
                p.dma("sp", "hVf", hVf[:, 0:n], projS[l][H0 + 2 * HGW + hd * 128:H0 + 2 * HGW + (hd + 1) * 128, cols], reads=["projS", "hVf"], writes=["hVf"])

            def hs_out(s0, n, bl=bl, hd=hd):
                p.dma("sp", "hO", hoS[l][hd * 128:(hd + 1) * 128, bl * TQS:(bl + 1) * TQS], hO[:, 0:n], reads=["hO"], writes=[])
            yield from hgrn_unit(TQS, min(CH, TQS), hs_load, hslb_d[l][u], st0_d[l][u], hs_out, hsS_d[l][u])

        def hgrn_sample_all():
            for u in range(NHS):
                yield from hgrn_sample_unit(u)

        def interleave(gens):
            gens = list(gens)
            while gens:
                for g in list(gens):
                    try:
                        next(g)
                    except StopIteration:
                        gens.remove(g)

        interleave([fox_sample_all(), hgrn_sample_all()])
        p.fence()
        p.arena_off = arena_mark
        alloc_stage_b()

        interleave([fox_prompt_all(), hgrn_prompt_all()])
        for _ in range(2):
            allgather(f"ag2_{l}", fl_src[:, :], fl_dst[:, :], reads=[], writes=["flush"])

    t_phase(None, 0, False, True)
    for l in range(L):
        p.fence(skip=(f"ag1_{l}",))
        m_phase(l)
        p.fence()
        if l + 1 < L:
            t_phase(l, l + 1, False, False)
        else:
            t_phase(l, None, True, False)
    p.emit()
    return nc, p


def make_idx(j, NPT):
    SEQ = 4 * NPT
    q1024, q512, s512 = NPT // 1024, NPT // 512, SEQ // 512
    idx = np.zeros((128, NLISTS), np.int32)
    d = np.arange(64)
    e = np.arange(128)
    for t in range(3):
        for u in range(2):
            for r in range(4):
                idx[0:64, LI_F(t, u, r)] = (((t * 2 + j // 2) * 4 + r) * 256 + (j % 2) * 128 + u * 64 + d) * q1024
    for u in range(2):
        for r in range(4):
            idx[0, LI_FL(u, r)] = (r * 8 + 2 * j + u) * q1024
            idx[1, LI_FL(u, r)] = (r * 8 + 2 * j + (1 - u)) * q1024
    for r in range(4):
        idx[:, LI_H(0, r)] = (((6 + j // 2) * 4 + r) * 256 + (j % 2) * 128 + e) * q512
        idx[:, LI_H(1, r)] = (((4 + j) * 4 + r) * 128 + e) * q512
        idx[:, LI_H(2, r)] = (((8 + j // 2) * 4 + r) * 256 + (j % 2) * 128 + e) * q512
    for c in range(4):
        idx[:, LI_P(0, c)] = (((e // 32) * 4 + c) * 32 + e % 32) * s512 + j * q512
        idx[:, LI_P(1, c)] = (((4 + e // 32) * 4 + c) * 32 + e % 32) * s512 + j * q512
    return idx


def run_fused(inp, SEQ, PAST, DFF, BATCH=2, DEC_BATCH=32, DEC_SEQ=16, D=1024, L=2):
    f32 = np.float32
    c_ = lambda a: np.ascontiguousarray(a, dtype=f32)
    NPT = SEQ // 4
    NSB = DEC_BATCH // 8
    H, HD, HH = 8, 64, 4
    nc, p = build_fused(NPT, NSB, DEC_SEQ, PAST, D, DFF, L)
    A = {k: np.asarray(v, f32) for k, v in inp.items()}
    shared = dict(norm_ffn1=c_(A["norm_ffn1"]), ffn1_wi=c_(A["ffn1_wi"]), ffn1_wo=c_(A["ffn1_wo"]), norm_mix=c_(A["norm_mix"]),
                  w_in=c_(A["w_in"]), b_fgate=c_(A["b_fgate"]), gnorm=c_(A["hgrn_gnorm"]), w_out=c_(A["w_out"]),
                  norm_ffn2=c_(A["norm_ffn2"]), ffn2_wi=c_(A["ffn2_wi"]), ffn2_wo=c_(A["ffn2_wo"]), norm_final=c_(A["norm_final"]))
    lbp = A["hgrn_lb"].reshape(L, HH, 128)
    maps = []
    for c in range(8):
        b, j = c // 4, c % 4
        bs = slice(NSB * c, NSB * (c + 1))
        xt = np.concatenate([A["x_prompt"][b, j * NPT:(j + 1) * NPT], A["x_sample"][bs].reshape(NSB * DEC_SEQ, D)], axis=0)
        lb3 = np.stack([np.stack([lbp[0], lbp[l], np.full((HH, 128), 1.0 if l > 0 else 0.0, f32)], axis=-1) for l in range(L)])
        m = dict(shared)
        m.update(xT=c_(xt.T),
                 cache_kT=c_(np.swapaxes(A["cache_k"][:, bs], -1, -2).reshape(L, NSB * H, HD, PAST)),
                 cache_v=c_(A["cache_v"][:, bs].reshape(L, NSB * H, PAST, HD)),
                 cache_logf=c_(A["cache_logf"][:, bs].reshape(L, NSB * H, PAST)),
                 state0=c_(A["state_hgrn"][:, bs].reshape(L, NSB * HH, 128, 128)),
                 hp_lb=c_(lb3[:, j]), hs_lb=c_(np.tile(lb3, (1, NSB, 1, 1))),
                 idx=make_idx(j, NPT))
        maps.append(m)
    res = run_bass_kernel_spmd(nc, maps, core_ids=list(range(8))).results
    NS = NSB * DEC_SEQ
    y_p = np.zeros((BATCH, SEQ, D), f32); y_s = np.zeros((DEC_BATCH, DEC_SEQ, D), f32)
    k_p = np.zeros((L, BATCH, H, SEQ, HD), f32); v_p = np.zeros_like(k_p)
    lf_p = np.zeros((L, BATCH, H, SEQ), f32)
    k_s = np.zeros((L, DEC_BATCH, H, DEC_SEQ, HD), f32); v_s = np.zeros_like(k_s)
    lf_s = np.zeros((L, DEC_BATCH, H, DEC_SEQ), f32)
    s_p = np.zeros((L, BATCH, HH, 128, 128), f32); s_s = np.zeros((L, DEC_BATCH, HH, 128, 128), f32)
    for c in range(8):
        b, j = c // 4, c % 4
        bs = slice(NSB * c, NSB * (c + 1)); ts = slice(j * NPT, (j + 1) * NPT)
        r = res[c]
        y = r["yT"].T
        y_p[b, ts] = y[:NPT]; y_s[bs] = y[NPT:].reshape(NSB, DEC_SEQ, D)
        kv = r["kvT"]
        k = kv[:, 0:512].reshape(L, H, HD, NPT + NS); v = kv[:, 512:1024].reshape(L, H, HD, NPT + NS)
        k_p[:, b, :, ts] = np.transpose(k[..., :NPT], (0, 1, 3, 2)); v_p[:, b, :, ts] = np.transpose(v[..., :NPT], (0, 1, 3, 2))
        k_s[:, bs] = np.transpose(k[..., NPT:].reshape(L, H, HD, NSB, DEC_SEQ), (0, 3, 1, 4, 2))
        v_s[:, bs] = np.transpose(v[..., NPT:].reshape(L, H, HD, NSB, DEC_SEQ), (0, 3, 1, 4, 2))
        lf = r["logfT"]
        lf_p[:, b, :, ts] = lf[..., :NPT]
        lf_s[:, bs] = np.transpose(lf[..., NPT:].reshape(L, H, NSB, DEC_SEQ), (0, 2, 1, 3))
        s_p[:, b, j] = r["hp_S"]
        s_s[:, bs] = r["hs_S"].reshape(L, NSB, HH, 128, 128)
    return (y_p, y_s, k_p, v_p, lf_p, s_p, k_s, v_s, lf_s, s_s)


def kernel(x_prompt, x_sample, cache_k, cache_v, cache_logf, state_hgrn,
           norm_ffn1, ffn1_wi, ffn1_wo, norm_mix, w_in, b_fgate, hgrn_lb, hgrn_gnorm,
           w_out, norm_ffn2, ffn2_wi, ffn2_wo, norm_final):
    inp = dict(x_prompt=x_prompt, x_sample=x_sample, cache_k=cache_k, cache_v=cache_v, cache_logf=cache_logf,
               state_hgrn=state_hgrn, norm_ffn1=norm_ffn1, ffn1_wi=ffn1_wi, ffn1_wo=ffn1_wo, norm_mix=norm_mix,
               w_in=w_in, b_fgate=b_fgate, hgrn_lb=hgrn_lb, hgrn_gnorm=hgrn_gnorm, w_out=w_out,
               norm_ffn2=norm_ffn2, ffn2_wi=ffn2_wi, ffn2_wo=ffn2_wo, norm_final=norm_final)
    return run_fused(inp, SEQ=8192, PAST=2048, DFF=2816)
```

```python
from concourse.bass_utils import run_bass_kernel_spmd
import numpy as np
import concourse.bass as bass
import concourse.mybir as mybir

F32 = mybir.dt.float32
BF16 = mybir.dt.bfloat16
I32 = mybir.dt.int32
AF = mybir.ActivationFunctionType
ALU = mybir.AluOpType
AX = mybir.AxisListType

ENGS = ["pe", "act", "dve", "pool", "sp"]
SEM_ROLL = 30000


class Prog:
    def __init__(self, nc):
        self.nc = nc
        self.q = {e: [] for e in ENGS}
        self.buf = {}
        self.dma_cnt = {}
        self.n_tensors = 0

    arena_base = None
    arena_off = 0
    arena_peak = 0

    def arena_begin(self):
        if self.arena_base is None:
            nc = self.nc
            self.arena_base = (nc.SBUF_PARTITION_SIZE_BYTES - nc.sbuf_bytes_remaining + 63) // 64 * 64
            self.arena_limit = nc.SBUF_PARTITION_SIZE_BYTES
        self.arena_off = self.arena_base

    def sb(self, shape, dtype, name=None):
        self.n_tensors += 1
        nm = "sb_" + (name or "t") + f"_{self.n_tensors}"
        if self.arena_base is None:
            return self.nc.alloc_sbuf_tensor(nm, list(shape), dtype)
        esz = 4 if dtype in (F32, I32) else 2
        nbytes = esz
        for d in shape[1:]:
            nbytes *= d
        nbytes = (nbytes + 63) // 64 * 64
        off = self.arena_off
        assert off + nbytes <= self.arena_limit, f"SBUF arena overflow: {off + nbytes} > {self.arena_limit} ({nm})"
        self.arena_off = off + nbytes
        self.arena_peak = max(self.arena_peak, self.arena_off)
        return self.nc.alloc_sbuf_tensor_at(nm, list(shape), dtype, offset=off)

    def ps(self, shape, dtype=F32, name=None):
        self.n_tensors += 1
        return self.nc.alloc_psum_tensor("ps_" + (name or "t") + f"_{self.n_tensors}", list(shape), dtype)

    def _add(self, eng, fn, reads, writes, dma_key=None, inc=16):
        idx = len(self.q[eng])
        deps = set()
        for b in reads:
            st = self.buf.get(b)
            if st is not None and st[0] is not None:
                deps.add(st[0])
        for b in writes:
            st = self.buf.get(b)
            if st is not None:
                if st[0] is not None:
                    deps.add(st[0])
                deps.update(st[1])
        me = (eng, idx)
        deps.discard(me)
        ins = {"fn": fn, "deps": deps, "dma_key": dma_key, "sig": False, "dma_val": None}
        if dma_key is not None:
            c = self.dma_cnt.get(dma_key, 0) + inc
            self.dma_cnt[dma_key] = c
            ins["dma_val"] = c
            ins["inc"] = inc
        self.q[eng].append(ins)
        for b in reads:
            st = self.buf.get(b)
            if st is None:
                self.buf[b] = [None, [me]]
            else:
                st[1].append(me)
        for b in writes:
            self.buf[b] = [me, []]
        return me

    def op(self, eng, fn, reads=(), writes=()):
        return self._add(eng, fn, tuple(reads), tuple(writes))

    def dma(self, eng, key, out, in_, reads=(), writes=(), **kw):
        return self._add(eng, lambda e: e.dma_start(out=out, in_=in_, **kw), tuple(reads), tuple(writes), dma_key=key)

    def custom_dma(self, eng, key, fn, reads=(), writes=(), inc=16):
        me = self._add(eng, fn, tuple(reads), tuple(writes), dma_key=key, inc=inc)
        return me

    def wait_keys(self, engs, keys):
        snap = {k: self.dma_cnt[k] for k in keys if k in self.dma_cnt}
        for e in engs:
            self.q[e].append({"fn": None, "deps": set(), "dma_key": None, "sig": False, "dma_val": None, "fence": snap})

    def fence(self, skip=()):
        last = set()
        for e in ENGS:
            k = len(self.q[e]) - 1
            while k >= 0 and self.q[e][k]["dma_key"] in skip and self.q[e][k]["dma_key"] is not None:
                k -= 1
            if k >= 0:
                last.add((e, k))
        snap = {k: v for k, v in self.dma_cnt.items() if k not in skip}
        for e in ENGS:
            self.q[e].append({"fn": None, "deps": set(x for x in last if x[0] != e), "dma_key": None, "sig": False,
                              "dma_val": None, "fence": snap})
        self.buf = {}

    def emit(self, final_wait_bufs=()):
        nc = self.nc
        fin = set()
        for b in final_wait_bufs:
            st = self.buf.get(b)
            if st is not None and st[0] is not None:
                fin.add(st[0])
        self.q["sp"].append({"fn": None, "deps": fin, "dma_key": None, "sig": False, "dma_val": None, "final": True})
        for e in ENGS:
            for ins in self.q[e]:
                nd = set()
                for (de, di) in ins["deps"]:
                    d = self.q[de][di]
                    if d["dma_key"] is None:
                        if de == "pe" and e == "pe":
                            continue
                        if d["fn"] is None:
                            k2 = di
                            while k2 >= 0 and (self.q[de][k2]["fn"] is None or self.q[de][k2]["dma_key"] is not None):
                                k2 -= 1
                            if k2 < 0:
                                continue
                            di = k2
                            d = self.q[de][di]
                        d["sig"] = True
                    nd.add((de, di))
                ins["deps"] = nd
        sems = {}
        for e in ENGS:
            cur = nc.alloc_semaphore(f"s_{e}_0")
            n = 0
            k = 0
            for ins in self.q[e]:
                if ins["sig"] and ins["dma_key"] is None:
                    if n >= SEM_ROLL:
                        k += 1
                        cur = nc.alloc_semaphore(f"s_{e}_{k}")
                        n = 0
                    n += 1
                    ins["sem"] = (cur, n)
        dsem = {}
        for key in self.dma_cnt:
            dsem[key] = nc.alloc_semaphore(f"d_{len(dsem)}")
        self.stats = {e: len(self.q[e]) for e in ENGS}

        def run(eng_name, eng):
            known = {}
            nwait = 0
            for ins in self.q[eng_name]:
                need = {}
                for (de, di) in ins["deps"]:
                    d = self.q[de][di]
                    if d["dma_key"] is not None:
                        s, v = dsem[d["dma_key"]], d["dma_val"]
                    else:
                        s, v = d["sem"]
                    kk = id(s)
                    if known.get(kk, 0) >= v:
                        continue
                    if kk not in need or need[kk][1] < v:
                        need[kk] = (s, v)
                for kk, (s, v) in need.items():
                    eng.wait_ge(s, v)
                    known[kk] = v
                    nwait += 1
                if ins["fn"] is None:
                    if ins.get("final"):
                        for key, cnt in self.dma_cnt.items():
                            eng.wait_ge(dsem[key], cnt)
                    if ins.get("fence") is not None:
                        for key, cnt in ins["fence"].items():
                            kk = id(dsem[key])
                            if known.get(kk, 0) < cnt:
                                eng.wait_ge(dsem[key], cnt)
                                known[kk] = cnt
                    continue
                r = ins["fn"](eng)
                if ins["dma_key"] is not None:
                    r.then_inc(dsem[ins["dma_key"]], ins.get("inc", 16))
                elif ins["sig"]:
                    r.then_inc(ins["sem"][0], 1)
            self.stats[eng_name + "_waits"] = nwait

        with nc.Block() as block:
            @block.tensor
            def _(e):
                run("pe", e)

            @block.scalar
            def _(e):
                run("act", e)

            @block.vector
            def _(e):
                run("dve", e)

            @block.gpsimd
            def _(e):
                run("pool", e)

            @block.sync
            def _(e):
                run("sp", e)


EPS = 1e-6
NEG = -30000.0
GROUPS4 = [[0, 1, 2, 3], [4, 5, 6, 7]]


class Ring:
    def __init__(self, tiles, name):
        self.tiles = tiles
        self.name = name
        self.i = 0

    def next(self):
        k = self.i % len(self.tiles)
        self.i += 1
        return self.tiles[k], f"{self.name}{k}"


def LI_F(t, u, r):
    return (t * 2 + u) * 4 + r


def LI_FL(u, r):
    return 24 + u * 4 + r


def LI_H(t, r):
    return 32 + t * 4 + r


def LI_P(t, c):
    return 44 + t * 4 + c


NLISTS = 52
FSEG = 1024
HSEG = 512
TW = 512


def build_fused(NPT, NSB, TQS, PAST, D, DFF, L=2):
    SEQ = 4 * NPT
    NS = NSB * TQS
    NTOK = NPT + NS
    TKS = PAST + TQS
    FOXW, HGW, NFG, HD = 512, 512, 8, 64
    DIN = 3 * FOXW + NFG + 4 * HGW
    KC = D // 128
    FC = DFF // 128
    HG0 = 3 * FOXW + NFG + 3 * HGW
    H0 = 3 * FOXW + NFG
    NFS = NSB * 8
    NHS = NSB * 4
    CH = 64
    assert NPT % TW == 0 and NPT % FSEG == 0 and NS <= TW

    nc = bass.Bass("TRN2", target_bir_lowering=False)
    p = Prog(nc)

    def din(name, shape, dt=F32):
        return nc.dram_tensor(name, list(shape), dt, kind="ExternalInput").ap()

    def dout(name, shape, dt=F32):
        return nc.dram_tensor(name, list(shape), dt, kind="ExternalOutput").ap()

    def dint(name, shape, dt=F32):
        return nc.dram_tensor(name, list(shape), dt).ap()

    xT_d = din("xT", [D, NTOK])
    Wd = dict(
        norm_ffn1=din("norm_ffn1", [L, D]), ffn1_wi=din("ffn1_wi", [L, D, 2 * DFF]), ffn1_wo=din("ffn1_wo", [L, DFF, D]),
        norm_mix=din("norm_mix", [L, D]), w_in=din("w_in", [L, D, DIN]), b_fgate=din("b_fgate", [L, NFG]),
        gnorm=din("gnorm", [L, 128]), w_out=din("w_out", [L, D, D]),
        norm_ffn2=din("norm_ffn2", [L, D]), ffn2_wi=din("ffn2_wi", [L, D, 2 * DFF]), ffn2_wo=din("ffn2_wo", [L, DFF, D]),
        norm_final=din("norm_final", [D]))
    ckT_d = din("cache_kT", [L, NFS, HD, PAST])
    cv_d = din("cache_v", [L, NFS, PAST, HD])
    clf_d = din("cache_logf", [L, NFS, PAST])
    st0_d = din("state0", [L, NHS, 128, 128])
    hplb_d = din("hp_lb", [L, 128, 3])
    hslb_d = din("hs_lb", [L, NHS, 128, 3])
    idx_d = din("idx", [128, NLISTS], I32)
    yT_d = dout("yT", [D, NTOK])
    kv_d = dout("kvT", [L, 2 * FOXW, NTOK])
    lfo_d = dout("logfT", [L, NFG, NTOK])
    hpS_d = dout("hp_S", [L, 128, 128])
    hsS_d = dout("hs_S", [L, NHS, 128, 128])
    projP = [dint(f"projP{l}", [DIN, NPT]) for l in range(L)]
    projS = [dint(f"projS{l}", [DIN, NS]) for l in range(L)]
    lfP = [dint(f"lfP{l}", [NFG, NPT]) for l in range(L)]
    lfS = [dint(f"lfS{l}", [NFG, NS]) for l in range(L)]
    G1h = [dint(f"G1h{l}", [12, 4 * 128, NPT]) for l in range(L)]
    G1l = [dint(f"G1l{l}", [4 * NFG, NPT]) for l in range(L)]
    P32 = [dint(f"P32_{l}", [5 * FOXW, NPT // 2]) for l in range(L)]
    G32 = [dint(f"G32_{l}", [10, 4 * 256, NPT // 2]) for l in range(L)]
    S2 = [dint(f"S2_{l}", [8, 32, SEQ]) for l in range(L)]
    G2 = [dint(f"G2_{l}", [8, 4 * 32, SEQ]) for l in range(L)]
    foxS = [dint(f"foxS{l}", [FOXW, NS]) for l in range(L)]
    fl_src = dint("flush_src", [8, 64])
    fl_dst = dint("flush_dst", [4 * 8, 64])
    hoS = [dint(f"hoS{l}", [HGW, NS]) for l in range(L)]

    xT = p.sb([128, KC, NTOK], F32, "xT")
    ones_bf = p.sb([128, 128], BF16, "ones_bf")
    ident = p.sb([128, 128], BF16, "ident")
    maskrow = p.sb([128, 512], BF16, "maskrow")
    mask01 = p.sb([128, 128], F32, "mask01")
    zcol = p.sb([128, 1], F32, "zcol")
    idx_sb = p.sb([128, NLISTS], I32, "idx")
    banks = [p.ps([128, 512], F32, f"bank{i}") for i in range(7)]
    pbf = p.ps([128, 1024], BF16, "pbf")

    def consts():
        p.op("pool", lambda e: e.memset(ones_bf[:], 1.0), writes=["ones_bf"])
        p.op("pool", lambda e: e.memset(ident[:], 0.0), writes=["ident"])
        p.op("pool", lambda e: e.affine_select(out=ident[:], in_=ident[:], pattern=[[-1, 128]], compare_op=ALU.not_equal, fill=1.0, base=0, channel_multiplier=1), reads=["ident"], writes=["ident"])
        p.op("pool", lambda e: e.memset(maskrow[:], 0.0), writes=["maskrow"])
        p.op("pool", lambda e: e.affine_select(out=maskrow[:, 0:128], in_=maskrow[:, 0:128], pattern=[[1, 128]], compare_op=ALU.is_ge, fill=NEG, base=0, channel_multiplier=-1), reads=["maskrow"], writes=["maskrow"])
        p.op("pool", lambda e: e.memset(mask01[:], 1.0), writes=["mask01"])
        p.op("pool", lambda e: e.affine_select(out=mask01[:], in_=mask01[:], pattern=[[1, 128]], compare_op=ALU.is_ge, fill=0.0, base=0, channel_multiplier=-1), reads=["mask01"], writes=["mask01"])
        p.op("pool", lambda e: e.memset(zcol[:], 0.0), writes=["zcol"])
        p.dma("sp", "idx", idx_sb[:], idx_d[:, :], writes=["idx"])

    tiles = [(t0, TW) for t0 in range(0, NPT, TW)] + [(NPT, NS)]
    NPTI = NPT // TW

    def gather(key, dst, src2d, n, li, npart, eoff, reads, writes):
        view = src2d.rearrange("r (a n) -> (r a) n", n=n)
        p.custom_dma("pool", key, lambda e: e.indirect_dma_start(out=dst, out_offset=None, in_=view[:, :], in_offset=bass.IndirectOffsetOnAxis(ap=idx_sb[0:npart, li:li + 1], axis=0), element_offset=eoff), reads=list(reads) + ["idx"], writes=writes)

    def allgather(key, src, dst, reads, writes):
        p.custom_dma("pool", key, lambda e: e.collective_compute("AllGather", ALU.bypass, replica_groups=GROUPS4, ins=[src.opt()], outs=[dst.opt()]), reads=reads, writes=writes, inc=1)

    def t_phase(l_post, l_ffn1, do_final, first):
        p.arena_begin()
        hT = p.sb([128, KC, NTOK], BF16, "hT")
        GMAX = 6
        gT = p.sb([128, GMAX, NTOK], BF16, "gT")
        psr = Ring(banks, "bank")
        wst = Ring([p.sb([128, KC, 256], F32, f"wst{i}") for i in range(2)], "wst")
        wbf = Ring([p.sb([128, KC, 256], BF16, f"wbf{i}") for i in range(2)], "wbf")
        wost = Ring([p.sb([128, GMAX, 128], F32, f"wost{i}") for i in range(2)], "wost")
        wobf = Ring([p.sb([128, GMAX, 128], BF16, f"wobf{i}") for i in range(2)], "wobf")
        sq = p.sb([128, KC, 512], BF16, "sq")
        tmp = Ring([p.sb([128, 512], F32, f"tmp{i}") for i in range(12)], "tmp")
        finr = Ring([p.sb([128, 512], F32, f"fin{i}") for i in range(3)], "fin") if do_final else None
        if first:
            consts()
            for ti, (t0, n) in enumerate(tiles):
                p.dma("sp", f"xin{ti}", xT[:, :, t0:t0 + n], xT_d[:, t0:t0 + n].rearrange("(c p) n -> p c n", p=128), writes=[f"xT{ti}"])

        def load_gain(g_d, key):
            g = p.sb([128, KC], F32, key)
            p.dma("sp", key, g[:], g_d.rearrange("(c p) -> p c", p=128), writes=[key], allow_slow_non_contiguous=True)
            return g

        def rmsnorm_to_hT(g, gkey, out_fp32_dram=None):
            for ti, (t0, n) in enumerate(tiles):
                p.op("act", lambda e, t0=t0, n=n: e.activation(out=sq[:, :, 0:n], in_=xT[:, :, t0:t0 + n], func=AF.Square), reads=[f"xT{ti}"], writes=["sq"])
                ps, pk = psr.next()
                for c in range(KC):
                    p.op("pe", lambda e, c=c, n=n, ps=ps: e.matmul(ps[:, 0:n], lhsT=ones_bf[:], rhs=sq[:, c, 0:n], start=(c == 0), stop=(c == KC - 1)), reads=["sq", "ones_bf"], writes=[pk])
                sd, sk = tmp.next()
                p.op("act", lambda e, n=n, ps=ps, sd=sd: e.activation(out=sd[:, 0:n], in_=ps[:, 0:n], func=AF.Sqrt, scale=1.0 / D, bias=EPS), reads=[pk], writes=[sk])
                p.op("dve", lambda e, n=n, sd=sd: e.reciprocal(out=sd[:, 0:n], in_=sd[:, 0:n]), reads=[sk], writes=[sk])
                for c in range(KC):
                    if out_fp32_dram is None:
                        p.op("dve", lambda e, c=c, t0=t0, n=n, sd=sd: e.scalar_tensor_tensor(out=hT[:, c, t0:t0 + n], in0=xT[:, c, t0:t0 + n], scalar=g[:, c:c + 1], in1=sd[:, 0:n], op0=ALU.mult, op1=ALU.mult), reads=[f"xT{ti}", sk, gkey], writes=[f"hT{ti}"])
                    else:
                        o, ok = finr.next()
                        p.op("dve", lambda e, c=c, t0=t0, n=n, sd=sd, o=o: e.scalar_tensor_tensor(out=o[:, 0:n], in0=xT[:, c, t0:t0 + n], scalar=g[:, c:c + 1], in1=sd[:, 0:n], op0=ALU.mult, op1=ALU.mult), reads=[f"xT{ti}", sk, gkey], writes=[ok])
                        p.dma("sp", "o_" + ok, out_fp32_dram[c * 128:(c + 1) * 128, t0:t0 + n], o[:, 0:n], reads=[ok], writes=[])

        def load_w_block(w_d, c0, ncols):
            st, stk = wst.next()
            p.dma("sp", stk, st[:, :, 0:ncols], w_d[:, c0:c0 + ncols].rearrange("(c p) n -> p c n", p=128), writes=[stk])
            wb, wbk = wbf.next()
            p.op("act", lambda e: e.activation(out=wb[:, :, 0:ncols], in_=st[:, :, 0:ncols], func=AF.Copy), reads=[stk], writes=[wbk])
            return wb, wbk

        def ffn(pref, l):
            g = load_gain(Wd["norm_" + pref][l], f"g_{pref}{l}")
            wi_d = Wd[pref + "_wi"][l]
            wo_d = Wd[pref + "_wo"][l]
            rmsnorm_to_hT(g, f"g_{pref}{l}")
            f = 0
            while f < FC:
                gsz = min(GMAX, FC - f)
                f0 = f
                fl = 0
                while fl < gsz:
                    nb = min(2, gsz - fl)
                    wa, wak = load_w_block(wi_d, (f0 + fl) * 128, nb * 128)
                    wb_, wbk_ = load_w_block(wi_d, DFF + (f0 + fl) * 128, nb * 128)
                    for j in range(nb):
                        for ti, (t0, n) in enumerate(tiles):
                            pa, pak = psr.next()
                            pb, pbk = psr.next()
                            for c in range(KC):
                                p.op("pe", lambda e, c=c, j=j, t0=t0, n=n, pa=pa, wa=wa: e.matmul(pa[:, 0:n], lhsT=wa[:, c, j * 128:(j + 1) * 128], rhs=hT[:, c, t0:t0 + n], start=(c == 0), stop=(c == KC - 1)), reads=[wak, f"hT{ti}"], writes=[pak])
                            for c in range(KC):
                                p.op("pe", lambda e, c=c, j=j, t0=t0, n=n, pb=pb, wb_=wb_: e.matmul(pb[:, 0:n], lhsT=wb_[:, c, j * 128:(j + 1) * 128], rhs=hT[:, c, t0:t0 + n], start=(c == 0), stop=(c == KC - 1)), reads=[wbk_, f"hT{ti}"], writes=[pbk])
                            sa, sak = tmp.next()
                            p.op("act", lambda e, n=n, pa=pa, sa=sa: e.activation(out=sa[:, 0:n], in_=pa[:, 0:n], func=AF.Silu), reads=[pak], writes=[sak])
                            p.op("dve", lambda e, n=n, t0=t0, pb=pb, sa=sa, fi=fl + j: e.tensor_tensor(out=gT[:, fi, t0:t0 + n], in0=sa[:, 0:n], in1=pb[:, 0:n], op=ALU.mult), reads=[sak, pbk], writes=[f"gT{ti}"])
                    fl += nb
                for oc in range(KC):
                    st, stk = wost.next()
                    p.dma("sp", stk, st[:, 0:gsz, :], wo_d[f0 * 128:(f0 + gsz) * 128, oc * 128:(oc + 1) * 128].rearrange("(c p) n -> p c n", p=128), writes=[stk])
                    wo, wok = wobf.next()
                    p.op("dve", lambda e, st=st, wo=wo, gsz=gsz: e.tensor_copy(out=wo[:, 0:gsz, :], in_=st[:, 0:gsz, :]), reads=[stk], writes=[wok])
                    for ti, (t0, n) in enumerate(tiles):
                        ps, pk = psr.next()
                        for fi in range(gsz):
                            p.op("pe", lambda e, fi=fi, t0=t0, n=n, ps=ps, wo=wo, gsz=gsz: e.matmul(ps[:, 0:n], lhsT=wo[:, fi, :], rhs=gT[:, fi, t0:t0 + n], start=(fi == 0), stop=(fi == gsz - 1)), reads=[wok, f"gT{ti}"], writes=[pk])
                        p.op("dve", lambda e, oc=oc, t0=t0, n=n, ps=ps: e.scalar_tensor_tensor(out=xT[:, oc, t0:t0 + n], in0=ps[:, 0:n], scalar=0.5, in1=xT[:, oc, t0:t0 + n], op0=ALU.mult, op1=ALU.add), reads=[pk, f"xT{ti}"], writes=[f"xT{ti}"])
                f += gsz

        if l_post is not None:
            l = l_post
            NH = HGW // 128
            FXC = FOXW // 128
            gn = p.sb([128, 1], F32, "gn")
            p.dma("sp", "gn", gn[:], Wd["gnorm"][l].rearrange("(p o) -> p o", o=1), writes=["gn"])
            g2flat = G2[l].rearrange("b r s -> (b r) s")
            for ti, (t0, n) in enumerate(tiles):
                prompt = ti < NPTI
                for c in range(FXC):
                    a, ak = tmp.next()
                    if prompt:
                        gather(ak, a[:, 0:n], g2flat, TW, LI_P(0, c), 128, t0, reads=["G2"], writes=[ak])
                    else:
                        p.dma("sp", ak, a[:, 0:n], foxS[l][c * 128:(c + 1) * 128, :], reads=["foxS"], writes=[ak])
                    p.op("pool", lambda e, a=a, c=c, t0=t0, n=n: e.tensor_copy(out=hT[:, c, t0:t0 + n], in_=a[:, 0:n]), reads=[ak], writes=[f"hT{ti}"])
                for hd in range(NH):
                    a, ak = tmp.next()
                    b, bk = tmp.next()
                    if prompt:
                        gather(ak, a[:, 0:n], g2flat, TW, LI_P(1, hd), 128, t0, reads=["G2"], writes=[ak])
                        p.dma("sp", bk, b[:, 0:n], projP[l][HG0 + hd * 128:HG0 + (hd + 1) * 128, t0:t0 + n], reads=["projP"], writes=[bk])
                    else:
                        p.dma("sp", ak, a[:, 0:n], hoS[l][hd * 128:(hd + 1) * 128, :], reads=["hoS"], writes=[ak])
                        p.dma("sp", bk, b[:, 0:n], projS[l][HG0 + hd * 128:HG0 + (hd + 1) * 128, :], reads=["projS"], writes=[bk])
                    p.op("act", lambda e, a=a, n=n: e.activation(out=sq[:, 0, 0:n], in_=a[:, 0:n], func=AF.Square), reads=[ak], writes=["sq"])
                    ps, pk = psr.next()
                    p.op("pe", lambda e, n=n, ps=ps: e.matmul(ps[:, 0:n], lhsT=ones_bf[:], rhs=sq[:, 0, 0:n], start=True, stop=True), reads=["sq", "ones_bf"], writes=[pk])
                    sd, sk = tmp.next()
                    p.op("act", lambda e, n=n, ps=ps, sd=sd: e.activation(out=sd[:, 0:n], in_=ps[:, 0:n], func=AF.Sqrt, scale=1.0 / 128, bias=EPS), reads=[pk], writes=[sk])
                    p.op("dve", lambda e, n=n, sd=sd: e.reciprocal(out=sd[:, 0:n], in_=sd[:, 0:n]), reads=[sk], writes=[sk])
                    p.op("act", lambda e, b=b, n=n: e.activation(out=b[:, 0:n], in_=b[:, 0:n], func=AF.Silu), reads=[bk], writes=[bk])
                    p.op("dve", lambda e, a=a, sd=sd, n=n: e.scalar_tensor_tensor(out=a[:, 0:n], in0=a[:, 0:n], scalar=gn[:, 0:1], in1=sd[:, 0:n], op0=ALU.mult, op1=ALU.mult), reads=[ak, sk, "gn"], writes=[ak])
                    p.op("dve", lambda e, a=a, b=b, hd=hd, t0=t0, n=n: e.tensor_tensor(out=hT[:, FXC + hd, t0:t0 + n], in0=a[:, 0:n], in1=b[:, 0:n], op=ALU.mult), reads=[ak, bk], writes=[f"hT{ti}"])
            for oc in range(0, KC, 2):
                w, wk = load_w_block(Wd["w_out"][l], oc * 128, 256)
                for j in range(2):
                    for ti, (t0, n) in enumerate(tiles):
                        ps, pk = psr.next()
                        for c in range(KC):
                            p.op("pe", lambda e, c=c, j=j, t0=t0, n=n, ps=ps, w=w: e.matmul(ps[:, 0:n], lhsT=w[:, c, j * 128:(j + 1) * 128], rhs=hT[:, c, t0:t0 + n], start=(c == 0), stop=(c == KC - 1)), reads=[wk, f"hT{ti}"], writes=[pk])
                        p.op("dve", lambda e, oc=oc + j, t0=t0, n=n, ps=ps: e.tensor_tensor(out=xT[:, oc, t0:t0 + n], in0=ps[:, 0:n], in1=xT[:, oc, t0:t0 + n], op=ALU.add), reads=[pk, f"xT{ti}"], writes=[f"xT{ti}"])
            ffn("ffn2", l)

        if l_ffn1 is not None:
            l = l_ffn1
            ffn("ffn1", l)
            g = load_gain(Wd["norm_mix"][l], f"g_mix{l}")
            win_d = Wd["w_in"][l]
            negb = p.sb([NFG, 1], F32, "negb")
            p.dma("sp", "negb", negb[:], Wd["b_fgate"][l].rearrange("(p o) -> p o", o=1), writes=["negb"])
            p.op("dve", lambda e: e.tensor_scalar(out=negb[:], in0=negb[:], scalar1=-1.0, scalar2=None, op0=ALU.mult), reads=["negb"], writes=["negb"])
            rmsnorm_to_hT(g, f"g_mix{l}")
            FG0 = 3 * FOXW
            blocks = []
            c0 = 0
            while c0 < FG0:
                blocks.append((c0, min(256, FG0 - c0)))
                c0 += 256
            blocks.append((FG0, NFG))
            c0 = FG0 + NFG
            while c0 < DIN:
                blocks.append((c0, min(256, DIN - c0)))
                c0 += 256
            o16r = Ring([p.sb([128, 512], BF16, f"o16_{i}") for i in range(4)], "o16")
            for (c0, ncols) in blocks:
                w, wk = load_w_block(win_d, c0, ncols)
                if c0 < 3 * FOXW:
                    row16 = c0
                elif H0 <= c0 < H0 + HGW:
                    row16 = 3 * FOXW + (c0 - H0)
                elif H0 + 2 * HGW <= c0 < HG0:
                    row16 = 4 * FOXW + (c0 - H0 - 2 * HGW)
                else:
                    row16 = None
                p16keys = []
                j0 = 0
                while j0 < ncols:
                    m = min(128, ncols - j0)
                    r0 = c0 + j0
                    rkey = f"projP_r{r0}"
                    for ti, (t0, n) in enumerate(tiles):
                        prompt = ti < NPTI
                        ps, pk = psr.next()
                        for c in range(KC):
                            p.op("pe", lambda e, c=c, j0=j0, m=m, t0=t0, n=n, ps=ps, w=w: e.matmul(ps[0:m, 0:n], lhsT=w[:, c, j0:j0 + m], rhs=hT[:, c, t0:t0 + n], start=(c == 0), stop=(c == KC - 1)), reads=[wk, f"hT{ti}"], writes=[pk])
                        o, ok = tmp.next()
                        p.op("act", lambda e, m=m, n=n, ps=ps, o=o: e.activation(out=o[0:m, 0:n], in_=ps[0:m, 0:n], func=AF.Copy), reads=[pk], writes=[ok])
                        if prompt and row16 is not None:
                            o16, o16k = o16r.next()
                            p.op("act", lambda e, m=m, n=n, ps=ps, o16=o16: e.activation(out=o16[0:m, 0:n], in_=ps[0:m, 0:n], func=AF.Copy), reads=[pk], writes=[o16k])
                            kk = f"p16_{r0}_{ti}"
                            p16keys.append(kk)
                            p.dma("act", "o_" + o16k, P32[l][row16 + j0:row16 + j0 + m, t0 // 2:(t0 + n) // 2], o16[0:m, 0:n].bitcast(F32), reads=[o16k], writes=[kk])
                        elif prompt:
                            p.dma("act", "o_" + ok, projP[l][r0:r0 + m, t0:t0 + n], o[0:m, 0:n], reads=[ok], writes=[rkey + f"_{ti}"])
                        else:
                            p.dma("act", "o_" + ok, projS[l][r0:r0 + m, :], o[0:m, 0:n], reads=[ok], writes=[])
                        if FOXW <= r0 < 3 * FOXW:
                            p.dma("act", "o_" + ok, kv_d[l][r0 - FOXW:r0 - FOXW + m, t0:t0 + n], o[0:m, 0:n], reads=[ok], writes=[])
                        if c0 == FG0:
                            lt, lk = tmp.next()
                            p.op("act", lambda e, m=m, n=n, o=o, lt=lt: e.activation(out=lt[0:m, 0:n], in_=o[0:m, 0:n], func=AF.Exp, scale=-1.0, bias=negb[:, 0:1]), reads=[ok, "negb"], writes=[lk])
                            p.op("act", lambda e, m=m, n=n, lt=lt: e.activation(out=lt[0:m, 0:n], in_=lt[0:m, 0:n], func=AF.Ln, bias=1.0), reads=[lk], writes=[lk])
                            p.op("dve", lambda e, m=m, n=n, lt=lt: e.tensor_scalar(out=lt[0:m, 0:n], in0=lt[0:m, 0:n], scalar1=-1.0, scalar2=None, op0=ALU.mult), reads=[lk], writes=[lk])
                            p.dma("act", "o_" + lk, lfo_d[l][:, t0:t0 + n], lt[0:m, 0:n], reads=[lk], writes=[])
                            if prompt:
                                p.dma("act", "o_" + lk, lfP[l][:, t0:t0 + n], lt[0:m, 0:n], reads=[lk], writes=[f"lfP_{ti}"])
                            else:
                                p.dma("act", "o_" + lk, lfS[l][:, :], lt[0:m, 0:n], reads=[lk], writes=[])
                    prk = [rkey + f"_{ti}" for ti in range(NPTI)]
                    if c0 == FG0:
                        allgather(f"ag1_{l}", lfP[l][:, :], G1l[l][:, :], reads=[f"lfP_{ti}" for ti in range(NPTI)], writes=["G1l"])
                    elif row16 is None and r0 < HG0:
                        b = (r0 - H0) // 128
                        allgather(f"ag1_{l}", projP[l][r0:r0 + 128, :], G1h[l][b], reads=prk, writes=[f"G1h_{b}"])
                    j0 += m
                if row16 is not None:
                    b16 = row16 // 256
                    allgather(f"ag1_{l}", P32[l][row16:row16 + 256, :], G32[l][b16], reads=p16keys, writes=[f"G32_{b16}"])
            for _ in range(2):
                allgather(f"ag1_{l}", fl_src[:, :], fl_dst[:, :], reads=[], writes=["flush"])

        if do_final:
            g = load_gain(Wd["norm_final"], "g_fin")
            rmsnorm_to_hT(g, "g_fin", out_fp32_dram=yT_d)

    def m_phase(l):
        p.arena_begin()
        g32 = G32[l].rearrange("b r n -> (b r) n")
        g1h = G1h[l].rearrange("b r n -> (b r) n")
        g1l = G1l[l]
        pS = Ring(banks[0:2], "bank")
        pO = Ring(banks[2:4], "bankO")
        pH = Ring(banks[4:7], "bankH")
        pO.name = "bankO"
        stg = Ring([p.sb([128, FSEG], F32, f"stg{i}") for i in range(3)], "stg")
        cbr = Ring([p.sb([128, FSEG], BF16, f"cb{i}") for i in range(6)], "cb")
        hOne = p.sb([128, FSEG], F32, "hOne")
        p.op("dve", lambda e: e.memset(hOne[:], 1.0), writes=["hOne"])
        carry = p.sb([128, 1], F32, "carry")
        ptr = Ring([p.sb([128, 512], BF16, f"pt{i}") for i in range(3)], "pt")
        fo = Ring([p.sb([64, 512], F32, f"fo{i}") for i in range(2)], "fo")
        frl = Ring([p.sb([64, 512], F32, f"frl{i}") for i in range(1)], "frl")

        def mkrings(tag, nb, TKx, TQx):
            NK = (TKx + 127) // 128
            kr = Ring([p.sb([70, TKx], BF16, f"{tag}kaug{i}") for i in range(nb)], f"{tag}kaug")
            qr = Ring([p.sb([70, TQx], BF16, f"{tag}qaug{i}") for i in range(nb)], f"{tag}qaug")
            vr = Ring([p.sb([128, NK, 128], BF16, f"{tag}vaug{i}") for i in range(nb)], f"{tag}vaug")
            for i in range(nb):
                t = kr.tiles[i]
                p.op("dve", lambda e, t=t: e.memset(t[64:70, :], 1.0), writes=[f"{tag}kaug{i}c"])
                t = qr.tiles[i]
                p.op("dve", lambda e, t=t: e.memset(t[64:70, :], -1.0), writes=[f"{tag}qaug{i}c"])
                t = vr.tiles[i]
                p.op("dve", lambda e, t=t: e.memset(t[:, :, 64:128], 1.0), writes=[f"{tag}vaug{i}o"])
            return kr, qr, vr

        arena_mark = p.arena_off
        rings_s = mkrings("S", 2, TKS, TQS)
        stgA = Ring([p.sb([128, PAST], F32, f"stgA{i}") for i in range(4)], "stgA")
        qkS = p.sb([64, 3, 8, NS], F32, "qkS")
        vtb = p.sb([64, TQS], BF16, "vtb")
        lf = p.sb([NFS, TKS], F32, "fs_lf")
        cs = p.sb([NFS, 3, TKS], BF16, "fs_cs")

        def fox_core(ka, kak, qa_, qak, va, vak, TQ, TK, out_cb, extra=()):
            P = TK - TQ
            W = min(512, TQ)
            for qt in range(TQ // W):
                q0 = qt * W
                po, pok = pO.next()
                last_kt = (P + q0 + W - 1) // 128
                pairs = []
                for kt in range(last_kt + 1):
                    nk = min(128, TK - kt * 128)
                    qa = max(0, kt * 128 - P - q0)
                    if qa >= W:
                        continue
                    pairs.append((kt, nk, qa, (P + q0 + qa) < (kt * 128 + nk - 1)))

                def mm2(kt, nk, qa, pt, ptk, first, last, po=po, pok=pok):
                    p.op("pe", lambda e: e.matmul(po[:, qa:W], lhsT=va[0:nk, kt, :], rhs=pt[0:nk, qa:W], start=first, stop=last), reads=[vak, vak + "o", ptk], writes=[pok])
                pend = None
                for i, (kt, nk, qa, diag) in enumerate(pairs):
                    ps, psk = pS.next()
                    p.op("pe", lambda e, ps=ps, kt=kt, nk=nk, qa=qa, diag=diag, q0=q0: e.matmul(ps[0:nk, qa:W], lhsT=ka[0:70, kt * 128:kt * 128 + nk], rhs=qa_[0:70, q0 + qa:q0 + W], start=True, stop=not diag), reads=[kak, kak + "c", qak, qak + "c"] + list(extra), writes=[psk])
                    if diag:
                        p.op("pe", lambda e, ps=ps, nk=nk, qa=qa: e.matmul(ps[0:nk, qa:W], lhsT=ident[0:nk, 0:nk], rhs=maskrow[0:nk, 0:W - qa], start=False, stop=True), reads=["ident", "maskrow"], writes=[psk])
                    pt, ptk = ptr.next()
                    p.op("act", lambda e, ps=ps, pt=pt, nk=nk, qa=qa: e.activation(out=pt[0:nk, qa:W], in_=ps[0:nk, qa:W], func=AF.Exp, scale=HD ** -0.5), reads=[psk], writes=[ptk])
                    if pend is not None:
                        mm2(*pend)
                    pend = (kt, nk, qa, pt, ptk, i == 0, i == len(pairs) - 1)
                mm2(*pend)
                rl, rlk = frl.next()
                o, ok = fo.next()
                p.op("dve", lambda e, po=po, rl=rl: e.reciprocal(out=rl[0:64, 0:W], in_=po[64:128, 0:W]), reads=[pok], writes=[rlk])
                p.op("dve", lambda e, po=po, rl=rl, o=o: e.tensor_tensor(out=o[0:64, 0:W], in0=po[0:64, 0:W], in1=rl[0:64, 0:W], op=ALU.mult), reads=[pok, rlk], writes=[ok])
                out_cb(o, ok, q0, W)
                yield

        def fox_core_small(ka, kak, qa_, qak, va, vak, TQ, TK, out_cb, strided=False):
            P = TK - TQ
            W = TQ
            NKT = (TK + 127) // 128
            assert NKT * W <= 512
            ps, psk = pS.next()
            for kt in range(NKT):
                nk = min(128, TK - kt * 128)
                diag = P < (kt * 128 + nk - 1)
                nfull_ = TK // 128
                kcols = slice(kt, nfull_ * 128, nfull_) if (strided and kt < nfull_) else slice(kt * 128, kt * 128 + nk)
                p.op("pe", lambda e, kt=kt, nk=nk, diag=diag, kcols=kcols: e.matmul(ps[0:nk, kt * W:(kt + 1) * W], lhsT=ka[0:70, kcols], rhs=qa_[0:70, 0:W], start=True, stop=not diag), reads=[kak, kak + "c", qak, qak + "c"], writes=[psk])
                if diag:
                    p.op("pe", lambda e, kt=kt, nk=nk: e.matmul(ps[0:nk, kt * W:(kt + 1) * W], lhsT=ident[0:nk, 0:nk], rhs=maskrow[0:nk, 0:W], start=False, stop=True), reads=["ident", "maskrow"], writes=[psk])
            pt, ptk = ptr.next()
            nfull = TK // 128
            rem = TK - nfull * 128
            p.op("act", lambda e: e.activation(out=pt[:, 0:nfull * W], in_=ps[:, 0:nfull * W], func=AF.Exp, scale=HD ** -0.5), reads=[psk], writes=[ptk])
            if rem:
                p.op("act", lambda e: e.activation(out=pt[0:rem, nfull * W:(nfull + 1) * W], in_=ps[0:rem, nfull * W:(nfull + 1) * W], func=AF.Exp, scale=HD ** -0.5), reads=[psk], writes=[ptk])
            po, pok = pO.next()
            for kt in range(NKT):
                nk = min(128, TK - kt * 128)
                p.op("pe", lambda e, kt=kt, nk=nk: e.matmul(po[:, 0:W], lhsT=va[0:nk, kt, :], rhs=pt[0:nk, kt * W:(kt + 1) * W], start=(kt == 0), stop=(kt == NKT - 1)), reads=[vak, vak + "o", ptk], writes=[pok])
            rl, rlk = frl.next()
            o, ok = fo.next()
            p.op("dve", lambda e: e.reciprocal(out=rl[0:64, 0:W], in_=po[64:128, 0:W]), reads=[pok], writes=[rlk])
            p.op("dve", lambda e: e.tensor_tensor(out=o[0:64, 0:W], in0=po[0:64, 0:W], in1=rl[0:64, 0:W], op=ALU.mult), reads=[pok, rlk], writes=[ok])
            out_cb(o, ok, 0, W)
            yield

        def csplit_seg(src, srck, dst, dstk, nu, n, emit_row):
            for r in range(3):
                cb, cbk = cbr.next()
                p.op("dve", lambda e, cb=cb, src=src: e.tensor_copy(out=cb[0:nu, 0:n], in_=src[0:nu, 0:n]), reads=[srck], writes=[cbk])
                if r < 2:
                    p.op("dve", lambda e, cb=cb, src=src, dst=dst: e.tensor_tensor(out=dst[0:nu, 0:n], in0=src[0:nu, 0:n], in1=cb[0:nu, 0:n], op=ALU.subtract), reads=[srck, cbk], writes=[dstk])
                emit_row(r, cb, cbk)
                src, srck, dst, dstk = dst, dstk, src, srck

        def fox_prompt_unit(u):
            kaug, qaug, vaug = rings_p
            ka, kak = kaug.next()
            qa_, qak = qaug.next()
            va, vak = vaug.next()
            for c0 in range(0, SEQ, FSEG):
                r, col = c0 // NPT, c0 % NPT
                kb, kbk = cbr.next()
                gather(kbk, kb[0:64, 0:FSEG].bitcast(F32), g32, FSEG // 2, LI_F(1, u, r), 64, col // 2, reads=["G1"], writes=[kbk])
                p.op("dve", lambda e, kb=kb, c0=c0: e.tensor_copy(out=ka[0:64, c0:c0 + FSEG], in_=kb[0:64, 0:FSEG]), reads=[kbk], writes=[f"{kak}_s{c0}"])
                qb, qbk = cbr.next()
                gather(qbk, qb[0:64, 0:FSEG].bitcast(F32), g32, FSEG // 2, LI_F(0, u, r), 64, col // 2, reads=["G1"], writes=[qbk])
                p.op("dve", lambda e, qb=qb, c0=c0: e.tensor_copy(out=qa_[0:64, c0:c0 + FSEG], in_=qb[0:64, 0:FSEG]), reads=[qbk], writes=[f"{qak}_s{c0}"])
                vb, vbk = cbr.next()
                gather(vbk, vb[0:64, 0:FSEG].bitcast(F32), g32, FSEG // 2, LI_F(2, u, r), 64, col // 2, reads=["G1"], writes=[vbk])
                for k8 in range(FSEG // 128):
                    p.op("pe", lambda e, vb=vb, k8=k8: e.transpose(pbf[:, k8 * 64:(k8 + 1) * 64], vb[0:64, k8 * 128:(k8 + 1) * 128], ident[0:64, 0:64]), reads=[vbk, "ident"], writes=["pbf"])
                kt0 = c0 // 128
                p.op("act", lambda e, kt0=kt0: e.activation(out=va[:, kt0:kt0 + FSEG // 128, 0:64], in_=pbf[:, 0:FSEG // 2].rearrange("p (t d) -> p t d", d=64), func=AF.Copy), reads=["pbf"], writes=[vak])
                lw, lwk = stg.next()
                gather(lwk, lw[0:2, 0:FSEG], g1l, FSEG, LI_FL(u, r), 2, col, reads=["G1"], writes=[lwk])
                lc, lck = stg.next()
                init = 0.0 if c0 == 0 else carry[0:1, 0:1]
                p.op("dve", lambda e, lw=lw, lc=lc, init=init: e.tensor_tensor_scan(out=lc[0:1, 0:FSEG], data0=hOne[0:1, 0:FSEG], data1=lw[0:1, 0:FSEG], initial=init, op0=ALU.mult, op1=ALU.add), reads=[lwk, "hOne", "carry"], writes=[lck])
                p.op("dve", lambda e, lc=lc: e.tensor_copy(out=carry[0:1, 0:1], in_=lc[0:1, FSEG - 1:FSEG]), reads=[lck], writes=["carry"])
                p.op("dve", lambda e, lc=lc: e.tensor_scalar(out=lc[0:1, 0:FSEG], in0=lc[0:1, 0:FSEG], scalar1=8.0, scalar2=None, op0=ALU.mult), reads=[lck], writes=[lck])

                def emit_row(r_, cb, cbk, c0=c0):
                    p.dma("sp", kak + "c", ka[67 + r_:68 + r_, c0:c0 + FSEG], cb[0:1, 0:FSEG], reads=[cbk, kak + "c"], writes=[kak + "c"])
                    p.dma("sp", qak + "c", qa_[64 + r_:65 + r_, c0:c0 + FSEG], cb[0:1, 0:FSEG], reads=[cbk, qak + "c"], writes=[qak + "c"])
                csplit_seg(lc, lck, lw, lwk, 1, FSEG, emit_row)
                yield

            def out_cb(o, ok, q0, W, u=u):
                for half in range(2):
                    p.dma("sp", "o_" + ok, S2[l][u * 2 + half, :, q0:q0 + W], o[half * 32:(half + 1) * 32, 0:W], reads=[ok], writes=[f"S2f{u}_{q0}_{half}"])
            segk = [f"{kak}_s{c0}" for c0 in range(0, SEQ, FSEG)] + [f"{qak}_s{c0}" for c0 in range(0, SEQ, FSEG)]
            yield from fox_core(ka, kak, qa_, qak, va, vak, SEQ, SEQ, out_cb, extra=segk)
            for half in range(2):
                b = u * 2 + half
                allgather(f"ag2_{l}", S2[l][b], G2[l][b], reads=[f"S2f{u}_{q0}_{half}" for q0 in range(0, SEQ, 512)], writes=[f"G2_{b}"])

        def fox_prompt_all():
            for u in range(2):
                yield from fox_prompt_unit(u)

        hQ16 = hA = hK = hB = hNB = hQ = hVf = hVt = QpT = KpT = hVb = hO = ebl = eqr = ekr = amr = ktr = S = Sb = St = lbt = rings_p = NCS = None
        def alloc_stage_b():
            nonlocal hQ16, hA, hK, hB, hNB, hQ, hVf, hVt, QpT, KpT, hVb, hO, ebl, eqr, ekr, amr, ktr, S, Sb, St, lbt, rings_p, NCS
            hA = p.sb([128, HSEG], F32, "hA")
            hK = p.sb([128, HSEG], F32, "hK")
            hB = p.sb([128, HSEG], F32, "hB")
            hNB = p.sb([128, HSEG], F32, "hNB")
            hQ = p.sb([128, HSEG], F32, "hQ")
            hVf = p.sb([128, HSEG], F32, "hVf")
            hVt = p.sb([128, HSEG], BF16, "hVt")
            hQ16 = p.sb([128, HSEG], BF16, "hQ16")
            QpT = p.sb([128, HSEG], BF16, "QpT")
            KpT = p.sb([128, HSEG], BF16, "KpT")
            NCS = HSEG // CH
            hVb = p.sb([CH, NCS, 128], BF16, "hVb")
            hO = p.sb([128, HSEG], F32, "hO")
            ebl = p.sb([128, NCS], F32, "ebl")
            eqr = Ring([p.sb([128, CH], F32, f"eq{i}") for i in range(2)], "eq")
            ekr = Ring([p.sb([128, CH], F32, f"ek{i}") for i in range(2)], "ek")
            amr = Ring([p.sb([CH, CH], BF16, f"am{i}") for i in range(2)], "am")
            ktr = Ring([p.sb([CH, 128], BF16, f"kt{i}") for i in range(2)], "kt")
            S = p.sb([128, 128], F32, "S")
            Sb = p.sb([128, 128], BF16, "Sb")
            St = p.sb([128, 128], F32, "St")
            lbt = p.sb([128, 4], F32, "lbt")

            rings_p = mkrings("P", 1, SEQ, SEQ)

        def hgrn_unit(T, C, load_seg, lb_ap, s0_ap, out_seg, sfin_ap):
            p.dma("sp", "lbt", lbt[:, 0:3], lb_ap, reads=["lbt"], writes=["lbt"])
            p.op("dve", lambda e: e.tensor_tensor(out=lbt[:, 3:4], in0=lbt[:, 1:2], in1=lbt[:, 0:1], op=ALU.subtract), reads=["lbt"], writes=["lbt"])
            p.op("act", lambda e: e.activation(out=lbt[:, 3:4], in_=lbt[:, 3:4], func=AF.Sigmoid), reads=["lbt"], writes=["lbt"])
            p.op("dve", lambda e: e.tensor_tensor(out=lbt[:, 0:1], in0=lbt[:, 3:4], in1=lbt[:, 2:3], op=ALU.mult), reads=["lbt"], writes=["lbt"])
            p.op("dve", lambda e: e.tensor_scalar(out=lbt[:, 1:2], in0=lbt[:, 0:1], scalar1=-1.0, scalar2=1.0, op0=ALU.mult, op1=ALU.add), reads=["lbt"], writes=["lbt"])
            if s0_ap is None:
                p.op("dve", lambda e: e.memset(S[:], 0.0), reads=["S"], writes=["S"])
                p.op("dve", lambda e: e.memset(Sb[:], 0.0), reads=["Sb"], writes=["Sb"])
            else:
                p.dma("sp", "S0", S[:], s0_ap, reads=["S"], writes=["S"])
                p.op("dve", lambda e: e.tensor_copy(out=Sb[:], in_=S[:]), reads=["S"], writes=["Sb"])
            s0 = 0
            while s0 < T:
                n = min(HSEG, T - s0)
                nch = n // C
                have_bf = load_seg(s0, n)
                if not have_bf:
                    p.op("dve", lambda e, n=n: e.tensor_copy(out=hVt[:, 0:n], in_=hVf[:, 0:n]), reads=["hVf"], writes=["hVt"])
                for ci in range(nch):
                    p.op("pe", lambda e, ci=ci: e.transpose(pbf[0:C, ci * 128:(ci + 1) * 128], hVt[:, ci * C:(ci + 1) * C], ident[:]), reads=["hVt", "ident"], writes=["pbf"])
                p.op("act", lambda e, nch=nch: e.activation(out=hVb[0:C, 0:nch, :], in_=pbf[0:C, 0:nch * 128].rearrange("p (t d) -> p t d", d=128), func=AF.Copy), reads=["pbf"], writes=["hVb"])
                p.op("act", lambda e, n=n: e.activation(out=hA[:, 0:n], in_=hA[:, 0:n], func=AF.Sigmoid), reads=["hA"], writes=["hA"])
                p.op("act", lambda e, n=n: e.activation(out=hQ[:, 0:n], in_=hQ[:, 0:n], func=AF.Silu), reads=["hQ"], writes=["hQ"])
                p.op("dve", lambda e, n=n: e.tensor_scalar(out=hA[:, 0:n], in0=hA[:, 0:n], scalar1=lbt[:, 1:2], scalar2=lbt[:, 0:1], op0=ALU.mult, op1=ALU.add), reads=["hA", "lbt"], writes=["hA"])
                p.op("dve", lambda e, n=n: e.tensor_scalar(out=hK[:, 0:n], in0=hA[:, 0:n], scalar1=-1.0, scalar2=1.0, op0=ALU.mult, op1=ALU.add), reads=["hA"], writes=["hK"])
                p.op("act", lambda e, n=n: e.activation(out=hA[:, 0:n], in_=hA[:, 0:n], func=AF.Ln), reads=["hA"], writes=["hA"])
                p.op("dve", lambda e, n=n: e.tensor_tensor_scan(out=hB[:, 0:n], data0=hOne[:, 0:n], data1=hA[:, 0:n], initial=0.0, op0=ALU.mult, op1=ALU.add), reads=["hA", "hOne"], writes=["hB"])
                p.op("dve", lambda e, n=n: e.tensor_scalar(out=hNB[:, 0:n], in0=hB[:, 0:n], scalar1=-1.0, scalar2=None, op0=ALU.mult), reads=["hB"], writes=["hNB"])
                for ci in range(nch):
                    c0 = ci * C
                    bq = zcol[:, 0:1] if ci == 0 else hNB[:, c0 - 1:c0]
                    bk = zcol[:, 0:1] if ci == 0 else hB[:, c0 - 1:c0]
                    eq, eqk = eqr.next()
                    ek, ekk = ekr.next()
                    p.op("act", lambda e, eq=eq, c0=c0, bq=bq: e.activation(out=eq[:, 0:C], in_=hB[:, c0:c0 + C], func=AF.Exp, bias=bq), reads=["hB", "hNB", "zcol"], writes=[eqk])
                    p.op("act", lambda e, ek=ek, c0=c0, bk=bk: e.activation(out=ek[:, 0:C], in_=hB[:, c0:c0 + C], func=AF.Exp, scale=-1.0, bias=bk), reads=["hB", "zcol"], writes=[ekk])
                    p.op("dve", lambda e, eq=eq, c0=c0: e.tensor_tensor(out=QpT[:, c0:c0 + C], in0=hQ[:, c0:c0 + C], in1=eq[:, 0:C], op=ALU.mult), reads=["hQ", eqk], writes=["QpT"])
                    p.op("dve", lambda e, ek=ek, c0=c0: e.tensor_tensor(out=KpT[:, c0:c0 + C], in0=hK[:, c0:c0 + C], in1=ek[:, 0:C], op=ALU.mult), reads=["hK", ekk], writes=["KpT"])
                    p.op("dve", lambda e, eq=eq, ci=ci: e.tensor_copy(out=ebl[:, ci:ci + 1], in_=eq[:, C - 1:C]), reads=[eqk], writes=["ebl"])
                for ci in range(nch):
                    c0 = ci * C
                    pa, pak = pH.next()
                    p.op("pe", lambda e, pa=pa, c0=c0: e.matmul(pa[0:C, 0:C], lhsT=KpT[:, c0:c0 + C], rhs=QpT[:, c0:c0 + C], start=True, stop=True), reads=["KpT", "QpT"], writes=[pak])
                    am, amk = amr.next()
                    p.op("dve", lambda e, pa=pa, am=am: e.tensor_tensor(out=am[0:C, 0:C], in0=pa[0:C, 0:C], in1=mask01[0:C, 0:C], op=ALU.mult), reads=[pak, "mask01"], writes=[amk])
                    p.op("pe", lambda e, c0=c0: e.transpose(pbf[0:C, 0:128], KpT[:, c0:c0 + C], ident[:]), reads=["KpT", "ident"], writes=["pbf"])
                    ktt, ktk = ktr.next()
                    p.op("act", lambda e, t=ktt: e.activation(out=t[0:C, :], in_=pbf[0:C, 0:128], func=AF.Copy), reads=["pbf"], writes=[ktk])
                    po, pok = pH.next()
                    p.op("pe", lambda e, po=po, am=am, ci=ci: e.matmul(po[:, 0:C], lhsT=hVb[0:C, ci, :], rhs=am[0:C, 0:C], start=True, stop=False), reads=["hVb", amk], writes=[pok])
                    p.op("pe", lambda e, po=po, c0=c0: e.matmul(po[:, 0:C], lhsT=Sb[:], rhs=QpT[:, c0:c0 + C], start=False, stop=True), reads=["Sb", "QpT"], writes=[pok])
                    p.op("act", lambda e, po=po, c0=c0: e.activation(out=hO[:, c0:c0 + C], in_=po[:, 0:C], func=AF.Copy), reads=[pok], writes=["hO"])
                    pu, puk = pH.next()
                    p.op("pe", lambda e, pu=pu, t=ktt, ci=ci: e.matmul(pu[:, 0:128], lhsT=t[0:C, :], rhs=hVb[0:C, ci, :], start=True, stop=True), reads=[ktk, "hVb"], writes=[puk])
                    p.op("dve", lambda e, pu=pu: e.tensor_tensor(out=St[:], in0=S[:], in1=pu[:, 0:128], op=ALU.add), reads=["S", puk], writes=["St"])
                    p.op("dve", lambda e, ci=ci: e.tensor_scalar(out=S[:], in0=St[:], scalar1=ebl[:, ci:ci + 1], scalar2=None, op0=ALU.mult), reads=["St", "ebl"], writes=["S"])
                    p.op("act", lambda e, ci=ci: e.activation(out=Sb[:], in_=St[:], func=AF.Copy, scale=ebl[:, ci:ci + 1]), reads=["St", "ebl"], writes=["Sb"])
                    if ci % 2 == 1:
                        yield
                out_seg(s0, n)
                yield
                s0 += n
            p.dma("sp", "Sout", sfin_ap, S[:], reads=["S"], writes=[])

        def hp_load(s0, n):
            r, col = s0 // NPT, s0 % NPT
            gather("hQ16", hQ16[:, 0:n].bitcast(F32), g32, HSEG // 2, LI_H(0, r), 128, col // 2, reads=["G1", "hQ16"], writes=["hQ16"])
            p.op("act", lambda e, n=n: e.activation(out=hQ[:, 0:n], in_=hQ16[:, 0:n], func=AF.Copy), reads=["hQ16", "hQ"], writes=["hQ"])
            gather("hA", hA[:, 0:n], g1h, HSEG, LI_H(1, r), 128, col, reads=["G1", "hA"], writes=["hA"])
            gather("hVt", hVt[:, 0:n].bitcast(F32), g32, HSEG // 2, LI_H(2, r), 128, col // 2, reads=["G1", "hVt"], writes=["hVt"])
            return True

        def hp_out(s0, n):
            for q4 in range(4):
                p.dma("sp", "hO", S2[l][4 + q4, :, s0:s0 + n], hO[q4 * 32:(q4 + 1) * 32, 0:n], reads=["hO"], writes=[f"S2h_{s0}_{q4}"])
        def hgrn_prompt_all():
            yield from hgrn_unit(SEQ, CH, hp_load, hplb_d[l], None, hp_out, hpS_d[l])
            for q4 in range(4):
                allgather(f"ag2_{l}", S2[l][4 + q4], G2[l][4 + q4], reads=[f"S2h_{s0}_{q4}" for s0 in range(0, SEQ, HSEG)], writes=[f"G2_{4 + q4}"])

        def fox_sample_prep():
            for t3 in range(3):
                p.dma("sp", "qkS", qkS[:, t3, :, :], projS[l][t3 * FOXW:(t3 + 1) * FOXW, :].rearrange("(h d) n -> d h n", d=64), reads=["projS"], writes=["qkS"])
            p.dma("sp", "fslf", lf[:, 0:PAST], clf_d[l][:, :], writes=["fslf"])
            for bl in range(NSB):
                p.dma("sp", "fslf", lf[bl * 8:(bl + 1) * 8, PAST:TKS], lfS[l][:, bl * TQS:(bl + 1) * TQS], reads=["lfS"], writes=["fslf"])
            c0 = 0
            while c0 < TKS:
                n = min(FSEG, TKS - c0)
                lc, lck = stg.next()
                init = 0.0 if c0 == 0 else carry[0:NFS, 0:1]
                p.op("dve", lambda e, lc=lc, c0=c0, n=n, init=init: e.tensor_tensor_scan(out=lc[0:NFS, 0:n], data0=hOne[0:NFS, 0:n], data1=lf[:, c0:c0 + n], initial=init, op0=ALU.mult, op1=ALU.add), reads=["fslf", "hOne", "carry"], writes=[lck])
                p.op("dve", lambda e, lc=lc, n=n: e.tensor_copy(out=carry[0:NFS, 0:1], in_=lc[0:NFS, n - 1:n]), reads=[lck], writes=["carry"])
                p.op("dve", lambda e, lc=lc, n=n: e.tensor_scalar(out=lc[0:NFS, 0:n], in0=lc[0:NFS, 0:n], scalar1=8.0, scalar2=None, op0=ALU.mult), reads=[lck], writes=[lck])
                lw, lwk = stg.next()

                def emit_row(r_, cb, cbk, c0=c0, n=n):
                    p.op("dve", lambda e, cb=cb: e.tensor_copy(out=cs[:, r_, c0:c0 + n], in_=cb[0:NFS, 0:n]), reads=[cbk], writes=["fscs"])
                csplit_seg(lc, lck, lw, lwk, NFS, n, emit_row)
                c0 += n
        def fox_sample_unit(u):
            bl, h = u // 8, u % 8
            kaug, qaug, vaug = rings_s
            ka, kak = kaug.next()
            qa_, qak = qaug.next()
            va, vak = vaug.next()
            nfull = PAST // 128
            cols = slice(bl * TQS, (bl + 1) * TQS)
            s, sk = stgA.next()
            p.dma("sp", sk, s[0:64, 0:PAST], ckT_d[l][u, :, :], writes=[sk])
            p.op("dve", lambda e, s=s: e.tensor_copy(out=ka[0:64, 0:PAST], in_=s[0:64, 0:PAST]), reads=[sk], writes=[kak])
            p.op("dve", lambda e: e.tensor_copy(out=ka[0:64, PAST:TKS], in_=qkS[:, 1, h, cols]), reads=["qkS"], writes=[kak])
            p.op("act", lambda e: e.activation(out=qa_[0:64, 0:TQS], in_=qkS[:, 0, h, cols], func=AF.Copy), reads=["qkS"], writes=[qak])
            for r in range(3):
                p.dma("act", kak + "c", ka[67 + r:68 + r, 0:TKS], cs[u:u + 1, r, 0:TKS], reads=["fscs", kak + "c"], writes=[kak + "c"])
                p.dma("act", qak + "c", qa_[64 + r:65 + r, 0:TQS], cs[u:u + 1, r, PAST:TKS], reads=["fscs", qak + "c"], writes=[qak + "c"])
            s, sk = stgA.next()
            p.dma("sp", sk, s[:, 0:nfull * 64], cv_d[l][u].rearrange("(p t) d -> p (t d)", t=nfull), writes=[sk])
            p.op("act", lambda e, s=s: e.activation(out=va[:, 0:nfull, 0:64], in_=s[:, 0:nfull * 64].rearrange("p (t d) -> p t d", d=64), func=AF.Copy), reads=[sk], writes=[vak])
            p.op("dve", lambda e: e.tensor_copy(out=vtb[:, :], in_=qkS[:, 2, h, cols]), reads=["qkS"], writes=["vtb"])
            p.op("pe", lambda e: e.transpose(pbf[0:TQS, 0:64], vtb[:, :], ident[0:64, 0:64]), reads=["vtb", "ident"], writes=["pbf"])
            p.op("act", lambda e: e.activation(out=va[0:TQS, nfull, 0:64], in_=pbf[0:TQS, 0:64], func=AF.Copy), reads=["pbf"], writes=[vak])

            def out_cb(o, ok, q0, W, bl=bl, h=h):
                p.dma("sp", "o_" + ok, foxS[l][h * 64:(h + 1) * 64, bl * TQS:(bl + 1) * TQS], o[0:64, 0:W], reads=[ok], writes=[])
            yield from fox_core_small(ka, kak, qa_, qak, va, vak, TQS, TKS, out_cb, strided=True)

        def fox_sample_all():
            fox_sample_prep()
            for u in range(NFS):
                yield from fox_sample_unit(u)

        def hgrn_sample_unit(u):
            bl, hd = u // 4, u % 4

            def hs_load(s0, n, bl=bl, hd=hd):
                cols = slice(bl * TQS, (bl + 1) * TQS)
                p.dma("sp", "hQ", hQ[:, 0:n], projS[l][H0 + hd * 128:H0 + (hd + 1) * 128, cols], reads=["projS", "hQ"], writes=["hQ"])
                p.dma("sp", "hA", hA[:, 0:n], projS[l][H0 + HGW + hd * 128:H0 + HGW + (hd + 1) * 128, cols], reads=["projS", "hA"], writes=["hA"])
                p.dma("sp", "hVf", hVf[:, 0:n], projS[l][H0 + 2 * HGW + hd * 128:H0 + 2 * HGW + (hd + 1) * 128, cols], reads=["projS", "hVf"], writes=["hVf"])

            def hs_out(s0, n, bl=bl, hd=hd):
                p.dma("sp", "hO", hoS[l][hd * 128:(hd + 1) * 128, bl * TQS:(bl + 1) * TQS], hO[:, 0:n], reads=["hO"], writes=[])
            yield from hgrn_unit(TQS, min(CH, TQS), hs_load, hslb_d[l][u], st0_d[l][u], hs_out, hsS_d[l][u])

        def hgrn_sample_all():
            for u in range(NHS):
                yield from hgrn_sample_unit(u)

        def interleave(gens):
            gens = list(gens)
            while gens:
                for g in list(gens):
                    try:
                        next(g)
                    except StopIteration:
                        gens.remove(g)

        interleave([fox_sample_all()])
        p.fence()
        p.arena_off = arena_mark
        alloc_stage_b()

        def hgrn_chain():
            yield from hgrn_prompt_all()
            yield from hgrn_sample_all()
        interleave([fox_prompt_all(), hgrn_chain()])
        for _ in range(2):
            allgather(f"ag2_{l}", fl_src[:, :], fl_dst[:, :], reads=[], writes=["flush"])

    t_phase(None, 0, False, True)
    for l in range(L):
        p.fence(skip=(f"ag1_{l}",))
        m_phase(l)
        p.fence()
        if l + 1 < L:
            t_phase(l, l + 1, False, False)
        else:
            t_phase(l, None, True, False)
    p.emit()
    return nc, p


def make_idx(j, NPT):
    SEQ = 4 * NPT
    q1024, q512, s512 = NPT // 1024, NPT // 512, SEQ // 512
    idx = np.zeros((128, NLISTS), np.int32)
    d = np.arange(64)
    e = np.arange(128)
    for t in range(3):
        for u in range(2):
            for r in range(4):
                idx[0:64, LI_F(t, u, r)] = (((t * 2 + j // 2) * 4 + r) * 256 + (j % 2) * 128 + u * 64 + d) * q1024
    for u in range(2):
        for r in range(4):
            idx[0, LI_FL(u, r)] = (r * 8 + 2 * j + u) * q1024
            idx[1, LI_FL(u, r)] = (r * 8 + 2 * j + (1 - u)) * q1024
    for r in range(4):
        idx[:, LI_H(0, r)] = (((6 + j // 2) * 4 + r) * 256 + (j % 2) * 128 + e) * q512
        idx[:, LI_H(1, r)] = (((4 + j) * 4 + r) * 128 + e) * q512
        idx[:, LI_H(2, r)] = (((8 + j // 2) * 4 + r) * 256 + (j % 2) * 128 + e) * q512
    for c in range(4):
        idx[:, LI_P(0, c)] = (((e // 32) * 4 + c) * 32 + e % 32) * s512 + j * q512
        idx[:, LI_P(1, c)] = (((4 + e // 32) * 4 + c) * 32 + e % 32) * s512 + j * q512
    return idx


def run_fused(inp, SEQ, PAST, DFF, BATCH=2, DEC_BATCH=32, DEC_SEQ=16, D=1024, L=2):
    f32 = np.float32
    c_ = lambda a: np.ascontiguousarray(a, dtype=f32)
    NPT = SEQ // 4
    NSB = DEC_BATCH // 8
    H, HD, HH = 8, 64, 4
    nc, p = build_fused(NPT, NSB, DEC_SEQ, PAST, D, DFF, L)
    A = {k: np.asarray(v, f32) for k, v in inp.items()}
    shared = dict(norm_ffn1=c_(A["norm_ffn1"]), ffn1_wi=c_(A["ffn1_wi"]), ffn1_wo=c_(A["ffn1_wo"]), norm_mix=c_(A["norm_mix"]),
                  w_in=c_(A["w_in"]), b_fgate=c_(A["b_fgate"]), gnorm=c_(A["hgrn_gnorm"]), w_out=c_(A["w_out"]),
                  norm_ffn2=c_(A["norm_ffn2"]), ffn2_wi=c_(A["ffn2_wi"]), ffn2_wo=c_(A["ffn2_wo"]), norm_final=c_(A["norm_final"]))
    lbp = A["hgrn_lb"].reshape(L, HH, 128)
    maps = []
    for c in range(8):
        b, j = c // 4, c % 4
        bs = slice(NSB * c, NSB * (c + 1))
        xt = np.concatenate([A["x_prompt"][b, j * NPT:(j + 1) * NPT], A["x_sample"][bs].reshape(NSB * DEC_SEQ, D)], axis=0)
        lb3 = np.stack([np.stack([lbp[0], lbp[l], np.full((HH, 128), 1.0 if l > 0 else 0.0, f32)], axis=-1) for l in range(L)])
        m = dict(shared)
        m.update(xT=c_(xt.T),
                 cache_kT=c_(np.swapaxes(A["cache_k"][:, bs], -1, -2).reshape(L, NSB * H, HD, PAST)),
                 cache_v=c_(A["cache_v"][:, bs].reshape(L, NSB * H, PAST, HD)),
                 cache_logf=c_(A["cache_logf"][:, bs].reshape(L, NSB * H, PAST)),
                 state0=c_(A["state_hgrn"][:, bs].reshape(L, NSB * HH, 128, 128)),
                 hp_lb=c_(lb3[:, j]), hs_lb=c_(np.tile(lb3, (1, NSB, 1, 1))),
                 idx=make_idx(j, NPT))
        maps.append(m)
    res = run_bass_kernel_spmd(nc, maps, core_ids=list(range(8))).results
    NS = NSB * DEC_SEQ
    y_p = np.zeros((BATCH, SEQ, D), f32); y_s = np.zeros((DEC_BATCH, DEC_SEQ, D), f32)
    k_p = np.zeros((L, BATCH, H, SEQ, HD), f32); v_p = np.zeros_like(k_p)
    lf_p = np.zeros((L, BATCH, H, SEQ), f32)
    k_s = np.zeros((L, DEC_BATCH, H, DEC_SEQ, HD), f32); v_s = np.zeros_like(k_s)
    lf_s = np.zeros((L, DEC_BATCH, H, DEC_SEQ), f32)
    s_p = np.zeros((L, BATCH, HH, 128, 128), f32); s_s = np.zeros((L, DEC_BATCH, HH, 128, 128), f32)
    for c in range(8):
        b, j = c // 4, c % 4
        bs = slice(NSB * c, NSB * (c + 1)); ts = slice(j * NPT, (j + 1) * NPT)
        r = res[c]
        y = r["yT"].T
        y_p[b, ts] = y[:NPT]; y_s[bs] = y[NPT:].reshape(NSB, DEC_SEQ, D)
        kv = r["kvT"]
        k = kv[:, 0:512].reshape(L, H, HD, NPT + NS); v = kv[:, 512:1024].reshape(L, H, HD, NPT + NS)
        k_p[:, b, :, ts] = np.transpose(k[..., :NPT], (0, 1, 3, 2)); v_p[:, b, :, ts] = np.transpose(v[..., :NPT], (0, 1, 3, 2))
        k_s[:, bs] = np.transpose(k[..., NPT:].reshape(L, H, HD, NSB, DEC_SEQ), (0, 3, 1, 4, 2))
        v_s[:, bs] = np.transpose(v[..., NPT:].reshape(L, H, HD, NSB, DEC_SEQ), (0, 3, 1, 4, 2))
        lf = r["logfT"]
        lf_p[:, b, :, ts] = lf[..., :NPT]
        lf_s[:, bs] = np.transpose(lf[..., NPT:].reshape(L, H, NSB, DEC_SEQ), (0, 2, 1, 3))
        s_p[:, b, j] = r["hp_S"]
        s_s[:, bs] = r["hs_S"].reshape(L, NSB, HH, 128, 128)
    return (y_p, y_s, k_p, v_p, lf_p, s_p, k_s, v_s, lf_s, s_s)


def kernel(x_prompt, x_sample, cache_k, cache_v, cache_logf, state_hgrn,
           norm_ffn1, ffn1_wi, ffn1_wo, norm_mix, w_in, b_fgate, hgrn_lb, hgrn_gnorm,
           w_out, norm_ffn2, ffn2_wi, ffn2_wo, norm_final):
    inp = dict(x_prompt=x_prompt, x_sample=x_sample, cache_k=cache_k, cache_v=cache_v, cache_logf=cache_logf,
               state_hgrn=state_hgrn, norm_ffn1=norm_ffn1, ffn1_wi=ffn1_wi, ffn1_wo=ffn1_wo, norm_mix=norm_mix,
               w_in=w_in, b_fgate=b_fgate, hgrn_lb=hgrn_lb, hgrn_gnorm=hgrn_gnorm, w_out=w_out,
               norm_ffn2=norm_ffn2, ffn2_wi=ffn2_wi, ffn2_wo=ffn2_wo, norm_final=norm_final)
    return run_fused(inp, SEQ=8192, PAST=2048, DFF=2816)
```
